# Optimizing a Trainium2 kernel written in Bass

```python
import math
import jax, jax.numpy as jnp
from jax import lax
import numpy as np

D_MODEL = 1024
BATCH = 8
SEQ = 2048
DEPTH = 1
DEC_BATCH = 32
DEC_SEQ = 4
PAST_LEN = 16384
PAGE_SIZE = 128

MIX_WIDTH = D_MODEL
RET_HEADS = 4
RET_HEAD_DIM = MIX_WIDTH // 2 // RET_HEADS
RET_WIDTH = RET_HEADS * RET_HEAD_DIM
DIFF_HEADS = 4
DIFF_VDIM = (MIX_WIDTH - RET_WIDTH) // DIFF_HEADS
DIFF_QK_DIM = DIFF_VDIM // 2
DIFF_WIDTH = DIFF_HEADS * DIFF_VDIM
PROJ_WIDTH = 4 * RET_WIDTH + 3 * DIFF_WIDTH
D_FF = -(-8 * D_MODEL // (3 * 256)) * 256
PLE_DIM = 256
RET_CHUNK = 128
Q_BLOCK = 128
EPS = 1e-6

kernel_name = 'hybrid_retention_diffattn_step'


def rmsnorm(x, w):
    xf = x.astype(jnp.float32)
    y = xf * lax.rsqrt(jnp.mean(xf * xf, axis=-1, keepdims=True) + EPS)
    return (y * w.astype(jnp.float32)).astype(x.dtype)


def retention_log_decay():
    return jnp.log1p(-jnp.exp2(-5.0 - jnp.arange(RET_HEADS, dtype=jnp.float32)))


def alibi_slopes():
    return jnp.exp2(-8.0 / DIFF_HEADS * jnp.arange(1, DIFF_HEADS + 1, dtype=jnp.float32))


def project_heads(x, ln1_w, w_in, q_norm_w, k_norm_w):
    b, t, _ = x.shape
    xn = rmsnorm(x, ln1_w)
    proj = xn @ w_in
    r, d = RET_WIDTH, DIFF_WIDTH
    rq, rk, rv, rg, dq, dk, dv = jnp.split(proj, [r, 2 * r, 3 * r, 4 * r, 4 * r + d, 4 * r + 2 * d], axis=-1)
    rq = rq.reshape(b, t, RET_HEADS, RET_HEAD_DIM).astype(jnp.float32)
    rk = rk.reshape(b, t, RET_HEADS, RET_HEAD_DIM).astype(jnp.float32) * (RET_HEAD_DIM ** -0.5)
    rv = rv.reshape(b, t, RET_HEADS, RET_HEAD_DIM).astype(jnp.float32)
    dq = rmsnorm(dq.reshape(b, t, DIFF_HEADS, 2, DIFF_QK_DIM), q_norm_w)
    dk = rmsnorm(dk.reshape(b, t, DIFF_HEADS, 2, DIFF_QK_DIM), k_norm_w)
    dv = dv.reshape(b, t, DIFF_HEADS, DIFF_VDIM)
    return rq, rk, rv, rg, dq, dk, dv


def retention_chunk(q, k, v, state, log_g):
    L = q.shape[1]
    pos = jnp.arange(L, dtype=jnp.float32)
    rel = pos[:, None] - pos[None, :]
    decay = jnp.where(rel >= 0, jnp.exp(log_g[:, None, None] * jnp.maximum(rel, 0.0)), 0.0)
    scores = jnp.einsum('blhd,bmhd->bhlm', q, k) * decay
    o = jnp.einsum('bhlm,bmhe->blhe', scores, v)
    cross_w = jnp.exp(log_g[None, :] * (pos[:, None] + 1.0))
    o = o + jnp.einsum('blhd,bhde->blhe', q, state) * cross_w[None, :, :, None]
    k_w = jnp.exp(log_g[None, :] * (L - 1.0 - pos[:, None]))
    new_state = jnp.exp(log_g * L)[None, :, None, None] * state + jnp.einsum('blhd,blhe,lh->bhde', k, v, k_w)
    return o, new_state


def prompt_retention(rq, rk, rv, log_g):
    b, s = rq.shape[:2]
    n = s // RET_CHUNK

    def to_chunks(t):
        return t.reshape(b, n, RET_CHUNK, RET_HEADS, RET_HEAD_DIM).swapaxes(0, 1)

    def step(carry, xs):
        q, k, v = xs
        o, new_carry = retention_chunk(q, k, v, carry, log_g)
        return new_carry, o

    init = jnp.zeros((b, RET_HEADS, RET_HEAD_DIM, RET_HEAD_DIM), jnp.float32)
    final, o = lax.scan(step, init, (to_chunks(rq), to_chunks(rk), to_chunks(rv)))
    return o.swapaxes(0, 1).reshape(b, s, RET_HEADS, RET_HEAD_DIM), final


def diff_scores(q, k, q_pos, k_pos):
    s = jnp.einsum('bqhcd,bkhcd->bhcqk', q.astype(jnp.float32), k.astype(jnp.float32)) * (DIFF_QK_DIM ** -0.5)
    dist = (q_pos[:, None] - k_pos[None, :]).astype(jnp.float32)
    s = s - alibi_slopes()[:, None, None, None] * dist
    return jnp.where(dist >= 0, s, -jnp.inf)


def diff_weights(s, lam):
    a = jax.nn.softmax(s, axis=-1)
    return a[:, :, 0] - lam * a[:, :, 1]


def prompt_diff_attention(dq, dk, dv, lam):
    b, s = dq.shape[:2]
    nb = s // Q_BLOCK
    q_blocks = dq.reshape(b, nb, Q_BLOCK, DIFF_HEADS, 2, DIFF_QK_DIM).swapaxes(0, 1)
    k_pos = jnp.arange(s, dtype=jnp.int32)
    vf = dv.astype(jnp.float32)

    def block(args):
        q_blk, bi = args
        q_pos = bi * Q_BLOCK + jnp.arange(Q_BLOCK, dtype=jnp.int32)
        w = diff_weights(diff_scores(q_blk, dk, q_pos, k_pos), lam)
        return jnp.einsum('bhqk,bkhe->bqhe', w, vf)

    o = lax.map(block, (q_blocks, jnp.arange(nb, dtype=jnp.int32)))
    return o.swapaxes(0, 1).reshape(b, s, DIFF_HEADS, DIFF_VDIM)


def merge_heads(ret_o, rg, diff_o, ret_gn_w, ret_gn_b, diff_subln_w, lam_init, w_o):
    b, t = ret_o.shape[:2]
    mu = jnp.mean(ret_o, axis=-1, keepdims=True)
    var = jnp.mean(jnp.square(ret_o - mu), axis=-1, keepdims=True)
    r = (ret_o - mu) * lax.rsqrt(var + EPS)
    r = r * ret_gn_w.astype(jnp.float32).reshape(RET_HEADS, RET_HEAD_DIM) + ret_gn_b.astype(jnp.float32).reshape(RET_HEADS, RET_HEAD_DIM)
    r = r.reshape(b, t, RET_WIDTH) * jax.nn.silu(rg.astype(jnp.float32))
    dn = diff_o * lax.rsqrt(jnp.mean(diff_o * diff_o, axis=-1, keepdims=True) + EPS)
    dn = dn * diff_subln_w.astype(jnp.float32).reshape(DIFF_HEADS, DIFF_VDIM) * (1.0 - lam_init)
    mixed = jnp.concatenate([r, dn.reshape(b, t, DIFF_WIDTH)], axis=-1)
    return mixed.astype(w_o.dtype) @ w_o


def channel_and_ple(h, ln2_w, w_ffn_in, w_ffn_out, p, ln_ple_w, w_ple_gate, w_ple_proj):
    gu = rmsnorm(h, ln2_w) @ w_ffn_in
    g, u = jnp.split(gu, 2, axis=-1)
    h = h + (jax.nn.silu(g) * u) @ w_ffn_out
    gate = jax.nn.sigmoid((rmsnorm(h, ln_ple_w) @ w_ple_gate).astype(jnp.float32))
    return h + (gate * (p @ w_ple_proj).astype(jnp.float32)).astype(h.dtype)


def setup_inputs(seed: int = 0) -> dict:
    key = jax.random.key(seed)
    ks = jax.random.split(key, 32)
    f32 = jnp.float32
    n_pages = PAST_LEN // PAGE_SIZE
    n_used = DEC_BATCH * n_pages
    n_phys = n_used + n_used // 4

    def nrm(k, shape, scale):
        return jax.random.normal(k, shape, f32) * scale

    def gain(k, shape):
        return 1.0 + 0.1 * jax.random.normal(k, shape, f32)

    page_table = jax.random.permutation(ks[5], n_phys)[:n_used].reshape(DEC_BATCH, n_pages).astype(jnp.int32)
    return {
        'x_prompt': nrm(ks[0], (BATCH, SEQ, D_MODEL), 1.0),
        'x_sample': nrm(ks[1], (DEC_BATCH, DEC_SEQ, D_MODEL), 1.0),
        'cache_k': nrm(ks[2], (DEPTH, n_phys, PAGE_SIZE, DIFF_HEADS, 2 * DIFF_QK_DIM), 1.0),
        'cache_v': nrm(ks[3], (DEPTH, n_phys, PAGE_SIZE, DIFF_HEADS, DIFF_VDIM), 1.0),
        'state_ret': nrm(ks[4], (DEPTH, DEC_BATCH, RET_HEADS, RET_HEAD_DIM, RET_HEAD_DIM), 0.5),
        'page_table': page_table,
        'p_prompt': nrm(ks[6], (DEPTH, BATCH, SEQ, PLE_DIM), 1.0),
        'p_sample': nrm(ks[7], (DEPTH, DEC_BATCH, DEC_SEQ, PLE_DIM), 1.0),
        'ln1_w': gain(ks[8], (DEPTH, D_MODEL)),
        'w_in': nrm(ks[9], (DEPTH, D_MODEL, PROJ_WIDTH), D_MODEL ** -0.5),
        'q_norm_w': gain(ks[10], (DEPTH, DIFF_QK_DIM)),
        'k_norm_w': gain(ks[11], (DEPTH, DIFF_QK_DIM)),
        'lambda_q1': nrm(ks[12], (DEPTH, DIFF_QK_DIM), 0.1),
        'lambda_k1': nrm(ks[13], (DEPTH, DIFF_QK_DIM), 0.1),
        'lambda_q2': nrm(ks[14], (DEPTH, DIFF_QK_DIM), 0.1),
        'lambda_k2': nrm(ks[15], (DEPTH, DIFF_QK_DIM), 0.1),
        'ret_gn_w': gain(ks[16], (DEPTH, RET_WIDTH)),
        'ret_gn_b': nrm(ks[17], (DEPTH, RET_WIDTH), 0.02),
        'diff_subln_w': gain(ks[18], (DEPTH, DIFF_WIDTH)),
        'w_o': nrm(ks[19], (DEPTH, MIX_WIDTH, D_MODEL), MIX_WIDTH ** -0.5),
        'ln2_w': gain(ks[20], (DEPTH, D_MODEL)),
        'w_ffn_in': nrm(ks[21], (DEPTH, D_MODEL, 2 * D_FF), D_MODEL ** -0.5),
        'w_ffn_out': nrm(ks[22], (DEPTH, D_FF, D_MODEL), D_FF ** -0.5),
        'ln_ple_w': gain(ks[23], (DEPTH, D_MODEL)),
        'w_ple_gate': nrm(ks[24], (DEPTH, D_MODEL, D_MODEL), D_MODEL ** -0.5),
        'w_ple_proj': nrm(ks[25], (DEPTH, PLE_DIM, D_MODEL), PLE_DIM ** -0.5),
    }


def reference(x_prompt, x_sample, cache_k, cache_v, state_ret, page_table, p_prompt, p_sample,
              ln1_w, w_in, q_norm_w, k_norm_w, lambda_q1, lambda_k1, lambda_q2, lambda_k2,
              ret_gn_w, ret_gn_b, diff_subln_w, w_o, ln2_w, w_ffn_in, w_ffn_out,
              ln_ple_w, w_ple_gate, w_ple_proj):
    f32 = jnp.float32
    log_g = retention_log_decay()
    db, ds = x_sample.shape[:2]
    past = page_table.shape[1] * cache_k.shape[2]
    q_pos_s = past + jnp.arange(ds, dtype=jnp.int32)
    k_pos_past = jnp.arange(past, dtype=jnp.int32)
    hp, hs = x_prompt, x_sample
    kp_l, vp_l, rp_l, ks_l, vs_l, rs_l = [], [], [], [], [], []
    for i in range(DEPTH):
        lam_init = 0.8 - 0.6 * math.exp(-0.3 * i)
        lam = (jnp.exp(jnp.sum(lambda_q1[i].astype(f32) * lambda_k1[i].astype(f32)))
               - jnp.exp(jnp.sum(lambda_q2[i].astype(f32) * lambda_k2[i].astype(f32))) + lam_init)

        b, s = hp.shape[:2]
        rq, rk, rv, rg, dq, dk, dv = project_heads(hp, ln1_w[i], w_in[i], q_norm_w[i], k_norm_w[i])
        ret_o, ret_fin = prompt_retention(rq, rk, rv, log_g)
        diff_o = prompt_diff_attention(dq, dk, dv, lam)
        hp = hp + merge_heads(ret_o, rg, diff_o, ret_gn_w[i], ret_gn_b[i], diff_subln_w[i], lam_init, w_o[i])
        hp = channel_and_ple(hp, ln2_w[i], w_ffn_in[i], w_ffn_out[i], p_prompt[i], ln_ple_w[i], w_ple_gate[i], w_ple_proj[i])
        kp_l.append(dk.reshape(b, s, DIFF_HEADS, 2 * DIFF_QK_DIM))
        vp_l.append(dv)
        rp_l.append(ret_fin)

        rq, rk, rv, rg, dq, dk, dv = project_heads(hs, ln1_w[i], w_in[i], q_norm_w[i], k_norm_w[i])
        ret_o, ret_new = retention_chunk(rq, rk, rv, state_ret[i].astype(f32), log_g)
        k_past = cache_k[i][page_table].reshape(db, past, DIFF_HEADS, 2, DIFF_QK_DIM)
        v_past = cache_v[i][page_table].reshape(db, past, DIFF_HEADS, DIFF_VDIM)
        sc = jnp.concatenate([diff_scores(dq, k_past, q_pos_s, k_pos_past),
                              diff_scores(dq, dk, q_pos_s, q_pos_s)], axis=-1)
        w = diff_weights(sc, lam)
        diff_o = (jnp.einsum('bhqk,bkhe->bqhe', w[..., :past], v_past.astype(f32))
                  + jnp.einsum('bhqk,bkhe->bqhe', w[..., past:], dv.astype(f32)))
        hs = hs + merge_heads(ret_o, rg, diff_o, ret_gn_w[i], ret_gn_b[i], diff_subln_w[i], lam_init, w_o[i])
        hs = channel_and_ple(hs, ln2_w[i], w_ffn_in[i], w_ffn_out[i], p_sample[i], ln_ple_w[i], w_ple_gate[i], w_ple_proj[i])
        ks_l.append(dk.reshape(db, ds, DIFF_HEADS, 2 * DIFF_QK_DIM))
        vs_l.append(dv)
        rs_l.append(ret_new)

    return (hp, hs, jnp.stack(kp_l), jnp.stack(vp_l), jnp.stack(rp_l), jnp.stack(ks_l), jnp.stack(vs_l), jnp.stack(rs_l))
```

```python
import math
from contextlib import ExitStack

import numpy as np
import concourse.bass as bass
import concourse.mybir as mybir
from concourse.bass_utils import run_bass_kernel_spmd

F32 = mybir.dt.float32
BF16 = mybir.dt.bfloat16
I32 = mybir.dt.int32
AF = mybir.ActivationFunctionType
ALU = mybir.AluOpType
AX = mybir.AxisListType

EPS = 1e-6
LAM_INIT = 0.8 - 0.6 * math.exp(-0.3 * 0)
D = 1024
KC = 8
PROJ = 3584


class Buf:
    __slots__ = ("name", "w", "r", "dsem", "psum")

    def __init__(self, name, psum=False):
        self.name = name
        self.w = None
        self.r = []
        self.dsem = None
        self.psum = psum


class _Rec:
    def __init__(self):
        self.calls = []

    def __getattr__(self, name):
        def f(*a, **k):
            self.calls.append((name, a, k))
            return None
        return f


def _record(fn):
    r = _Rec()
    fn(r)
    assert r.calls
    return r.calls


class Sched:
    COMPUTE = ("pe", "act", "dve", "pool")

    def __init__(self, nc, stack):
        self.nc = nc
        self.stack = stack
        self.lists = {e: [] for e in ("pe", "act", "dve", "pool", "sp")}
        self.sem = {e: stack.enter_context(nc.semaphore("s_" + e)) for e in self.COMPUTE}
        self.cnt = {e: 0 for e in self.COMPUTE}
        self.known = {e: {} for e in self.lists}
        self.dma_sems = {}
        self.nsem = 4

    def _need(self, eng, waits, tok):
        if tok is None:
            return
        sem, val = tok
        if eng == "pe" and sem is self.sem["pe"]:
            return
        if self.known[eng].get(id(sem), (None, 0))[1] >= val:
            return
        if id(sem) not in waits or waits[id(sem)][1] < val:
            waits[id(sem)] = (sem, val)

    def _waits(self, eng, reads, writes):
        waits = {}
        for b in reads:
            self._need(eng, waits, b.w)
            if b.psum:
                for t in b.r:
                    if t[0] is not self.sem.get(eng):
                        self._need(eng, waits, t)
        for b in writes:
            self._need(eng, waits, b.w)
            for t in b.r:
                self._need(eng, waits, t)
        for k, sv in waits.items():
            self.known[eng][k] = sv
        return list(waits.values())

    @staticmethod
    def _commit(tok, reads, writes):
        for b in reads:
            b.r.append(tok)
            if len(b.r) > 64:
                best = {}
                for s, v in b.r:
                    if id(s) not in best or best[id(s)][1] < v:
                        best[id(s)] = (s, v)
                b.r = list(best.values())
        for b in writes:
            b.w = tok
            b.r = []

    def op(self, eng, fn, reads=(), writes=()):
        waits = self._waits(eng, reads, writes)
        self.cnt[eng] += 1
        tok = (self.sem[eng], self.cnt[eng])
        self.lists[eng].append((waits, _record(fn), tok[0], 1))
        self._commit(tok, reads, writes)
        return tok

    def dma(self, q, fn, owner, reads=(), writes=(), inc=16):
        waits = self._waits(q, reads, writes)
        if owner.dsem is None:
            owner.dsem = {}
        if q not in owner.dsem:
            sem = self.stack.enter_context(self.nc.semaphore("d%s_%s" % (q, owner.name)))
            owner.dsem[q] = [sem, 0]
            self.dma_sems[id(sem)] = owner.dsem[q]
            self.nsem += 1
        owner.dsem[q][1] += inc
        tok = (owner.dsem[q][0], owner.dsem[q][1])
        self.lists[q].append((waits, _record(fn), tok[0], inc))
        self._commit(tok, reads, writes)
        return tok

    def barrier(self):
        toks = [(self.sem[e], self.cnt[e]) for e in self.COMPUTE if self.cnt[e] > 0]
        toks += [(s, c) for (s, c) in self.dma_sems.values()]
        for eng in self.lists:
            waits = {}
            for t in toks:
                self._need(eng, waits, t)
            for k, sv in waits.items():
                self.known[eng][k] = sv
            if waits:
                self.lists[eng].append((list(waits.values()), None, None, 0))

    def emit(self, block):
        def mk(name):
            lst = self.lists[name]

            def body(e):
                for waits, fn, sem, inc in lst:
                    for (s, v) in waits:
                        e.wait_ge(s, v)
                    if fn is not None:
                        for (mname, a, k) in fn:
                            ins = getattr(e, mname)(*a, **k)
                        ins.then_inc(sem, inc)
            return body

        block.tensor(mk("pe"))
        block.scalar(mk("act"))
        block.vector(mk("dve"))
        block.gpsimd(mk("pool"))
        block.sync(mk("sp"))


class T:
    def __init__(self, nc, stack, name, shape, dt):
        self.t = stack.enter_context(nc.sbuf_tensor("sb_" + name, list(shape), dt))
        self.b = Buf(name)
        self.shape = list(shape)

    def __getitem__(self, k):
        return self.t[k]


class Ring:
    def __init__(self, nc, stack, name, shape, dt, n):
        self.items = [T(nc, stack, "%s%d" % (name, i), shape, dt) for i in range(n)]
        self.i = 0

    def next(self):
        t = self.items[self.i % len(self.items)]
        self.i += 1
        return t


def mkap(a, dims):
    return bass.AP(a.tensor, a.offset, [[a.ap[0][0], a.ap[0][1]]] + [list(d) for d in dims])


class Cfg:
    def __init__(self, NT=16, DFF=2816, NPG=128, PAGE=128, NPHYS=5120, groups=None):
        self.NT, self.DFF, self.NPG, self.PAGE, self.NPHYS = NT, DFF, NPG, PAGE, NPHYS
        self.POSH = PAGE // 2
        self.NB = DFF // 128
        assert self.NB % 2 == 0
        self.NBH = self.NB // 2
        self.PAST = NPG * PAGE
        if groups is None:
            half = NT // 2
            groups = [list(range(half)), list(range(half, NT)) + [NT]]
        self.groups = groups
        off = {}
        c = 0
        for name, w in [("ident", 128), ("maskT", 128), ("maskS", 128), ("tabq", 4), ("tabk", 4),
                        ("tabq_s", 4), ("tabk_s", 4), ("gL", 4), ("abias", NT * 4), ("abias_s", 4),
                        ("alibi8", 8 * self.POSH), ("g4s", 16), ("mask16", 4), ("ones", 1)]:
            off[name] = (c, w)
            c += w
        self.coff = off
        self.NC = c
        self.CCW = 258
        self.ARW = 4 * 258 + 8 * 128


def make_consts(cfg):
    C = np.zeros((128, cfg.NC), np.float64)

    def put(name, arr):
        o, w = cfg.coff[name]
        C[:, o:o + w] = arr

    p = np.arange(128, dtype=np.float64)
    put("ident", np.eye(128))
    put("maskT", (p[:, None] <= p[None, :]).astype(np.float64))
    put("maskS", ((p[:, None] // 4 == p[None, :] // 4) & (p[:, None] % 4 <= p[None, :] % 4)).astype(np.float64))
    gam = 1.0 - np.exp2(-5.0 - np.arange(4))
    lg = np.log(gam)
    put("tabq", np.exp(lg[None, :] * (p[:, None] + 1.0)))
    put("tabk", np.exp(-lg[None, :] * (p[:, None] + 1.0)) * (128.0 ** -0.5))
    t4 = p % 4
    put("tabq_s", np.exp(lg[None, :] * (t4[:, None] + 1.0)))
    put("tabk_s", np.exp(-lg[None, :] * (t4[:, None] + 1.0)) * (128.0 ** -0.5))
    put("gL", np.tile(np.exp(lg * 128.0)[None, :], (128, 1)))
    slopes = np.exp2(-8.0 / 4 * np.arange(1, 5))
    ab = np.zeros((128, cfg.NT * 4))
    for dl in range(cfg.NT):
        for hh in range(4):
            ab[:, dl * 4 + hh] = slopes[hh] * (p - 127.0 - 128.0 * dl)
    put("abias", ab)
    put("abias_s", slopes[None, :] * t4[:, None])
    pos = np.arange(cfg.POSH, dtype=np.float64)
    al = np.zeros((128, 8 * cfg.POSH))
    for r in range(8):
        hh, jj = r % 4, r // 4
        kpos = p[:, None] * cfg.PAGE + jj * cfg.POSH + pos[None, :]
        al[:, r * cfg.POSH:(r + 1) * cfg.POSH] = 8.0 * slopes[hh] * (kpos - cfg.PAST)
    put("alibi8", al)
    g4s = np.zeros((128, 16))
    for sl in range(16):
        g4s[:, sl] = gam[sl % 4] ** 4
    put("g4s", g4s)
    m16 = np.zeros((128, 4))
    for t in range(16):
        m16[t, t // 4] = 1.0
    put("mask16", m16)
    put("ones", np.ones((128, 1)))
    return C.astype(np.float32)


def build(cfg):
    nc = bass.Bass("TRN2", target_bir_lowering=False)
    NT, DFF, NPG, PAGE, NPHYS = cfg.NT, cfg.DFF, cfg.NPG, cfg.PAGE, cfg.NPHYS
    POSH, NB, NBH, CCW = cfg.POSH, cfg.NB, cfg.NBH, cfg.CCW
    assert POSH * 8 <= 512 and POSH % 4 == 0

    def din(name, shape, dt=F32):
        return nc.dram_tensor(name, list(shape), dt, kind="ExternalInput").ap()

    def dout(name, shape):
        return nc.dram_tensor(name, list(shape), F32, kind="ExternalOutput").ap()

    x_d = din("x", [NT * 128, D]); xs_d = din("xs", [128, D])
    p_d = din("p", [NT * 128, 256]); psm_d = din("psm", [128, 256])
    HW = (POSH // 2) * 128
    ck_d = [[din("ck%d_%d" % (r, i), [NPHYS, HW]) for i in range(2)] for r in range(8)]
    cv_d = [[din("cv%d_%d" % (r, i), [NPHYS, HW]) for i in range(2)] for r in range(8)]
    ptT_d = din("ptT", [NPG, 4], I32)
    stin_d = din("st_in", [16, 128, 128])
    win_d = din("w_in", [D, PROJ]); wo_d = din("w_o", [D, D]); wsel_d = din("w_sel", [D, 2048])
    wfi_d = din("w_fi", [NB, 128, 8 * 256]); wfo_d = din("w_fo", [DFF, D])
    wpg_d = din("w_pg", [D, D]); wpp_d = din("w_pp", [256, D])
    consts_d = din("consts", [128, cfg.NC]); lncols_d = din("lncols", [128, 24])
    subln_d = din("sublncol", [128, 4]); vecs_d = din("vecs", [1, 1408])

    y_d = dout("y", [NT * 128, D]); ys_d = dout("ys", [128, D])
    ko_d = dout("ko", [NT * 128, 512]); vo_d = dout("vo", [NT * 128, 512])
    reto_d = dout("reto", [4, 128, 128]); kso_d = dout("kso", [128, 512]); vso_d = dout("vso", [128, 512])
    retso_d = dout("retso", [16, 128, 128])

    h1s_d = nc.dram_tensor("h1s", [(NT + 1) * 128, D], F32).ap()
    cctm_d = nc.dram_tensor("cc_tm", [16, 8 * 258], F32).ap()
    B_cctm = Buf("cctm")
    B_h1s = [Buf("h1s%d" % i) for i in range(NT + 1)]
    out_bufs = []

    with ExitStack() as gst:
        S = Sched(nc, gst)

        def gT(name, shape, dt):
            return T(nc, gst, name, shape, dt)

        PSB = []
        for k in range(8):
            t = gst.enter_context(nc.psum_tensor("psb%d" % k, [128, 512], F32))
            PSB.append((t, Buf("psb%d" % k, psum=True)))
        ps_ctr = {}

        def ps_next(pool, banks):
            i = ps_ctr.get(pool, 0)
            ps_ctr[pool] = i + 1
            return PSB[banks[i % len(banks)]]

        Cc = gT("consts", [128, cfg.NC], F32)
        lnc = gT("lncols", [128, 24], F32)
        subc = gT("sublncol", [128, 4], F32)
        vec = gT("vecs", [128, 1408], F32)
        S.dma("sp", lambda e: e.dma_start(out=Cc[:], in_=consts_d[:, :]), Cc.b, writes=[Cc.b])
        S.dma("sp", lambda e: e.dma_start(out=lnc[:], in_=lncols_d[:, :]), lnc.b, writes=[lnc.b])
        S.dma("sp", lambda e: e.dma_start(out=subc[:], in_=subln_d[:, :]), subc.b, writes=[subc.b])
        S.dma("sp", lambda e: e.dma_start(out=vec[:], in_=vecs_d[0, :].partition_broadcast(128)), vec.b, writes=[vec.b])

        def cs(name, rows=128):
            o, w = cfg.coff[name]
            return Cc[0:rows, o:o + w]

        V_QW, V_KW, V_L = 0, 64, 128
        V_GNW, V_GNB = 384, 896
        identb = gT("identb", [128, 128], BF16)
        S.op("dve", lambda e: e.tensor_copy(out=identb[:], in_=cs("ident")), reads=[Cc.b], writes=[identb.b])
        epsc = gT("epsc", [128, 1], F32)
        S.op("pool", lambda e: e.memset(epsc[:], EPS), writes=[epsc.b])

        lamt = gT("lamt", [128, 128], F32)
        lams = gT("lams", [128, 2], F32)
        lame = gT("lame", [128, 2], F32)
        nlam = gT("nlam", [128, 1], F32)
        S.op("dve", lambda e: e.tensor_tensor(
            out=lamt[:].rearrange("p (a b) -> p a b", a=2),
            in0=mkap(vec[:, 128:129], [[64, 2], [1, 64]]),
            in1=mkap(vec[:, 256:257], [[64, 2], [1, 64]]), op=ALU.mult),
            reads=[vec.b], writes=[lamt.b])
        S.op("dve", lambda e: e.tensor_reduce(out=lams[:], in_=lamt[:].rearrange("p (a b) -> p a b", a=2),
                                              axis=AX.X, op=ALU.add), reads=[lamt.b], writes=[lams.b])
        S.op("act", lambda e: e.activation(out=lame[:], in_=lams[:], func=AF.Exp), reads=[lams.b], writes=[lame.b])
        S.op("dve", lambda e: e.tensor_tensor(out=nlam[:], in0=lame[:, 1:2], in1=lame[:, 0:1], op=ALU.subtract),
             reads=[lame.b], writes=[nlam.b])
        S.op("dve", lambda e: e.tensor_scalar(out=nlam[:], in0=nlam[:], scalar1=-LAM_INIT, scalar2=None, op0=ALU.add),
             reads=[nlam.b], writes=[nlam.b])

        def rstd_from_ss(ss, rstd, n, inv_count, tag_reads=()):
            S.op("act", lambda e: e.activation(out=rstd[:], in_=ss[:], func=AF.Ln, bias=epsc[0:ss.shape[0], :],
                                               scale=inv_count), reads=[ss.b, epsc.b], writes=[rstd.b])
            S.op("act", lambda e: e.activation(out=rstd[:], in_=rstd[:], func=AF.Exp, scale=-0.5),
                 reads=[rstd.b], writes=[rstd.b])

        def norm_T(src, rows, junk, ss, rstd, xn, xT_dst, xT_buf, pool_T):
            S.op("act", lambda e: e.activation(out=junk[0:rows, :], in_=src[0:rows, :], func=AF.Square,
                                               accum_out=ss[0:rows, :]), reads=[src.b], writes=[junk.b, ss.b])
            S.op("act", lambda e: e.activation(out=rstd[0:rows, :], in_=ss[0:rows, :], func=AF.Ln,
                                               bias=epsc[0:rows, :], scale=1.0 / D),
                 reads=[ss.b, epsc.b], writes=[rstd.b])
            S.op("act", lambda e: e.activation(out=rstd[0:rows, :], in_=rstd[0:rows, :], func=AF.Exp, scale=-0.5),
                 reads=[rstd.b], writes=[rstd.b])
            S.op("dve", lambda e: e.tensor_scalar(out=xn[0:rows, :], in0=src[0:rows, :], scalar1=rstd[0:rows, :],
                                                  scalar2=None, op0=ALU.mult),
                 reads=[src.b, rstd.b], writes=[xn.b])
            pt, pb = ps_next("T", pool_T)
            ptb = pt[:].bitcast(BF16)

            def tr(e):
                for k in range(KC):
                    ins = e.transpose(out=ptb[:, k * rows:(k + 1) * rows], in_=xn[0:rows, k * 128:(k + 1) * 128],
                                      identity=identb[0:rows, 0:rows])
                return ins
            S.op("pe", tr, reads=[xn.b, identb.b], writes=[pb])
            S.op("dve", lambda e: e.tensor_copy(out=xT_dst(None), in_=ptb[:, 0:KC * rows]), reads=[pb], writes=[xT_buf])

        def cast_w(eng, dst_ap, src_ap, col_ap, rd, wr, extra=None):
            if extra is None:
                S.op(eng, lambda e: e.tensor_tensor(out=dst_ap, in0=src_ap, in1=col_ap, op=ALU.mult), reads=rd, writes=wr)
            else:
                S.op(eng, lambda e: e.scalar_tensor_tensor(out=dst_ap, in0=src_ap, scalar=extra, in1=col_ap,
                                                           op0=ALU.mult, op1=ALU.mult), reads=rd, writes=wr)

        cross_sb = gT("cross_sb", [128, 512], F32)
        pps_sb = gT("pps_sb", [128, 4, 258], F32)
        S.op("pool", lambda e: e.memset(pps_sb[:], 0.0), writes=[pps_sb.b])
        with ExitStack() as st:
            def sT(name, shape, dt):
                return T(nc, st, name, shape, dt)
            PT_T = [0, 1]; PT_M = [2, 3]; PT_SC = [4, 5]; PT_OA = [6, 7]
            HP = POSH // 2

            xs_t = sT("s_xs", [128, D], F32)
            S.dma("sp", lambda e: e.dma_start(out=xs_t[:], in_=xs_d[:, :]), xs_t.b, writes=[xs_t.b])
            ptT = sT("s_ptT", [NPG, 4], I32)
            S.dma("sp", lambda e: e.dma_start(out=ptT[:], in_=ptT_d[:, :]), ptT.b, writes=[ptT.b])
            st32 = sT("s_st32", [128, 16, 128], F32)
            S.dma("sp", lambda e: e.dma_start(out=st32[:], in_=stin_d.rearrange("b d e -> d b e")), st32.b, writes=[st32.b])
            stbf = sT("s_stbf", [128, 16, 128], BF16)
            S.op("pool", lambda e: e.tensor_copy(out=stbf[:], in_=st32[:]), reads=[st32.b], writes=[stbf.b])
            junk = sT("s_junk", [128, D], BF16)
            ss = sT("s_ss", [128, 1], F32); rstd = sT("s_rstd", [128, 1], F32)
            xn = sT("s_xn", [128, D], BF16)
            xsT = sT("s_xsT", [128, KC, 128], BF16)
            norm_T(xs_t, 128, junk, ss, rstd, xn, lambda _: xsT[:].rearrange("p k t -> p (k t)"), xsT.b, PT_T)

            rqo = sT("s_rqo", [128, 512], BF16); rko = sT("s_rko", [128, 512], BF16); rvo = sT("s_rvo", [128, 512], BF16)
            qn = sT("s_qn", [128, 512], BF16)
            sq = sT("s_sq", [128, 512], F32); qt = sT("s_qt", [128, 512], F32)
            ssq = sT("s_ssq", [128, 8], F32); rs8 = sT("s_rs8", [128, 8], F32)
            with ExitStack() as st_w:
                wst_r = Ring(nc, st_w, "s_wst", [128, 8, 512], F32, 2)
                wpc_r = Ring(nc, st_w, "s_wpc", [128, 8, 512], BF16, 2)
                for pc in range(4):
                    wst = wst_r.next(); wpc = wpc_r.next()
                    S.dma("sp", lambda e, wst=wst, pc=pc: e.dma_start(
                        out=wst[:], in_=wsel_d[:, pc * 512:(pc + 1) * 512].rearrange("(c p) n -> p c n", p=128)),
                        wst.b, writes=[wst.b])
                    cast_w("pool" if pc % 2 == 0 else "dve", wpc[:], wst[:], mkap(lnc[:, 0:1], [[1, 8], [0, 512]]), [wst.b, lnc.b], [wpc.b])
                    pm, pmb = ps_next("M", PT_M)

                    def mm_p(e, pm=pm, wpc=wpc):
                        for k in range(KC):
                            ins = e.matmul(pm[:, :], lhsT=xsT[:, k, :], rhs=wpc[:, k, :], start=(k == 0), stop=(k == KC - 1))
                        return ins
                    S.op("pe", mm_p, reads=[xsT.b, wpc.b], writes=[pmb])
                    if pc == 0 or pc == 1:
                        dst = rqo if pc == 0 else rko
                        tab = cs("tabq_s") if pc == 0 else cs("tabk_s")
                        S.op("dve", lambda e, pm=pm, dst=dst, tab=tab: e.tensor_tensor(
                            out=dst[:].rearrange("p (a b) -> p a b", a=4), in0=pm[:, :].rearrange("p (a b) -> p a b", a=4),
                            in1=mkap(tab[:, 0:1], [[1, 4], [0, 128]]), op=ALU.mult), reads=[pmb, Cc.b], writes=[dst.b])
                    elif pc == 2:
                        S.op("act", lambda e, pm=pm: e.activation(out=rvo[:], in_=pm[:, :], func=AF.Copy), reads=[pmb], writes=[rvo.b])
                    else:
                        S.op("act", lambda e, pm=pm: e.activation(out=sq[:], in_=pm[:, :], func=AF.Square), reads=[pmb], writes=[sq.b])
                        S.op("dve", lambda e: e.tensor_reduce(out=ssq[:], in_=sq[:].rearrange("p (a b) -> p a b", a=8), axis=AX.X, op=ALU.add),
                             reads=[sq.b], writes=[ssq.b])
                        S.op("act", lambda e: e.activation(out=rs8[:], in_=ssq[:], func=AF.Ln, bias=epsc[:], scale=1.0 / 64), reads=[ssq.b, epsc.b], writes=[rs8.b])
                        S.op("act", lambda e: e.activation(out=rs8[:], in_=rs8[:], func=AF.Exp, scale=-0.5), reads=[rs8.b], writes=[rs8.b])
                        S.op("dve", lambda e, pm=pm: e.tensor_tensor(out=qt[:].rearrange("p (a b) -> p a b", a=8), in0=pm[:, :].rearrange("p (a b) -> p a b", a=8),
                                                                    in1=mkap(rs8[:, 0:1], [[1, 8], [0, 64]]), op=ALU.mult), reads=[pmb, rs8.b], writes=[qt.b])
                        S.op("dve", lambda e: e.tensor_tensor(out=qn[:].rearrange("p (a b) -> p a b", a=8), in0=qt[:].rearrange("p (a b) -> p a b", a=8),
                                                              in1=mkap(vec[:, V_QW:V_QW + 1], [[0, 8], [1, 64]]), op=ALU.mult), reads=[qt.b, vec.b], writes=[qn.b])
                S.barrier()

            pt, ptb_ = ps_next("T", PT_T)
            ptv = pt[:].bitcast(BF16)

            def trq(e, ptv=ptv):
                for hh in range(4):
                    ins = e.transpose(out=ptv[:, hh * 128:(hh + 1) * 128], in_=qn[:, hh * 128:(hh + 1) * 128], identity=identb[:])
                return ins
            S.op("pe", trq, reads=[qn.b, identb.b], writes=[ptb_])
            qbd = sT("s_qbd", [128, 4 * 4 * 8], BF16)
            S.op("pool", lambda e: e.memset(qbd[:], 0.0), writes=[qbd.b])
            S.op("dve", lambda e, ptv=ptv: e.tensor_copy(out=mkap(qbd[0:64, 0:1], [[32, 4], [8, 4], [1, 4]]),
                                                         in_=mkap(ptv[0:64, 0:1], [[128, 4], [4, 4], [1, 4]])),
                 reads=[ptb_, qbd.b], writes=[qbd.b])
            S.op("dve", lambda e, ptv=ptv: e.tensor_copy(out=mkap(qbd[64:128, 4:5], [[32, 4], [8, 4], [1, 4]]),
                                                         in_=mkap(ptv[64:128, 0:1], [[128, 4], [4, 4], [1, 4]])),
                 reads=[ptb_, qbd.b], writes=[qbd.b])

            pt, ptb_ = ps_next("T", PT_T)
            ptv = pt[:].bitcast(BF16)

            def trr(e, ptv=ptv):
                for hh in range(4):
                    ins = e.transpose(out=ptv[:, hh * 128:(hh + 1) * 128], in_=rqo[:, hh * 128:(hh + 1) * 128], identity=identb[:])
                return ins
            S.op("pe", trr, reads=[rqo.b, identb.b], writes=[ptb_])
            qm = sT("s_qm", [128, 16 * 128], BF16)
            S.op("pool", lambda e: e.memset(qm[:], 0.0), writes=[qm.b])
            S.op("dve", lambda e, ptv=ptv: e.tensor_copy(out=mkap(qm[:, 0:1], [[128, 4], [516, 4], [1, 4]]),
                                                         in_=mkap(ptv[:, 0:1], [[128, 4], [4, 4], [1, 4]])),
                 reads=[ptb_, qm.b], writes=[qm.b])
            km = sT("s_km", [128, 16 * 128], BF16)
            S.op("dve", lambda e: e.tensor_tensor(out=km[:].rearrange("p (b h d) -> p b h d", b=4, h=4),
                                                  in0=mkap(rko[:, 0:1], [[0, 4], [128, 4], [1, 128]]),
                                                  in1=mkap(cs("mask16")[:, 0:1], [[1, 4], [0, 4], [0, 128]]), op=ALU.mult),
                 reads=[rko.b, Cc.b], writes=[km.b])
            pm, pmb = ps_next("M", PT_M)

            def mm_cross(e, pm=pm):
                for hh in range(4):
                    for bl in range(4):
                        sl = bl * 4 + hh
                        ins = e.matmul(pm[:, hh * 128:(hh + 1) * 128], lhsT=qm[:, sl * 128:(sl + 1) * 128], rhs=stbf[:, sl, :],
                                       start=(bl == 0), stop=(bl == 3))
                return ins
            S.op("pe", mm_cross, reads=[qm.b, stbf.b], writes=[pmb])
            S.op("act", lambda e, pm=pm: e.activation(out=cross_sb[:], in_=pm[:, :], func=AF.Copy), reads=[pmb], writes=[cross_sb.b])
            stnew = sT("s_stnew", [128, 16, 128], F32)
            for g4 in range(4):
                pm, pmb = ps_next("M", PT_M)

                def mm_kv(e, g4=g4, pm=pm):
                    for bb in range(4):
                        sl = g4 * 4 + bb
                        ins = e.matmul(pm[:, bb * 128:(bb + 1) * 128], lhsT=km[:, sl * 128:(sl + 1) * 128], rhs=rvo[:, bb * 128:(bb + 1) * 128],
                                       start=True, stop=True)
                    return ins
                S.op("pe", mm_kv, reads=[km.b, rvo.b], writes=[pmb])
                S.op("dve", lambda e, g4=g4, pm=pm: e.tensor_tensor(
                    out=stnew[:, g4 * 4:(g4 + 1) * 4, :].rearrange("p a b -> p (a b)"), in0=pm[:, :],
                    in1=st32[:, g4 * 4:(g4 + 1) * 4, :].rearrange("p a b -> p (a b)"), op=ALU.add),
                    reads=[pmb, st32.b, stnew.b], writes=[stnew.b])
            S.op("pool", lambda e: e.tensor_tensor(out=stnew[:], in0=stnew[:], in1=mkap(cs("g4s")[:, 0:1], [[1, 16], [0, 128]]), op=ALU.mult),
                 reads=[stnew.b, Cc.b], writes=[stnew.b])
            S.dma("sp", lambda e: e.dma_start(out=retso_d.rearrange("b d e -> d b e"), in_=stnew[:]), stnew.b, reads=[stnew.b])

            KG = Ring(nc, st, "s_kg", [NPG, HW], F32, 2)
            VG = Ring(nc, st, "s_vg", [NPG, HW], F32, 1)
            VB = Ring(nc, st, "s_vb", [NPG, POSH * 128], BF16, 2)
            B_vbh = [[Buf("vbh%d_%d" % (i, hf)) for hf in range(2)] for i in range(2)]
            KTr = Ring(nc, st, "s_kt", [128, 4 * NPG], BF16, 3)
            SBt = Ring(nc, st, "s_sb", [NPG, POSH * 8], F32, 2)
            ATr = Ring(nc, st, "s_at", [NPG, POSH * 8], BF16, 2)
            Rr = Ring(nc, st, "s_r", [NPG, 8], F32, 2)
            stall = sT("s_stall", [8, 32, 130], F32)
            evi = 0
            for it in range(32):
                r8, bl = it // 4, it % 4
                hh = r8 % 4
                alib = cs("alibi8", NPG)[:, r8 * POSH:(r8 + 1) * POSH]
                qb = qbd[:, (hh * 4 + bl) * 8:(hh * 4 + bl + 1) * 8]
                vb = VB.next()
                vbh = B_vbh[it % 2]
                psc, pscb = ps_next("SC", PT_SC)
                for hf in range(2):
                    kg = KG.next(); vg = VG.next()
                    S.dma("pool", lambda e, kg=kg, bl=bl, hf=hf, r8=r8: e.indirect_dma_start(
                        out=kg[:], out_offset=None, in_=ck_d[r8][hf][:, :],
                        in_offset=bass.IndirectOffsetOnAxis(ap=ptT[:, bl:bl + 1], axis=0)), kg.b, reads=[ptT.b], writes=[kg.b])
                    S.dma("pool", lambda e, vg=vg, bl=bl, hf=hf, r8=r8: e.indirect_dma_start(
                        out=vg[:], out_offset=None, in_=cv_d[r8][hf][:, :],
                        in_offset=bass.IndirectOffsetOnAxis(ap=ptT[:, bl:bl + 1], axis=0)), vg.b, reads=[ptT.b], writes=[vg.b])
                    S.op("pool", lambda e, vb=vb, vg=vg, hf=hf: e.tensor_copy(out=vb[:, hf * HW:(hf + 1) * HW], in_=vg[:]),
                         reads=[vg.b, vbh[hf]], writes=[vbh[hf]])
                    for pg in range(HP // 4):
                        ptr, ptrb = ps_next("T", PT_T)

                        def trk(e, pg=pg, kg=kg, ptr=ptr):
                            for k in range(4):
                                pl = pg * 4 + k
                                ins = e.transpose(out=ptr[:, k * NPG:(k + 1) * NPG], in_=kg[:, pl * 128:(pl + 1) * 128],
                                                  identity=cs("ident", NPG)[:, 0:NPG])
                            return ins
                        S.op("pe", trk, reads=[kg.b, Cc.b], writes=[ptrb])
                        kt = KTr.next()
                        if evi % 2 == 0:
                            S.op("dve", lambda e, kt=kt, ptr=ptr: e.tensor_copy(out=kt[:], in_=ptr[:, 0:4 * NPG]), reads=[ptrb], writes=[kt.b])
                        else:
                            S.op("act", lambda e, kt=kt, ptr=ptr: e.activation(out=kt[:], in_=ptr[:, 0:4 * NPG], func=AF.Copy), reads=[ptrb], writes=[kt.b])
                        evi += 1

                        def mms(e, pg=pg, kt=kt, psc=psc, qb=qb, hf=hf):
                            for k in range(4):
                                pos = hf * HP + pg * 4 + k
                                ins = e.matmul(psc[0:NPG, pos * 8:(pos + 1) * 8], lhsT=kt[:, k * NPG:(k + 1) * NPG], rhs=qb,
                                               start=True, stop=True)
                            return ins
                        S.op("pe", mms, reads=[kt.b, qbd.b], writes=[pscb])
                sbt = SBt.next(); at = ATr.next(); rr = Rr.next()
                S.op("dve", lambda e, sbt=sbt, psc=psc, alib=alib: e.tensor_tensor(
                    out=sbt[:].rearrange("p (a b) -> p a b", b=8), in0=psc[0:NPG, 0:POSH * 8].rearrange("p (a b) -> p a b", b=8),
                    in1=mkap(alib[:, 0:1], [[1, POSH], [0, 8]]), op=ALU.add), reads=[pscb, Cc.b], writes=[sbt.b])
                S.op("act", lambda e, sbt=sbt, at=at: e.activation(out=at[:], in_=sbt[:], func=AF.Exp, scale=0.125), reads=[sbt.b], writes=[at.b])
                S.op("dve", lambda e, at=at, rr=rr: e.tensor_reduce(out=rr[:], in_=at[:].rearrange("p (a b) -> p b a", b=8), axis=AX.X, op=ALU.add),
                     reads=[at.b], writes=[rr.b])
                poa, poab = ps_next("OA", PT_OA)

                def mmo(e, at=at, vb=vb, poa=poa, rr=rr):
                    for pos in range(POSH):
                        e.matmul(poa[0:8, 0:128], lhsT=at[:, pos * 8:(pos + 1) * 8], rhs=vb[:, pos * 128:(pos + 1) * 128],
                                 start=(pos == 0), stop=(pos == POSH - 1))
                    return e.matmul(poa[0:8, 128:129], lhsT=rr[:], rhs=cs("ones", NPG), start=True, stop=True)
                S.op("pe", mmo, reads=[at.b, vbh[0], vbh[1], rr.b, Cc.b], writes=[poab])
                S.op("dve", lambda e, poa=poa, it=it: e.tensor_copy(out=stall[:, it, 0:129], in_=poa[0:8, 0:129]), reads=[poab, stall.b], writes=[stall.b])
            for c2 in range(2):
                dst = bass.AP(cctm_d.tensor, cctm_d.offset + c2 * 129, [[8 * 258, 4], [258, 8], [4 * 8 * 258, 4], [1, 129]])
                S.dma("sp", lambda e, c2=c2, dst=dst: e.dma_start(
                    out=dst, in_=stall[c2 * 4:(c2 + 1) * 4, :, 0:129].rearrange("p (r b) e -> p r b e", r=8)), stall.b,
                    reads=[stall.b], writes=[B_cctm])
            tokmaj = sT("s_tokmaj", [16, 8, 258], F32)
            S.dma("sp", lambda e: e.dma_start(out=tokmaj[:].rearrange("p a b -> p (a b)"), in_=cctm_d[:, :]), tokmaj.b, reads=[B_cctm], writes=[tokmaj.b])
            S.op("dve", lambda e: e.tensor_tensor(out=pps_sb[0:16, :, :], in0=tokmaj[:, 0:4, :], in1=tokmaj[:, 4:8, :], op=ALU.add),
                 reads=[tokmaj.b, pps_sb.b], writes=[pps_sb.b])
            S.barrier()

        with ExitStack() as st:
          if "skipA" not in DBG:
            def sT(name, shape, dt):
                return T(nc, st, name, shape, dt)
            PT_T = [0, 1]; PT_M = [2, 3]; PT_DS = [4, 5]; PT_DA = [6, 7]

            winb = sT("a_winb", [128, KC, PROJ], BF16)
            wob = sT("a_wob", [128, KC, D], BF16)
            with ExitStack() as st_w:
                stg = Ring(nc, st_w, "a_stg", [128, PROJ], F32, 2)
                for c in range(KC):
                    sg_ = stg.next()
                    S.dma("sp", lambda e, sg_=sg_, c=c: e.dma_start(out=sg_[:], in_=win_d[c * 128:(c + 1) * 128, :]), sg_.b, writes=[sg_.b])
                    for (eng, c0, c1) in (("pool", 0, 1024), ("dve", 1024, 2304), ("act", 2304, PROJ)):
                        if eng == "act":
                            S.op(eng, lambda e, sg_=sg_, c=c, c0=c0, c1=c1: e.activation(out=winb[:, c, c0:c1], in_=sg_[:, c0:c1], func=AF.Copy, scale=lnc[:, c:c + 1]),
                                 reads=[sg_.b, lnc.b, winb.b], writes=[winb.b])
                        else:
                            S.op(eng, lambda e, sg_=sg_, c=c, c0=c0, c1=c1: e.tensor_scalar(out=winb[:, c, c0:c1], in0=sg_[:, c0:c1], scalar1=lnc[:, c:c + 1], scalar2=None, op0=ALU.mult),
                                 reads=[sg_.b, lnc.b, winb.b], writes=[winb.b])
                for c in range(KC):
                    sg_ = stg.next()
                    S.dma("sp", lambda e, sg_=sg_, c=c: e.dma_start(out=sg_[:, 0:D], in_=wo_d[c * 128:(c + 1) * 128, :]), sg_.b, writes=[sg_.b])
                    eng = "pool" if c % 2 == 0 else "dve"
                    if c < 4:
                        S.op(eng, lambda e, sg_=sg_, c=c: e.tensor_copy(out=wob[:, c, :], in_=sg_[:, 0:D]), reads=[sg_.b, wob.b], writes=[wob.b])
                    else:
                        S.op(eng, lambda e, sg_=sg_, c=c: e.tensor_scalar(out=wob[:, c, :], in0=sg_[:, 0:D], scalar1=subc[:, c - 4:c - 3],
                                                                          scalar2=1.0 - LAM_INIT, op0=ALU.mult, op1=ALU.mult),
                             reads=[sg_.b, subc.b, wob.b], writes=[wob.b])
                S.barrier()

            dkT_all = sT("a_dkT", [128, 4, (NT + 1) * 128], BF16)
            vaug = sT("a_vaug", [128, NT + 1, 4 * 130], BF16)
            S.op("pool", lambda e: e.memset(vaug[:], 1.0), writes=[vaug.b])
            B_dkT = [Buf("dkT%d" % i) for i in range(NT + 1)]
            B_va = [Buf("va%d" % i) for i in range(NT + 1)]
            for bb in B_va:
                bb.w = vaug.b.w
            state32 = sT("a_state32", [128, 512], F32)
            statebf = sT("a_statebf", [128, 512], BF16)

            xt_r = Ring(nc, st, "a_xt", [128, D], F32, 3)
            junk = sT("a_junk", [128, D], BF16)
            ss_r = Ring(nc, st, "a_ss", [128, 1], F32, 2); rstd_r = Ring(nc, st, "a_rstd", [128, 1], F32, 2)
            xn_r = Ring(nc, st, "a_xn", [128, D], BF16, 1)
            xnT_r = Ring(nc, st, "a_xnT", [128, KC * 128], BF16, 2)
            rq_r = Ring(nc, st, "a_rq", [128, 512], BF16, 2); rk_r = Ring(nc, st, "a_rk", [128, 512], BF16, 2)
            rv_r = Ring(nc, st, "a_rv", [128, 512], BF16, 2); sg_r = Ring(nc, st, "a_sg", [128, 512], BF16, 2)
            tmp_r = Ring(nc, st, "a_tmp", [128, 512], F32, 3)
            sq_r = qt_r = ro_r = rsq_r = tmp_r
            ssq_r = Ring(nc, st, "a_ssq", [128, 8], F32, 2); rs8_r = Ring(nc, st, "a_rs8", [128, 8], F32, 2)
            qn_r = Ring(nc, st, "a_qn", [128, 512], BF16, 2); kn_r = Ring(nc, st, "a_kn", [128, 512], BF16, 2)
            k32_r = Ring(nc, st, "a_k32", [128, 512], F32, 1); v32_r = Ring(nc, st, "a_v32", [128, 512], F32, 1)
            rqT_r = Ring(nc, st, "a_rqT", [128, 512], BF16, 2); rkT_r = Ring(nc, st, "a_rkT", [128, 512], BF16, 2)
            dqT_r = Ring(nc, st, "a_dqT", [128, 1024], BF16, 2)
            for _t in dqT_r.items:
                S.op("pool", lambda e, _t=_t: e.memset(_t[:], 0.0), writes=[_t.b])
            sm_r = Ring(nc, st, "a_sm", [128, 512], BF16, 1)
            st4_r = Ring(nc, st, "a_st4", [128, 16], F32, 2)
            mixed_r = Ring(nc, st, "a_mixed", [128, D], BF16, 1)
            mixT_r = Ring(nc, st, "a_mixT", [128, KC * 128], BF16, 1)
            at_r = Ring(nc, st, "a_at", [128, 256], BF16, 4)
            acc_r = Ring(nc, st, "a_acc", [128, 2 * 130], F32, 2)
            fin_r = Ring(nc, st, "a_fin", [128, 8], F32, 2)
            o1_r = Ring(nc, st, "a_o1", [128, 128], F32, 2); dd_r = Ring(nc, st, "a_dd", [128, 128], F32, 2)
            jk_r = Ring(nc, st, "a_jk", [128, 128], BF16, 2)

            def attn_tile(i):
                smp = (i == NT)
                xt = xt_r.next()
                src = xs_d[:, :] if smp else x_d[i * 128:(i + 1) * 128, :]
                S.dma("sp", lambda e: e.dma_start(out=xt[:], in_=src), xt.b, writes=[xt.b])
                ss = ss_r.next(); rstd = rstd_r.next(); xn = xn_r.next(); xnT = xnT_r.next()
                norm_T(xt, 128, junk, ss, rstd, xn, lambda _: xnT[:], xnT.b, PT_T)
                if ASTOP <= 1:
                    return
                tq = cs("tabq_s") if smp else cs("tabq")
                tk = cs("tabk_s") if smp else cs("tabk")
                rq = rq_r.next(); rk = rk_r.next(); rv = rv_r.next(); sg = sg_r.next()
                qn = qn_r.next(); kn = kn_r.next(); k32 = k32_r.next(); v32 = v32_r.next()
                for n in range(7):
                    pm, pmb = ps_next("M", PT_M)

                    def mm(e, n=n, pm=pm):
                        for k in range(KC):
                            ins = e.matmul(pm[:, :], lhsT=xnT[:, k * 128:(k + 1) * 128], rhs=winb[:, k, n * 512:(n + 1) * 512],
                                           start=(k == 0), stop=(k == KC - 1))
                        return ins
                    S.op("pe", mm, reads=[xnT.b, winb.b], writes=[pmb])
                    if n == 0 or n == 1:
                        dst = rq if n == 0 else rk
                        tab = tq if n == 0 else tk
                        S.op("dve", lambda e, pm=pm, dst=dst, tab=tab: e.tensor_tensor(
                            out=dst[:].rearrange("p (a b) -> p a b", a=4), in0=pm[:, :].rearrange("p (a b) -> p a b", a=4),
                            in1=mkap(tab[:, 0:1], [[1, 4], [0, 128]]), op=ALU.mult), reads=[pmb, Cc.b], writes=[dst.b])
                    elif n == 2:
                        S.op("act", lambda e, pm=pm: e.activation(out=rv[:], in_=pm[:, :], func=AF.Copy), reads=[pmb], writes=[rv.b])
                    elif n == 3:
                        S.op("act", lambda e, pm=pm: e.activation(out=sg[:], in_=pm[:, :], func=AF.Silu), reads=[pmb], writes=[sg.b])
                    elif n == 4 or n == 5:
                        sq = sq_r.next(); ssq = ssq_r.next(); rs8 = rs8_r.next(); qt = qt_r.next()
                        S.op("act", lambda e, pm=pm, sq=sq: e.activation(out=sq[:], in_=pm[:, :], func=AF.Square), reads=[pmb], writes=[sq.b])
                        S.op("dve", lambda e, sq=sq, ssq=ssq: e.tensor_reduce(out=ssq[:], in_=sq[:].rearrange("p (a b) -> p a b", a=8),
                                                                              axis=AX.X, op=ALU.add), reads=[sq.b], writes=[ssq.b])
                        S.op("act", lambda e, ssq=ssq, rs8=rs8: e.activation(out=rs8[:], in_=ssq[:], func=AF.Ln, bias=epsc[:], scale=1.0 / 64),
                             reads=[ssq.b, epsc.b], writes=[rs8.b])
                        S.op("act", lambda e, rs8=rs8: e.activation(out=rs8[:], in_=rs8[:], func=AF.Exp, scale=-0.5), reads=[rs8.b], writes=[rs8.b])
                        S.op("dve", lambda e, pm=pm, qt=qt, rs8=rs8: e.tensor_tensor(
                            out=qt[:].rearrange("p (a b) -> p a b", a=8), in0=pm[:, :].rearrange("p (a b) -> p a b", a=8),
                            in1=mkap(rs8[:, 0:1], [[1, 8], [0, 64]]), op=ALU.mult), reads=[pmb, rs8.b], writes=[qt.b])
                        voff = V_QW if n == 4 else V_KW
                        wv = mkap(vec[:, voff:voff + 1], [[0, 8], [1, 64]])
                        if n == 4:
                            S.op("pool", lambda e, qt=qt, wv=wv: e.tensor_tensor(out=qn[:].rearrange("p (a b) -> p a b", a=8),
                                                                                 in0=qt[:].rearrange("p (a b) -> p a b", a=8), in1=wv, op=ALU.mult),
                                 reads=[qt.b, vec.b], writes=[qn.b])
                        else:
                            S.op("pool", lambda e, qt=qt, wv=wv: e.tensor_tensor(out=k32[:].rearrange("p (a b) -> p a b", a=8),
                                                                                 in0=qt[:].rearrange("p (a b) -> p a b", a=8), in1=wv, op=ALU.mult),
                                 reads=[qt.b, vec.b], writes=[k32.b])
                            S.op("pool", lambda e: e.tensor_copy(out=kn[:], in_=k32[:]), reads=[k32.b], writes=[kn.b])
                            kdst = kso_d[:, :] if smp else ko_d[i * 128:(i + 1) * 128, :]
                            S.dma("sp", lambda e, kdst=kdst: e.dma_start(out=kdst, in_=k32[:]), k32.b, reads=[k32.b])
                    else:
                        S.op("act", lambda e, pm=pm: e.activation(out=v32[:], in_=pm[:, :], func=AF.Copy), reads=[pmb], writes=[v32.b])
                        S.op("pool", lambda e: e.tensor_copy(out=vaug[:, i, :].rearrange("p (a b) -> p a b", a=4)[:, :, 0:128],
                                                             in_=v32[:].rearrange("p (a b) -> p a b", a=4)),
                             reads=[v32.b, B_va[i]], writes=[B_va[i]])
                        vdst = vso_d[:, :] if smp else vo_d[i * 128:(i + 1) * 128, :]
                        S.dma("sp", lambda e, vdst=vdst: e.dma_start(out=vdst, in_=v32[:]), v32.b, reads=[v32.b])
                if ASTOP <= 2:
                    return
                rqT = rqT_r.next(); rkT = rkT_r.next(); dqT = dqT_r.next()
                for (srcT, dst_ap, dst_b) in ((rq, rqT[:], rqT.b), (rk, rkT[:], rkT.b), (qn, "dq", dqT.b),
                                              (kn, None, B_dkT[i])):
                    pt, ptb_ = ps_next("T", PT_T)
                    ptv = pt[:].bitcast(BF16)

                    def tr4(e, srcT=srcT, ptv=ptv):
                        for hh in range(4):
                            ins = e.transpose(out=ptv[:, hh * 128:(hh + 1) * 128], in_=srcT[:, hh * 128:(hh + 1) * 128], identity=identb[:])
                        return ins
                    S.op("pe", tr4, reads=[srcT.b, identb.b], writes=[ptb_])
                    if dst_ap is None:
                        S.op("dve", lambda e, ptv=ptv: e.tensor_copy(out=dkT_all[:, :, i * 128:(i + 1) * 128],
                                                                     in_=ptv[:, 0:512].rearrange("p (a b) -> p a b", a=4)),
                             reads=[ptb_], writes=[dst_b])
                    elif dst_ap == "dq":
                        dq4 = dqT[:].rearrange("p (h c q) -> p h c q", h=4, c=2)
                        S.op("act", lambda e, ptv=ptv, dq4=dq4: e.activation(out=dq4[0:64, :, 0, :], in_=ptv[0:64, 0:512].rearrange("p (h q) -> p h q", h=4), func=AF.Copy),
                             reads=[ptb_, dst_b], writes=[dst_b])
                        S.op("dve", lambda e, ptv=ptv, dq4=dq4: e.tensor_copy(out=dq4[64:128, :, 1, :], in_=ptv[64:128, 0:512].rearrange("p (h q) -> p h q", h=4)),
                             reads=[ptb_, dst_b], writes=[dst_b])
                    else:
                        S.op("act", lambda e, ptv=ptv, dst_ap=dst_ap: e.activation(out=dst_ap, in_=ptv[:, 0:512], func=AF.Copy),
                             reads=[ptb_], writes=[dst_b])
                if ASTOP <= 3:
                    return
                pm, pmb = ps_next("M", PT_M)

                def mm_s(e, pm=pm):
                    for hh in range(4):
                        ins = e.matmul(pm[:, hh * 128:(hh + 1) * 128], lhsT=rkT[:, hh * 128:(hh + 1) * 128], rhs=rqT[:, hh * 128:(hh + 1) * 128],
                                       start=True, stop=True)
                    return ins
                S.op("pe", mm_s, reads=[rkT.b, rqT.b], writes=[pmb])
                sm = sm_r.next()
                mk_ = cs("maskS") if smp else cs("maskT")
                S.op("dve", lambda e, pm=pm: e.tensor_tensor(out=sm[:].rearrange("p (a b) -> p a b", a=4), in0=pm[:, :].rearrange("p (a b) -> p a b", a=4),
                                                             in1=mkap(mk_[:, 0:1], [[0, 4], [1, 128]]), op=ALU.mult), reads=[pmb, Cc.b], writes=[sm.b])
                po, pob = ps_next("M", PT_M)
                use_state = (not smp) and i > 0

                def mm_o(e, po=po):
                    for hh in range(4):
                        ins = e.matmul(po[:, hh * 128:(hh + 1) * 128], lhsT=sm[:, hh * 128:(hh + 1) * 128], rhs=rv[:, hh * 128:(hh + 1) * 128],
                                       start=True, stop=not use_state)
                        if use_state:
                            ins = e.matmul(po[:, hh * 128:(hh + 1) * 128], lhsT=rqT[:, hh * 128:(hh + 1) * 128], rhs=statebf[:, hh * 128:(hh + 1) * 128],
                                           start=False, stop=True)
                    return ins
                S.op("pe", mm_o, reads=[sm.b, rv.b, rqT.b] + ([statebf.b] if use_state else []), writes=[pob])
                ro = ro_r.next()
                if smp:
                    S.op("dve", lambda e, po=po: e.tensor_tensor(out=ro[:], in0=po[:, :], in1=cross_sb[:], op=ALU.add), reads=[pob, cross_sb.b], writes=[ro.b])
                else:
                    S.op("act", lambda e, po=po: e.activation(out=ro[:], in_=po[:, :], func=AF.Copy), reads=[pob], writes=[ro.b])
                    pk, pkb = ps_next("M", PT_M)

                    def mm_kv(e, pk=pk):
                        for hh in range(4):
                            ins = e.matmul(pk[:, hh * 128:(hh + 1) * 128], lhsT=rk[:, hh * 128:(hh + 1) * 128], rhs=rv[:, hh * 128:(hh + 1) * 128],
                                           start=True, stop=True)
                        return ins
                    S.op("pe", mm_kv, reads=[rk.b, rv.b], writes=[pkb])
                    gLb = mkap(cs("gL")[:, 0:1], [[1, 4], [0, 128]])
                    if i == 0:
                        S.op("dve", lambda e, pk=pk: e.tensor_tensor(out=state32[:].rearrange("p (a b) -> p a b", a=4), in0=pk[:, :].rearrange("p (a b) -> p a b", a=4),
                                                                     in1=gLb, op=ALU.mult), reads=[pkb, Cc.b, state32.b], writes=[state32.b])
                    else:
                        S.op("dve", lambda e, pk=pk: e.tensor_tensor(out=state32[:], in0=pk[:, :], in1=state32[:], op=ALU.add),
                             reads=[pkb, state32.b], writes=[state32.b])
                        S.op("pool", lambda e: e.tensor_tensor(out=state32[:].rearrange("p (a b) -> p a b", a=4), in0=state32[:].rearrange("p (a b) -> p a b", a=4),
                                                               in1=gLb, op=ALU.mult), reads=[state32.b, Cc.b], writes=[state32.b])
                    if i < NT - 1:
                        S.op("pool", lambda e: e.tensor_copy(out=statebf[:], in_=state32[:]), reads=[state32.b, statebf.b], writes=[statebf.b])
                    else:
                        S.dma("sp", lambda e: e.dma_start(out=reto_d.rearrange("h d e -> d h e"), in_=state32[:].rearrange("p (a b) -> p a b", a=4)),
                              state32.b, reads=[state32.b])
                        out_bufs.append(state32.b)
                if ASTOP <= 4:
                    return
                mixed = mixed_r.next()
                rsq = rsq_r.next(); st4 = st4_r.next()
                S.op("act", lambda e: e.activation(out=rsq[:], in_=ro[:], func=AF.Square), reads=[ro.b], writes=[rsq.b])
                S.op("dve", lambda e: e.tensor_reduce(out=st4[:, 0:4], in_=ro[:].rearrange("p (a b) -> p a b", a=4), axis=AX.X, op=ALU.add),
                     reads=[ro.b], writes=[st4.b])
                S.op("dve", lambda e: e.tensor_reduce(out=st4[:, 4:8], in_=rsq[:].rearrange("p (a b) -> p a b", a=4), axis=AX.X, op=ALU.add),
                     reads=[rsq.b, st4.b], writes=[st4.b])
                S.op("dve", lambda e: e.tensor_scalar(out=st4[:, 0:4], in0=st4[:, 0:4], scalar1=1.0 / 128, scalar2=None, op0=ALU.mult),
                     reads=[st4.b], writes=[st4.b])
                S.op("dve", lambda e: e.tensor_tensor(out=st4[:, 8:12], in0=st4[:, 0:4], in1=st4[:, 0:4], op=ALU.mult), reads=[st4.b], writes=[st4.b])
                S.op("dve", lambda e: e.scalar_tensor_tensor(out=st4[:, 4:8], in0=st4[:, 4:8], scalar=1.0 / 128, in1=st4[:, 8:12],
                                                             op0=ALU.mult, op1=ALU.subtract), reads=[st4.b], writes=[st4.b])
                S.op("act", lambda e: e.activation(out=st4[:, 12:16], in_=st4[:, 4:8], func=AF.Ln, bias=epsc[:], scale=1.0), reads=[st4.b, epsc.b], writes=[st4.b])
                S.op("act", lambda e: e.activation(out=st4[:, 12:16], in_=st4[:, 12:16], func=AF.Exp, scale=-0.5), reads=[st4.b], writes=[st4.b])
                S.op("dve", lambda e: e.tensor_tensor(out=ro[:].rearrange("p (a b) -> p a b", a=4), in0=ro[:].rearrange("p (a b) -> p a b", a=4),
                                                      in1=mkap(st4[:, 0:1], [[1, 4], [0, 128]]), op=ALU.subtract), reads=[ro.b, st4.b], writes=[ro.b])
                S.op("pool", lambda e: e.tensor_tensor(out=ro[:].rearrange("p (a b) -> p a b", a=4), in0=ro[:].rearrange("p (a b) -> p a b", a=4),
                                                       in1=mkap(st4[:, 12:13], [[1, 4], [0, 128]]), op=ALU.mult), reads=[ro.b, st4.b], writes=[ro.b])
                S.op("pool", lambda e: e.tensor_tensor(out=ro[:], in0=ro[:], in1=vec[:, V_GNW:V_GNW + 512], op=ALU.mult), reads=[ro.b, vec.b], writes=[ro.b])
                S.op("pool", lambda e: e.tensor_tensor(out=ro[:], in0=ro[:], in1=vec[:, V_GNB:V_GNB + 512], op=ALU.add), reads=[ro.b, vec.b], writes=[ro.b])
                S.op("dve", lambda e: e.tensor_tensor(out=mixed[:, 0:512], in0=ro[:], in1=sg[:], op=ALU.mult), reads=[ro.b, sg.b, mixed.b], writes=[mixed.b])
                if ASTOP <= 5:
                    return
                if smp:
                    pass
                jlist = [NT] if smp else list(range(i + 1))
                for hh in range(4):
                    pa, pab = ps_next("DA", PT_DA)
                    for jn, jt in enumerate(jlist):
                        pd, pdb = ps_next("DS", PT_DS)

                        def mm_ds(e, pd=pd, jt=jt, hh=hh):
                            return e.matmul(pd[:, 0:256], lhsT=dkT_all[:, hh, jt * 128:(jt + 1) * 128],
                                            rhs=dqT[:, hh * 256:(hh + 1) * 256], start=True, stop=True)
                        S.op("pe", mm_ds, reads=[B_dkT[jt], dqT.b], writes=[pdb])
                        at = at_r.next()
                        if smp:
                            bo = cfg.coff["abias_s"][0] + hh
                        else:
                            bo = cfg.coff["abias"][0] + (i - jt) * 4 + hh
                        S.op("act", lambda e, pd=pd, at=at, bo=bo: e.activation(out=at[:], in_=pd[:, 0:256], func=AF.Exp, bias=Cc[:, bo:bo + 1], scale=0.125),
                             reads=[pdb, Cc.b], writes=[at.b])
                        if smp or jt == i:
                            S.op("pool", lambda e, at=at: e.tensor_tensor(out=at[:].rearrange("p (a b) -> p a b", a=2), in0=at[:].rearrange("p (a b) -> p a b", a=2),
                                                                          in1=mkap(mk_[:, 0:1], [[0, 2], [1, 128]]), op=ALU.mult),
                                 reads=[at.b, Cc.b], writes=[at.b])

                        def mm_av(e, pa=pa, at=at, jt=jt, hh=hh, jn=jn):
                            for c2 in range(2):
                                ins = e.matmul(pa[:, c2 * 256:c2 * 256 + 129], lhsT=at[:, c2 * 128:(c2 + 1) * 128],
                                               rhs=vaug[:, jt, hh * 130:hh * 130 + 129], start=(jn == 0 and c2 == 0),
                                               stop=(jn == len(jlist) - 1 and c2 == 1))
                            return ins
                        S.op("pe", mm_av, reads=[at.b, B_va[jt]], writes=[pab])
                    acc = acc_r.next(); fin = fin_r.next(); o1 = o1_r.next(); dd = dd_r.next(); jk = jk_r.next()
                    accv = acc[:].rearrange("p (a b) -> p a b", a=2)
                    pav = pa[:, :].rearrange("p (a b) -> p a b", a=2)[:, :, 0:129]
                    if smp:
                        S.op("dve", lambda e, pav=pav, accv=accv, hh=hh: e.tensor_tensor(
                            out=accv[:, :, 0:129], in0=pav, in1=pps_sb[:, hh, :].rearrange("p (a b) -> p a b", a=2), op=ALU.add),
                            reads=[pab, pps_sb.b], writes=[acc.b])
                    else:
                        S.op("act", lambda e, pav=pav, accv=accv: e.activation(out=accv[:, :, 0:129], in_=pav, func=AF.Copy), reads=[pab], writes=[acc.b])
                    S.op("dve", lambda e, accv=accv, fin=fin: e.reciprocal(out=fin[:, 0:2], in_=accv[:, :, 128]), reads=[acc.b], writes=[fin.b])
                    S.op("dve", lambda e, fin=fin: e.tensor_tensor(out=fin[:, 2:3], in0=fin[:, 1:2], in1=nlam[:], op=ALU.mult), reads=[fin.b, nlam.b], writes=[fin.b])
                    S.op("dve", lambda e, accv=accv, fin=fin, o1=o1: e.tensor_scalar(out=o1[:], in0=accv[:, 0, 0:128], scalar1=fin[:, 0:1], scalar2=None, op0=ALU.mult),
                         reads=[acc.b, fin.b], writes=[o1.b])
                    S.op("dve", lambda e, accv=accv, fin=fin, o1=o1, dd=dd: e.scalar_tensor_tensor(out=dd[:], in0=accv[:, 1, 0:128], scalar=fin[:, 2:3], in1=o1[:],
                                                                                                 op0=ALU.mult, op1=ALU.add), reads=[acc.b, fin.b, o1.b], writes=[dd.b])
                    S.op("act", lambda e, dd=dd, jk=jk, fin=fin: e.activation(out=jk[:], in_=dd[:], func=AF.Square, accum_out=fin[:, 3:4]), reads=[dd.b, fin.b], writes=[jk.b, fin.b])
                    S.op("act", lambda e, fin=fin: e.activation(out=fin[:, 4:5], in_=fin[:, 3:4], func=AF.Ln, bias=epsc[:], scale=1.0 / 128), reads=[fin.b, epsc.b], writes=[fin.b])
                    S.op("act", lambda e, fin=fin: e.activation(out=fin[:, 4:5], in_=fin[:, 4:5], func=AF.Exp, scale=-0.5), reads=[fin.b], writes=[fin.b])
                    S.op("dve", lambda e, dd=dd, fin=fin, hh=hh: e.tensor_scalar(out=mixed[:, 512 + hh * 128:512 + (hh + 1) * 128], in0=dd[:], scalar1=fin[:, 4:5],
                                                                               scalar2=None, op0=ALU.mult), reads=[dd.b, fin.b, mixed.b], writes=[mixed.b])
                if ASTOP <= 6:
                    return
                mixT = mixT_r.next()
                pt, ptb_ = ps_next("T", PT_T)
                ptv = pt[:].bitcast(BF16)

                def tr8(e, ptv=ptv):
                    for k in range(KC):
                        ins = e.transpose(out=ptv[:, k * 128:(k + 1) * 128], in_=mixed[:, k * 128:(k + 1) * 128], identity=identb[:])
                    return ins
                S.op("pe", tr8, reads=[mixed.b, identb.b], writes=[ptb_])
                S.op("act", lambda e, ptv=ptv: e.activation(out=mixT[:], in_=ptv[:, 0:KC * 128], func=AF.Copy), reads=[ptb_], writes=[mixT.b])
                for mh in range(2):
                    pm, pmb = ps_next("M", PT_M)

                    def mm_wo(e, pm=pm, mh=mh):
                        for k in range(KC):
                            ins = e.matmul(pm[:, :], lhsT=mixT[:, k * 128:(k + 1) * 128], rhs=wob[:, k, mh * 512:(mh + 1) * 512],
                                           start=(k == 0), stop=(k == KC - 1))
                        return ins
                    S.op("pe", mm_wo, reads=[mixT.b, wob.b], writes=[pmb])
                    S.op("dve", lambda e, pm=pm, mh=mh: e.tensor_tensor(out=xt[:, mh * 512:(mh + 1) * 512], in0=pm[:, :], in1=xt[:, mh * 512:(mh + 1) * 512], op=ALU.add),
                         reads=[pmb, xt.b], writes=[xt.b])
                S.dma("sp", lambda e: e.dma_start(out=h1s_d[i * 128:(i + 1) * 128, :], in_=xt[:]), xt.b, reads=[xt.b], writes=[B_h1s[i]])

            for i in range(min(NT + 1, ATILES)):
                attn_tile(i)
            S.barrier()

        with ExitStack() as st:
          if "skipB" not in DBG:
            def sT(name, shape, dt):
                return T(nc, st, name, shape, dt)
            PT_T = [0, 1]; PT_G = [2, 3]; PT_U = [4, 5]; PT_M = [6, 7]
            GMAX = max(len(g) for g in cfg.groups)
            TOKMAX = GMAX * 128
            wpgb = sT("b_wpgb", [128, KC, D], BF16)
            wppb = sT("b_wppb", [128, 2, D], BF16)
            stg = Ring(nc, st, "b_stg", [128, 8 * 256], F32, 2)
            stg_o = Ring(nc, st, "b_stgo", [128, D], F32, 2)
            for c in range(KC):
                sg_ = stg_o.next()
                S.dma("sp", lambda e, sg_=sg_, c=c: e.dma_start(out=sg_[:], in_=wpg_d[c * 128:(c + 1) * 128, :]), sg_.b, writes=[sg_.b])
                S.op("pool", lambda e, sg_=sg_, c=c: e.tensor_scalar(out=wpgb[:, c, :], in0=sg_[:], scalar1=lnc[:, 16 + c:17 + c], scalar2=None, op0=ALU.mult),
                     reads=[sg_.b, lnc.b, wpgb.b], writes=[wpgb.b])
            for c in range(2):
                sg_ = stg_o.next()
                S.dma("sp", lambda e, sg_=sg_, c=c: e.dma_start(out=sg_[:], in_=wpp_d[c * 128:(c + 1) * 128, :]), sg_.b, writes=[sg_.b])
                S.op("pool", lambda e, sg_=sg_, c=c: e.tensor_copy(out=wppb[:, c, :], in_=sg_[:]), reads=[sg_.b, wppb.b], writes=[wppb.b])

            H = sT("b_H", [128, GMAX, D], F32)
            B_H = [Buf("H%d" % k) for k in range(GMAX)]
            hnT = sT("b_hnT", [128, KC, TOKMAX], BF16)
            B_hnT = [Buf("hnT%d" % k) for k in range(GMAX)]
            actT = sT("b_actT", [128, NBH, TOKMAX], BF16)
            wfib = Ring(nc, st, "b_wfib", [128, 8 * 256], BF16, 2)
            wfob = sT("b_wfob", [128, NBH, D], BF16)
            B_wfo = [Buf("wfo%d" % k) for k in range(NBH)]
            B_act = [Buf("act%d" % k) for k in range(NBH)]
            junk = sT("b_junk", [128, D], BF16)
            ss_r = Ring(nc, st, "b_ss", [128, 1], F32, 2); rstd_r = Ring(nc, st, "b_rstd", [128, 1], F32, 2)
            xn_r = Ring(nc, st, "b_xn", [128, D], BF16, 1)
            sgt_r = Ring(nc, st, "b_sgt", [128, 512], BF16, 2)
            h2nT_r = Ring(nc, st, "b_h2nT", [128, KC * 128], BF16, 1)
            p32_r = Ring(nc, st, "b_p32", [128, 256], F32, 1); pbf_r = Ring(nc, st, "b_pbf", [128, 256], BF16, 1)
            pT_r = Ring(nc, st, "b_pT", [128, 256], BF16, 1)
            sig_r = Ring(nc, st, "b_sig", [128, 512], F32, 1)
            yt_r = Ring(nc, st, "b_yt", [128, D], F32, 1)

            for grp in cfg.groups:
                G = len(grp)
                NTOK = G * 128
                tgs = [(t0, min(512, NTOK - t0)) for t0 in range(0, NTOK, 512)]
                for k, ti in enumerate(grp):
                    S.dma("sp", lambda e, k=k, ti=ti: e.dma_start(out=H[:, k, :], in_=h1s_d[ti * 128:(ti + 1) * 128, :]), B_H[k],
                          reads=[B_h1s[ti]], writes=[B_H[k]])
                    ss = ss_r.next(); rstd = rstd_r.next(); xn = xn_r.next()
                    Hk = _View(H, k, B_H[k])
                    _norm_T_view(S, nc, Hk, junk, ss, rstd, xn, hnT, k, B_hnT[k], ps_next, PT_T, identb, epsc)
                for half in range(2):
                    for nbl in range(NBH):
                        nb = half * NBH + nbl
                        sg_ = stg.next()
                        S.dma("sp", lambda e, sg_=sg_, nb=nb: e.dma_start(out=sg_[:], in_=wfi_d[nb, :, :]), sg_.b, writes=[sg_.b])
                        wb = wfib.next()
                        S.op("pool", lambda e, sg_=sg_, wb=wb: e.tensor_tensor(out=wb[:].rearrange("p (c n) -> p c n", c=8), in0=sg_[:].rearrange("p (c n) -> p c n", c=8),
                                                                               in1=mkap(lnc[:, 8:9], [[1, 8], [0, 256]]), op=ALU.mult),
                             reads=[sg_.b, lnc.b], writes=[wb.b])
                        so = stg_o.next()
                        S.dma("sp", lambda e, so=so, nb=nb: e.dma_start(out=so[:], in_=wfo_d[nb * 128:(nb + 1) * 128, :]), so.b, writes=[so.b])
                        S.op("pool", lambda e, so=so, nbl=nbl: e.tensor_copy(out=wfob[:, nbl, :], in_=so[:]), reads=[so.b, B_wfo[nbl]], writes=[B_wfo[nbl]])
                        for (t0, tn) in tgs:
                            ks = list(range(t0 // 128, (t0 + tn) // 128))
                            pg, pgb = ps_next("G", PT_G)
                            pu, pub = ps_next("U", PT_U)

                            def mm_gu(e, wb=wb, pg=pg, pu=pu, t0=t0, tn=tn):
                                for (pp, co) in ((pg, 0), (pu, 128)):
                                    for k in range(KC):
                                        ins = e.matmul(pp[:, 0:tn], lhsT=wb[:, k * 256 + co:k * 256 + co + 128], rhs=hnT[:, k, t0:t0 + tn],
                                                       start=(k == 0), stop=(k == KC - 1))
                                return ins
                            S.op("pe", mm_gu, reads=[wb.b] + [B_hnT[k] for k in ks], writes=[pgb, pub])
                            sgt = sgt_r.next()
                            S.op("act", lambda e, pg=pg, sgt=sgt, tn=tn: e.activation(out=sgt[:, 0:tn], in_=pg[:, 0:tn], func=AF.Silu), reads=[pgb], writes=[sgt.b])
                            S.op("dve", lambda e, pu=pu, sgt=sgt, nbl=nbl, t0=t0, tn=tn: e.tensor_tensor(out=actT[:, nbl, t0:t0 + tn], in0=pu[:, 0:tn], in1=sgt[:, 0:tn], op=ALU.mult),
                                 reads=[pub, sgt.b, B_act[nbl]], writes=[B_act[nbl]])
                    for k in range(G):
                        for mh in range(2):
                            pm, pmb = ps_next("M", PT_M)

                            def mm_fo(e, pm=pm, k=k, mh=mh):
                                for nbl in range(NBH):
                                    ins = e.matmul(pm[:, :], lhsT=actT[:, nbl, k * 128:(k + 1) * 128], rhs=wfob[:, nbl, mh * 512:(mh + 1) * 512],
                                                   start=(nbl == 0), stop=(nbl == NBH - 1))
                                return ins
                            S.op("pe", mm_fo, reads=B_act + B_wfo, writes=[pmb])
                            S.op("dve", lambda e, pm=pm, k=k, mh=mh: e.tensor_tensor(out=H[:, k, mh * 512:(mh + 1) * 512], in0=pm[:, :], in1=H[:, k, mh * 512:(mh + 1) * 512], op=ALU.add),
                                 reads=[pmb, B_H[k]], writes=[B_H[k]])
                for k, ti in enumerate(grp):
                    smp = (ti == NT)
                    ss = ss_r.next(); rstd = rstd_r.next(); xn = xn_r.next(); h2nT = h2nT_r.next()
                    Hk = _View(H, k, B_H[k])
                    _norm_T_flat(S, nc, Hk, junk, ss, rstd, xn, h2nT, ps_next, PT_T, identb, epsc)
                    p32 = p32_r.next(); pbf = pbf_r.next(); pT = pT_r.next()
                    psrc = psm_d[:, :] if smp else p_d[ti * 128:(ti + 1) * 128, :]
                    S.dma("sp", lambda e, p32=p32, psrc=psrc: e.dma_start(out=p32[:], in_=psrc), p32.b, writes=[p32.b])
                    S.op("act", lambda e, p32=p32, pbf=pbf: e.activation(out=pbf[:], in_=p32[:], func=AF.Copy), reads=[p32.b], writes=[pbf.b])
                    pt, ptb_ = ps_next("T", PT_T)
                    ptv = pt[:].bitcast(BF16)

                    def tr2(e, ptv=ptv, pbf=pbf):
                        for c in range(2):
                            ins = e.transpose(out=ptv[:, c * 128:(c + 1) * 128], in_=pbf[:, c * 128:(c + 1) * 128], identity=identb[:])
                        return ins
                    S.op("pe", tr2, reads=[pbf.b, identb.b], writes=[ptb_])
                    S.op("dve", lambda e, ptv=ptv, pT=pT: e.tensor_copy(out=pT[:], in_=ptv[:, 0:256]), reads=[ptb_], writes=[pT.b])
                    yt = yt_r.next()
                    for mh in range(2):
                        pgt, pgtb = ps_next("G", PT_G)
                        ppj, ppjb = ps_next("U", PT_U)

                        def mm_gate(e, pgt=pgt, mh=mh, h2nT=h2nT):
                            for c in range(KC):
                                ins = e.matmul(pgt[:, :], lhsT=h2nT[:, c * 128:(c + 1) * 128], rhs=wpgb[:, c, mh * 512:(mh + 1) * 512], start=(c == 0), stop=(c == KC - 1))
                            return ins
                        S.op("pe", mm_gate, reads=[h2nT.b, wpgb.b], writes=[pgtb])

                        def mm_pp(e, ppj=ppj, mh=mh, pT=pT):
                            for c in range(2):
                                ins = e.matmul(ppj[:, :], lhsT=pT[:, c * 128:(c + 1) * 128], rhs=wppb[:, c, mh * 512:(mh + 1) * 512], start=(c == 0), stop=(c == 1))
                            return ins
                        S.op("pe", mm_pp, reads=[pT.b, wppb.b], writes=[ppjb])
                        sig = sig_r.next()
                        S.op("act", lambda e, pgt=pgt, sig=sig: e.activation(out=sig[:], in_=pgt[:, :], func=AF.Sigmoid), reads=[pgtb], writes=[sig.b])
                        S.op("dve", lambda e, ppj=ppj, sig=sig: e.tensor_tensor(out=sig[:], in0=ppj[:, :], in1=sig[:], op=ALU.mult), reads=[ppjb, sig.b], writes=[sig.b])
                        S.op("pool", lambda e, sig=sig, yt=yt, k=k, mh=mh: e.tensor_tensor(out=yt[:, mh * 512:(mh + 1) * 512], in0=sig[:], in1=H[:, k, mh * 512:(mh + 1) * 512], op=ALU.add),
                             reads=[sig.b, B_H[k], yt.b], writes=[yt.b])
                    ydst = ys_d[:, :] if smp else y_d[ti * 128:(ti + 1) * 128, :]
                    S.dma("sp", lambda e, yt=yt, ydst=ydst: e.dma_start(out=ydst, in_=yt[:]), yt.b, reads=[yt.b])
            S.barrier()
        S.barrier()

        with nc.Block() as block:
            S.emit(block)
    return nc


CC_INC = 16
import os as _os
DBG = set(_os.environ.get("KDBG", "").split(","))
ASTOP = int(_os.environ.get("ASTOP", "99"))
ATILES = int(_os.environ.get("ATILES", "99"))
SSTOP = int(_os.environ.get("SSTOP", "99"))
SUB = int(_os.environ.get("SUB", "99"))


class _Stop(Exception):
    pass


def _chk(n):
    if SUB <= n:
        raise _Stop()
SBATCH = int(_os.environ.get("SBATCH", "32"))


class _View:
    def __init__(self, base, k, buf):
        self.base, self.k, self.b = base, k, buf
        self.shape = [128, D]

    def __getitem__(self, key):
        return self.base.t[:, self.k, :][key]


def _norm_core(S, src, junk, ss, rstd, xn, epsc):
    S.op("act", lambda e: e.activation(out=junk[:], in_=src[:, :], func=AF.Square, accum_out=ss[:]), reads=[src.b], writes=[junk.b, ss.b])
    S.op("act", lambda e: e.activation(out=rstd[:], in_=ss[:], func=AF.Ln, bias=epsc[:], scale=1.0 / D), reads=[ss.b, epsc.b], writes=[rstd.b])
    S.op("act", lambda e: e.activation(out=rstd[:], in_=rstd[:], func=AF.Exp, scale=-0.5), reads=[rstd.b], writes=[rstd.b])
    S.op("dve", lambda e: e.tensor_scalar(out=xn[:], in0=src[:, :], scalar1=rstd[:], scalar2=None, op0=ALU.mult), reads=[src.b, rstd.b], writes=[xn.b])


def _tr8(S, xn, identb, ps_next, PT_T):
    pt, pb = ps_next("T", PT_T)
    ptb = pt[:].bitcast(BF16)

    def tr(e):
        for k in range(KC):
            ins = e.transpose(out=ptb[:, k * 128:(k + 1) * 128], in_=xn[:, k * 128:(k + 1) * 128], identity=identb[:])
        return ins
    S.op("pe", tr, reads=[xn.b, identb.b], writes=[pb])
    return ptb, pb


def _norm_T_view(S, nc, src, junk, ss, rstd, xn, hnT, k, hbuf, ps_next, PT_T, identb, epsc):
    _norm_core(S, src, junk, ss, rstd, xn, epsc)
    ptb, pb = _tr8(S, xn, identb, ps_next, PT_T)
    S.op("dve", lambda e: e.tensor_copy(out=hnT[:, :, k * 128:(k + 1) * 128], in_=ptb[:, 0:KC * 128].rearrange("p (c t) -> p c t", c=KC)),
         reads=[pb], writes=[hbuf])


def _norm_T_flat(S, nc, src, junk, ss, rstd, xn, dst, ps_next, PT_T, identb, epsc):
    _norm_core(S, src, junk, ss, rstd, xn, epsc)
    ptb, pb = _tr8(S, xn, identb, ps_next, PT_T)
    S.op("act", lambda e: e.activation(out=dst[:], in_=ptb[:, 0:KC * 128], func=AF.Copy), reads=[pb], writes=[dst.b])


def make_in_maps(cfg, inp):
    NT, NB = cfg.NT, cfg.NB
    HP = cfg.POSH // 2
    f = lambda a: np.ascontiguousarray(a, dtype=np.float32)
    w_in = f(inp["w_in"][0]); w_o = f(inp["w_o"][0])
    wfi = inp["w_ffn_in"][0]
    wfi_r = f(wfi.reshape(8, 128, 2, NB, 128).transpose(3, 1, 0, 2, 4).reshape(NB, 128, 8 * 256))
    w_fo = f(inp["w_ffn_out"][0]); w_pg = f(inp["w_ple_gate"][0]); w_pp = f(inp["w_ple_proj"][0])
    lncols = f(np.concatenate([inp["ln1_w"][0].reshape(8, 128).T, inp["ln2_w"][0].reshape(8, 128).T,
                               inp["ln_ple_w"][0].reshape(8, 128).T], axis=1))
    sublncol = f(inp["diff_subln_w"][0].reshape(4, 128).T)
    vecs = f(np.concatenate([inp["q_norm_w"][0], inp["k_norm_w"][0], inp["lambda_q1"][0], inp["lambda_q2"][0],
                             inp["lambda_k1"][0], inp["lambda_k2"][0], inp["ret_gn_w"][0], inp["ret_gn_b"][0]])[None, :])
    w_sel = f(np.concatenate([w_in[:, 0:1536], w_in[:, 2048:2560]], axis=1))
    consts = make_consts(cfg)
    ck, cv = inp["cache_k"][0], inp["cache_v"][0]
    shared = dict(w_in=w_in, w_o=w_o, w_sel=w_sel, w_fi=wfi_r, w_fo=w_fo, w_pg=w_pg, w_pp=w_pp,
                  consts=consts, lncols=lncols, sublncol=sublncol, vecs=vecs)
    for r in range(8):
        h, j = r % 4, r // 4
        for hf in range(2):
            p0 = j * cfg.POSH + hf * HP
            shared["ck%d_%d" % (r, hf)] = f(ck[:, p0:p0 + HP, h, :].reshape(cfg.NPHYS, HP * 128))
            shared["cv%d_%d" % (r, hf)] = f(cv[:, p0:p0 + HP, h, :].reshape(cfg.NPHYS, HP * 128))
    maps = []
    for c in range(8):
        xs = np.zeros((128, D), np.float32)
        xs[0:16] = inp["x_sample"][4 * c:4 * c + 4].reshape(16, D)
        psm = np.zeros((128, 256), np.float32)
        psm[0:16] = inp["p_sample"][0, 4 * c:4 * c + 4].reshape(16, 256)
        m = dict(shared)
        m.update(x=f(inp["x_prompt"][c]), xs=xs, p=f(inp["p_prompt"][0, c]), psm=psm,
                 ptT=np.ascontiguousarray(inp["page_table"][4 * c:4 * c + 4].T.astype(np.int32)),
                 st_in=f(inp["state_ret"][0, 4 * c:4 * c + 4].reshape(16, 128, 128)))
        maps.append(m)
    return maps


def assemble(cfg, res):
    NT = cfg.NT
    y = np.stack([res[c]["y"].reshape(NT * 128, D) for c in range(8)])
    kp = np.stack([res[c]["ko"].reshape(NT * 128, 4, 128) for c in range(8)])[None]
    vp = np.stack([res[c]["vo"].reshape(NT * 128, 4, 128) for c in range(8)])[None]
    rp = np.stack([res[c]["reto"].reshape(4, 128, 128) for c in range(8)])[None]
    ysm = np.concatenate([res[c]["ys"][0:16].reshape(4, 4, D) for c in range(8)], axis=0)
    ks = np.concatenate([res[c]["kso"][0:16].reshape(4, 4, 4, 128) for c in range(8)], axis=0)[None]
    vs = np.concatenate([res[c]["vso"][0:16].reshape(4, 4, 4, 128) for c in range(8)], axis=0)[None]
    rs = np.concatenate([res[c]["retso"].reshape(4, 4, 128, 128) for c in range(8)], axis=0)[None]
    return tuple(np.ascontiguousarray(a, dtype=np.float32) for a in (y, ysm, kp, vp, rp, ks, vs, rs))


_NC_CACHE = {}


def kernel(**inputs):
    cfg = Cfg()
    inputs = {k: np.asarray(v) for k, v in inputs.items()}
    if "nc" not in _NC_CACHE:
        _NC_CACHE["nc"] = build(cfg)
    nc = _NC_CACHE["nc"]
    in_maps = make_in_maps(cfg, inputs)
    res = run_bass_kernel_spmd(nc, in_maps, core_ids=list(range(8)))
    return assemble(cfg, res.results)
```

```python
import math
from contextlib import ExitStack

import numpy as np
import concourse.bass as bass
import concourse.mybir as mybir
from concourse.bass_utils import run_bass_kernel_spmd

F32 = mybir.dt.float32
BF16 = mybir.dt.bfloat16
I32 = mybir.dt.int32
AF = mybir.ActivationFunctionType
ALU = mybir.AluOpType
AX = mybir.AxisListType

EPS = 1e-6
LAM_INIT = 0.8 - 0.6 * math.exp(-0.3 * 0)
D = 1024
KC = 8
PROJ = 3584


class Buf:
    __slots__ = ("name", "w", "r", "dsem", "psum")

    def __init__(self, name, psum=False):
        self.name = name
        self.w = None
        self.r = []
        self.dsem = None
        self.psum = psum


class _Rec:
    def __init__(self):
        self.calls = []

    def __getattr__(self, name):
        def f(*a, **k):
            self.calls.append((name, a, k))
            return None
        return f


def _record(fn):
    r = _Rec()
    fn(r)
    assert r.calls
    return r.calls


class Sched:
    COMPUTE = ("pe", "act", "dve", "pool")

    def __init__(self, nc, stack):
        self.nc = nc
        self.stack = stack
        self.lists = {e: [] for e in ("pe", "act", "dve", "pool", "sp")}
        self.sem = {e: stack.enter_context(nc.semaphore("s_" + e)) for e in self.COMPUTE}
        self.cnt = {e: 0 for e in self.COMPUTE}
        self.known = {e: {} for e in self.lists}
        self.dma_sems = {}
        self.nsem = 4

    def _need(self, eng, waits, tok):
        if tok is None:
            return
        sem, val = tok
        if eng == "pe" and sem is self.sem["pe"]:
            return
        if self.known[eng].get(id(sem), (None, 0))[1] >= val:
            return
        if id(sem) not in waits or waits[id(sem)][1] < val:
            waits[id(sem)] = (sem, val)

    def _waits(self, eng, reads, writes):
        waits = {}
        for b in reads:
            self._need(eng, waits, b.w)
            if b.psum:
                for t in b.r:
                    if t[0] is not self.sem.get(eng):
                        self._need(eng, waits, t)
        for b in writes:
            self._need(eng, waits, b.w)
            for t in b.r:
                self._need(eng, waits, t)
        for k, sv in waits.items():
            self.known[eng][k] = sv
        return list(waits.values())

    @staticmethod
    def _commit(tok, reads, writes):
        for b in reads:
            b.r.append(tok)
            if len(b.r) > 64:
                best = {}
                for s, v in b.r:
                    if id(s) not in best or best[id(s)][1] < v:
                        best[id(s)] = (s, v)
                b.r = list(best.values())
        for b in writes:
            b.w = tok
            b.r = []

    def op(self, eng, fn, reads=(), writes=()):
        waits = self._waits(eng, reads, writes)
        self.cnt[eng] += 1
        tok = (self.sem[eng], self.cnt[eng])
        self.lists[eng].append((waits, _record(fn), tok[0], 1))
        self._commit(tok, reads, writes)
        return tok

    def dma(self, q, fn, owner, reads=(), writes=(), inc=16):
        waits = self._waits(q, reads, writes)
        if owner.dsem is None:
            owner.dsem = {}
        if q not in owner.dsem:
            sem = self.stack.enter_context(self.nc.semaphore("d%s_%s" % (q, owner.name)))
            owner.dsem[q] = [sem, 0]
            self.dma_sems[id(sem)] = owner.dsem[q]
            self.nsem += 1
        owner.dsem[q][1] += inc
        tok = (owner.dsem[q][0], owner.dsem[q][1])
        self.lists[q].append((waits, _record(fn), tok[0], inc))
        self._commit(tok, reads, writes)
        return tok

    def barrier(self):
        toks = [(self.sem[e], self.cnt[e]) for e in self.COMPUTE if self.cnt[e] > 0]
        toks += [(s, c) for (s, c) in self.dma_sems.values()]
        for eng in self.lists:
            waits = {}
            for t in toks:
                self._need(eng, waits, t)
            for k, sv in waits.items():
                self.known[eng][k] = sv
            if waits:
                self.lists[eng].append((list(waits.values()), None, None, 0))

    def emit(self, block):
        def mk(name):
            lst = self.lists[name]

            def body(e):
                for waits, fn, sem, inc in lst:
                    for (s, v) in waits:
                        e.wait_ge(s, v)
                    if fn is not None:
                        for (mname, a, k) in fn:
                            ins = getattr(e, mname)(*a, **k)
                        ins.then_inc(sem, inc)
            return body

        block.tensor(mk("pe"))
        block.scalar(mk("act"))
        block.vector(mk("dve"))
        block.gpsimd(mk("pool"))
        block.sync(mk("sp"))


class T:
    def __init__(self, nc, stack, name, shape, dt):
        self.t = stack.enter_context(nc.sbuf_tensor("sb_" + name, list(shape), dt))
        self.b = Buf(name)
        self.shape = list(shape)

    def __getitem__(self, k):
        return self.t[k]


class Ring:
    def __init__(self, nc, stack, name, shape, dt, n):
        self.items = [T(nc, stack, "%s%d" % (name, i), shape, dt) for i in range(n)]
        self.i = 0

    def next(self):
        t = self.items[self.i % len(self.items)]
        self.i += 1
        return t


def mkap(a, dims):
    return bass.AP(a.tensor, a.offset, [[a.ap[0][0], a.ap[0][1]]] + [list(d) for d in dims])


class Cfg:
    def __init__(self, NT=16, DFF=2816, NPG=128, PAGE=128, NPHYS=5120, groups=None):
        self.NT, self.DFF, self.NPG, self.PAGE, self.NPHYS = NT, DFF, NPG, PAGE, NPHYS
        self.POSH = PAGE // 2
        self.NB = DFF // 128
        assert self.NB % 2 == 0
        self.NBH = self.NB // 2
        self.PAST = NPG * PAGE
        if groups is None:
            half = NT // 2
            groups = [list(range(half)), list(range(half, NT)) + [NT]]
        self.groups = groups
        off = {}
        c = 0
        for name, w in [("ident", 128), ("maskT", 128), ("maskS", 128), ("tabq", 4), ("tabk", 4),
                        ("tabq_s", 4), ("tabk_s", 4), ("gL", 4), ("abias", NT * 4), ("abias_s", 4),
                        ("alibi8", 8 * self.POSH), ("g4s", 16), ("mask16", 4), ("ones", 1)]:
            off[name] = (c, w)
            c += w
        self.coff = off
        self.NC = c
        self.CCW = 258
        self.ARW = 4 * 258 + 8 * 128


def make_consts(cfg):
    C = np.zeros((128, cfg.NC), np.float64)

    def put(name, arr):
        o, w = cfg.coff[name]
        C[:, o:o + w] = arr

    p = np.arange(128, dtype=np.float64)
    put("ident", np.eye(128))
    put("maskT", (p[:, None] <= p[None, :]).astype(np.float64))
    put("maskS", ((p[:, None] // 4 == p[None, :] // 4) & (p[:, None] % 4 <= p[None, :] % 4)).astype(np.float64))
    gam = 1.0 - np.exp2(-5.0 - np.arange(4))
    lg = np.log(gam)
    put("tabq", np.exp(lg[None, :] * (p[:, None] + 1.0)))
    put("tabk", np.exp(-lg[None, :] * (p[:, None] + 1.0)) * (128.0 ** -0.5))
    t4 = p % 4
    put("tabq_s", np.exp(lg[None, :] * (t4[:, None] + 1.0)))
    put("tabk_s", np.exp(-lg[None, :] * (t4[:, None] + 1.0)) * (128.0 ** -0.5))
    put("gL", np.tile(np.exp(lg * 128.0)[None, :], (128, 1)))
    slopes = np.exp2(-8.0 / 4 * np.arange(1, 5))
    ab = np.zeros((128, cfg.NT * 4))
    for dl in range(cfg.NT):
        for hh in range(4):
            ab[:, dl * 4 + hh] = slopes[hh] * (p - 127.0 - 128.0 * dl)
    put("abias", ab)
    put("abias_s", slopes[None, :] * t4[:, None])
    pos = np.arange(cfg.POSH, dtype=np.float64)
    al = np.zeros((128, 8 * cfg.POSH))
    for r in range(8):
        hh, jj = r % 4, r // 4
        kpos = p[:, None] * cfg.PAGE + jj * cfg.POSH + pos[None, :]
        al[:, r * cfg.POSH:(r + 1) * cfg.POSH] = 8.0 * slopes[hh] * (kpos - cfg.PAST)
    put("alibi8", al)
    g4s = np.zeros((128, 16))
    for sl in range(16):
        g4s[:, sl] = gam[sl % 4] ** 4
    put("g4s", g4s)
    m16 = np.zeros((128, 4))
    for t in range(16):
        m16[t, t // 4] = 1.0
    put("mask16", m16)
    put("ones", np.ones((128, 1)))
    return C.astype(np.float32)


def build(cfg):
    nc = bass.Bass("TRN2", target_bir_lowering=False)
    NT, DFF, NPG, PAGE, NPHYS = cfg.NT, cfg.DFF, cfg.NPG, cfg.PAGE, cfg.NPHYS
    POSH, NB, NBH, CCW = cfg.POSH, cfg.NB, cfg.NBH, cfg.CCW
    assert POSH * 8 <= 512 and POSH % 4 == 0

    def din(name, shape, dt=F32):
        return nc.dram_tensor(name, list(shape), dt, kind="ExternalInput").ap()

    def dout(name, shape):
        return nc.dram_tensor(name, list(shape), F32, kind="ExternalOutput").ap()

    x_d = din("x", [NT * 128, D]); xs_d = din("xs", [128, D])
    p_d = din("p", [NT * 128, 256]); psm_d = din("psm", [128, 256])
    HW = (POSH // 2) * 128
    ck_d = [[din("ck%d_%d" % (r, i), [NPHYS, HW]) for i in range(2)] for r in range(8)]
    cv_d = [[din("cv%d_%d" % (r, i), [NPHYS, HW]) for i in range(2)] for r in range(8)]
    ptT_d = din("ptT", [NPG, 4], I32)
    stin_d = din("st_in", [16, 128, 128])
    win_d = din("w_in", [D, PROJ]); wo_d = din("w_o", [D, D]); wsel_d = din("w_sel", [D, 2048])
    wfi_d = din("w_fi", [NB, 128, 8 * 256]); wfo_d = din("w_fo", [DFF, D])
    wpg_d = din("w_pg", [D, D]); wpp_d = din("w_pp", [256, D])
    consts_d = din("consts", [128, cfg.NC]); lncols_d = din("lncols", [128, 24])
    subln_d = din("sublncol", [128, 4]); vecs_d = din("vecs", [1, 1408])

    y_d = dout("y", [NT * 128, D]); ys_d = dout("ys", [128, D])
    ko_d = dout("ko", [NT * 128, 512]); vo_d = dout("vo", [NT * 128, 512])
    reto_d = dout("reto", [4, 128, 128]); kso_d = dout("kso", [128, 512]); vso_d = dout("vso", [128, 512])
    retso_d = dout("retso", [16, 128, 128])

    h1s_d = nc.dram_tensor("h1s", [(NT + 1) * 128, D], F32).ap()
    cctm_d = nc.dram_tensor("cc_tm", [16, 8 * 258], F32).ap()
    B_cctm = Buf("cctm")
    B_h1s = [Buf("h1s%d" % i) for i in range(NT + 1)]
    out_bufs = []

    with ExitStack() as gst:
        S = Sched(nc, gst)

        def gT(name, shape, dt):
            return T(nc, gst, name, shape, dt)

        PSB = []
        for k in range(8):
            t = gst.enter_context(nc.psum_tensor("psb%d" % k, [128, 512], F32))
            PSB.append((t, Buf("psb%d" % k, psum=True)))
        ps_ctr = {}

        def ps_next(pool, banks):
            i = ps_ctr.get(pool, 0)
            ps_ctr[pool] = i + 1
            return PSB[banks[i % len(banks)]]

        Cc = gT("consts", [128, cfg.NC], F32)
        lnc = gT("lncols", [128, 24], F32)
        subc = gT("sublncol", [128, 4], F32)
        vec = gT("vecs", [128, 1408], F32)
        S.dma("sp", lambda e: e.dma_start(out=Cc[:], in_=consts_d[:, :]), Cc.b, writes=[Cc.b])
        S.dma("sp", lambda e: e.dma_start(out=lnc[:], in_=lncols_d[:, :]), lnc.b, writes=[lnc.b])
        S.dma("sp", lambda e: e.dma_start(out=subc[:], in_=subln_d[:, :]), subc.b, writes=[subc.b])
        S.dma("sp", lambda e: e.dma_start(out=vec[:], in_=vecs_d[0, :].partition_broadcast(128)), vec.b, writes=[vec.b])

        def cs(name, rows=128):
            o, w = cfg.coff[name]
            return Cc[0:rows, o:o + w]

        V_QW, V_KW, V_L = 0, 64, 128
        V_GNW, V_GNB = 384, 896
        identb = gT("identb", [128, 128], BF16)
        S.op("dve", lambda e: e.tensor_copy(out=identb[:], in_=cs("ident")), reads=[Cc.b], writes=[identb.b])
        epsc = gT("epsc", [128, 1], F32)
        S.op("pool", lambda e: e.memset(epsc[:], EPS), writes=[epsc.b])

        lamt = gT("lamt", [128, 128], F32)
        lams = gT("lams", [128, 2], F32)
        lame = gT("lame", [128, 2], F32)
        nlam = gT("nlam", [128, 1], F32)
        S.op("dve", lambda e: e.tensor_tensor(
            out=lamt[:].rearrange("p (a b) -> p a b", a=2),
            in0=mkap(vec[:, 128:129], [[64, 2], [1, 64]]),
            in1=mkap(vec[:, 256:257], [[64, 2], [1, 64]]), op=ALU.mult),
            reads=[vec.b], writes=[lamt.b])
        S.op("dve", lambda e: e.tensor_reduce(out=lams[:], in_=lamt[:].rearrange("p (a b) -> p a b", a=2),
                                              axis=AX.X, op=ALU.add), reads=[lamt.b], writes=[lams.b])
        S.op("act", lambda e: e.activation(out=lame[:], in_=lams[:], func=AF.Exp), reads=[lams.b], writes=[lame.b])
        S.op("dve", lambda e: e.tensor_tensor(out=nlam[:], in0=lame[:, 1:2], in1=lame[:, 0:1], op=ALU.subtract),
             reads=[lame.b], writes=[nlam.b])
        S.op("dve", lambda e: e.tensor_scalar(out=nlam[:], in0=nlam[:], scalar1=-LAM_INIT, scalar2=None, op0=ALU.add),
             reads=[nlam.b], writes=[nlam.b])

        def rstd_from_ss(ss, rstd, n, inv_count, tag_reads=()):
            S.op("act", lambda e: e.activation(out=rstd[:], in_=ss[:], func=AF.Ln, bias=epsc[0:ss.shape[0], :],
                                               scale=inv_count), reads=[ss.b, epsc.b], writes=[rstd.b])
            S.op("act", lambda e: e.activation(out=rstd[:], in_=rstd[:], func=AF.Exp, scale=-0.5),
                 reads=[rstd.b], writes=[rstd.b])

        def norm_T(src, rows, junk, ss, rstd, xn, xT_dst, xT_buf, pool_T):
            S.op("act", lambda e: e.activation(out=junk[0:rows, :], in_=src[0:rows, :], func=AF.Square,
                                               accum_out=ss[0:rows, :]), reads=[src.b], writes=[junk.b, ss.b])
            S.op("act", lambda e: e.activation(out=rstd[0:rows, :], in_=ss[0:rows, :], func=AF.Ln,
                                               bias=epsc[0:rows, :], scale=1.0 / D),
                 reads=[ss.b, epsc.b], writes=[rstd.b])
            S.op("act", lambda e: e.activation(out=rstd[0:rows, :], in_=rstd[0:rows, :], func=AF.Exp, scale=-0.5),
                 reads=[rstd.b], writes=[rstd.b])
            S.op("dve", lambda e: e.tensor_scalar(out=xn[0:rows, :], in0=src[0:rows, :], scalar1=rstd[0:rows, :],
                                                  scalar2=None, op0=ALU.mult),
                 reads=[src.b, rstd.b], writes=[xn.b])
            pt, pb = ps_next("T", pool_T)
            ptb = pt[:].bitcast(BF16)

            def tr(e):
                for k in range(KC):
                    ins = e.transpose(out=ptb[:, k * rows:(k + 1) * rows], in_=xn[0:rows, k * 128:(k + 1) * 128],
                                      identity=identb[0:rows, 0:rows])
                return ins
            S.op("pe", tr, reads=[xn.b, identb.b], writes=[pb])
            S.op("dve", lambda e: e.tensor_copy(out=xT_dst(None), in_=ptb[:, 0:KC * rows]), reads=[pb], writes=[xT_buf])

        def cast_w(eng, dst_ap, src_ap, col_ap, rd, wr, extra=None):
            if extra is None:
                S.op(eng, lambda e: e.tensor_tensor(out=dst_ap, in0=src_ap, in1=col_ap, op=ALU.mult), reads=rd, writes=wr)
            else:
                S.op(eng, lambda e: e.scalar_tensor_tensor(out=dst_ap, in0=src_ap, scalar=extra, in1=col_ap,
                                                           op0=ALU.mult, op1=ALU.mult), reads=rd, writes=wr)

        cross_sb = gT("cross_sb", [128, 512], F32)
        pps_sb = gT("pps_sb", [128, 4, 258], F32)
        S.op("pool", lambda e: e.memset(pps_sb[:], 0.0), writes=[pps_sb.b])
        with ExitStack() as st:
            def sT(name, shape, dt):
                return T(nc, st, name, shape, dt)
            PT_T = [0, 1]; PT_M = [2, 3]; PT_SC = [4, 5]; PT_OA = [6, 7]
            HP = POSH // 2

            xs_t = sT("s_xs", [128, D], F32)
            S.dma("sp", lambda e: e.dma_start(out=xs_t[:], in_=xs_d[:, :]), xs_t.b, writes=[xs_t.b])
            ptT = sT("s_ptT", [NPG, 4], I32)
            S.dma("sp", lambda e: e.dma_start(out=ptT[:], in_=ptT_d[:, :]), ptT.b, writes=[ptT.b])
            st32 = sT("s_st32", [128, 16, 128], F32)
            S.dma("sp", lambda e: e.dma_start(out=st32[:], in_=stin_d.rearrange("b d e -> d b e")), st32.b, writes=[st32.b])
            stbf = sT("s_stbf", [128, 16, 128], BF16)
            S.op("pool", lambda e: e.tensor_copy(out=stbf[:], in_=st32[:]), reads=[st32.b], writes=[stbf.b])
            junk = sT("s_junk", [128, D], BF16)
            ss = sT("s_ss", [128, 1], F32); rstd = sT("s_rstd", [128, 1], F32)
            xn = sT("s_xn", [128, D], BF16)
            xsT = sT("s_xsT", [128, KC, 128], BF16)
            norm_T(xs_t, 128, junk, ss, rstd, xn, lambda _: xsT[:].rearrange("p k t -> p (k t)"), xsT.b, PT_T)

            rqo = sT("s_rqo", [128, 512], BF16); rko = sT("s_rko", [128, 512], BF16); rvo = sT("s_rvo", [128, 512], BF16)
            qn = sT("s_qn", [128, 512], BF16)
            sq = sT("s_sq", [128, 512], F32); qt = sT("s_qt", [128, 512], F32)
            ssq = sT("s_ssq", [128, 8], F32); rs8 = sT("s_rs8", [128, 8], F32)
            with ExitStack() as st_w:
                wst_r = Ring(nc, st_w, "s_wst", [128, 8, 512], F32, 2)
                wpc_r = Ring(nc, st_w, "s_wpc", [128, 8, 512], BF16, 2)
                for pc in range(4):
                    wst = wst_r.next(); wpc = wpc_r.next()
                    S.dma("sp", lambda e, wst=wst, pc=pc: e.dma_start(
                        out=wst[:], in_=wsel_d[:, pc * 512:(pc + 1) * 512].rearrange("(c p) n -> p c n", p=128)),
                        wst.b, writes=[wst.b])
                    cast_w("pool" if pc % 2 == 0 else "dve", wpc[:], wst[:], mkap(lnc[:, 0:1], [[1, 8], [0, 512]]), [wst.b, lnc.b], [wpc.b])
                    pm, pmb = ps_next("M", PT_M)

                    def mm_p(e, pm=pm, wpc=wpc):
                        for k in range(KC):
                            ins = e.matmul(pm[:, :], lhsT=xsT[:, k, :], rhs=wpc[:, k, :], start=(k == 0), stop=(k == KC - 1))
                        return ins
                    S.op("pe", mm_p, reads=[xsT.b, wpc.b], writes=[pmb])
                    if pc == 0 or pc == 1:
                        dst = rqo if pc == 0 else rko
                        tab = cs("tabq_s") if pc == 0 else cs("tabk_s")
                        S.op("dve", lambda e, pm=pm, dst=dst, tab=tab: e.tensor_tensor(
                            out=dst[:].rearrange("p (a b) -> p a b", a=4), in0=pm[:, :].rearrange("p (a b) -> p a b", a=4),
                            in1=mkap(tab[:, 0:1], [[1, 4], [0, 128]]), op=ALU.mult), reads=[pmb, Cc.b], writes=[dst.b])
                    elif pc == 2:
                        S.op("act", lambda e, pm=pm: e.activation(out=rvo[:], in_=pm[:, :], func=AF.Copy), reads=[pmb], writes=[rvo.b])
                    else:
                        S.op("act", lambda e, pm=pm: e.activation(out=sq[:], in_=pm[:, :], func=AF.Square), reads=[pmb], writes=[sq.b])
                        S.op("dve", lambda e: e.tensor_reduce(out=ssq[:], in_=sq[:].rearrange("p (a b) -> p a b", a=8), axis=AX.X, op=ALU.add),
                             reads=[sq.b], writes=[ssq.b])
                        S.op("act", lambda e: e.activation(out=rs8[:], in_=ssq[:], func=AF.Ln, bias=epsc[:], scale=1.0 / 64), reads=[ssq.b, epsc.b], writes=[rs8.b])
                        S.op("act", lambda e: e.activation(out=rs8[:], in_=rs8[:], func=AF.Exp, scale=-0.5), reads=[rs8.b], writes=[rs8.b])
                        S.op("dve", lambda e, pm=pm: e.tensor_tensor(out=qt[:].rearrange("p (a b) -> p a b", a=8), in0=pm[:, :].rearrange("p (a b) -> p a b", a=8),
                                                                    in1=mkap(rs8[:, 0:1], [[1, 8], [0, 64]]), op=ALU.mult), reads=[pmb, rs8.b], writes=[qt.b])
                        S.op("dve", lambda e: e.tensor_tensor(out=qn[:].rearrange("p (a b) -> p a b", a=8), in0=qt[:].rearrange("p (a b) -> p a b", a=8),
                                                              in1=mkap(vec[:, V_QW:V_QW + 1], [[0, 8], [1, 64]]), op=ALU.mult), reads=[qt.b, vec.b], writes=[qn.b])
                S.barrier()

            pt, ptb_ = ps_next("T", PT_T)
            ptv = pt[:].bitcast(BF16)

            def trq(e, ptv=ptv):
                for hh in range(4):
                    ins = e.transpose(out=ptv[:, hh * 128:(hh + 1) * 128], in_=qn[:, hh * 128:(hh + 1) * 128], identity=identb[:])
                return ins
            S.op("pe", trq, reads=[qn.b, identb.b], writes=[ptb_])
            qbd = sT("s_qbd", [128, 4 * 4 * 8], BF16)
            S.op("pool", lambda e: e.memset(qbd[:], 0.0), writes=[qbd.b])
            S.op("dve", lambda e, ptv=ptv: e.tensor_copy(out=mkap(qbd[0:64, 0:1], [[32, 4], [8, 4], [1, 4]]),
                                                         in_=mkap(ptv[0:64, 0:1], [[128, 4], [4, 4], [1, 4]])),
                 reads=[ptb_, qbd.b], writes=[qbd.b])
            S.op("dve", lambda e, ptv=ptv: e.tensor_copy(out=mkap(qbd[64:128, 4:5], [[32, 4], [8, 4], [1, 4]]),
                                                         in_=mkap(ptv[64:128, 0:1], [[128, 4], [4, 4], [1, 4]])),
                 reads=[ptb_, qbd.b], writes=[qbd.b])

            pt, ptb_ = ps_next("T", PT_T)
            ptv = pt[:].bitcast(BF16)

            def trr(e, ptv=ptv):
                for hh in range(4):
                    ins = e.transpose(out=ptv[:, hh * 128:(hh + 1) * 128], in_=rqo[:, hh * 128:(hh + 1) * 128], identity=identb[:])
                return ins
            S.op("pe", trr, reads=[rqo.b, identb.b], writes=[ptb_])
            qm = sT("s_qm", [128, 16 * 128], BF16)
            S.op("pool", lambda e: e.memset(qm[:], 0.0), writes=[qm.b])
            S.op("dve", lambda e, ptv=ptv: e.tensor_copy(out=mkap(qm[:, 0:1], [[128, 4], [516, 4], [1, 4]]),
                                                         in_=mkap(ptv[:, 0:1], [[128, 4], [4, 4], [1, 4]])),
                 reads=[ptb_, qm.b], writes=[qm.b])
            km = sT("s_km", [128, 16 * 128], BF16)
            S.op("dve", lambda e: e.tensor_tensor(out=km[:].rearrange("p (b h d) -> p b h d", b=4, h=4),
                                                  in0=mkap(rko[:, 0:1], [[0, 4], [128, 4], [1, 128]]),
                                                  in1=mkap(cs("mask16")[:, 0:1], [[1, 4], [0, 4], [0, 128]]), op=ALU.mult),
                 reads=[rko.b, Cc.b], writes=[km.b])
            pm, pmb = ps_next("M", PT_M)

            def mm_cross(e, pm=pm):
                for hh in range(4):
                    for bl in range(4):
                        sl = bl * 4 + hh
                        ins = e.matmul(pm[:, hh * 128:(hh + 1) * 128], lhsT=qm[:, sl * 128:(sl + 1) * 128], rhs=stbf[:, sl, :],
                                       start=(bl == 0), stop=(bl == 3))
                return ins
            S.op("pe", mm_cross, reads=[qm.b, stbf.b], writes=[pmb])
            S.op("act", lambda e, pm=pm: e.activation(out=cross_sb[:], in_=pm[:, :], func=AF.Copy), reads=[pmb], writes=[cross_sb.b])
            stnew = sT("s_stnew", [128, 16, 128], F32)
            for g4 in range(4):
                pm, pmb = ps_next("M", PT_M)

                def mm_kv(e, g4=g4, pm=pm):
                    for bb in range(4):
                        sl = g4 * 4 + bb
                        ins = e.matmul(pm[:, bb * 128:(bb + 1) * 128], lhsT=km[:, sl * 128:(sl + 1) * 128], rhs=rvo[:, bb * 128:(bb + 1) * 128],
                                       start=True, stop=True)
                    return ins
                S.op("pe", mm_kv, reads=[km.b, rvo.b], writes=[pmb])
                S.op("dve", lambda e, g4=g4, pm=pm: e.tensor_tensor(
                    out=stnew[:, g4 * 4:(g4 + 1) * 4, :].rearrange("p a b -> p (a b)"), in0=pm[:, :],
                    in1=st32[:, g4 * 4:(g4 + 1) * 4, :].rearrange("p a b -> p (a b)"), op=ALU.add),
                    reads=[pmb, st32.b, stnew.b], writes=[stnew.b])
            S.op("pool", lambda e: e.tensor_tensor(out=stnew[:], in0=stnew[:], in1=mkap(cs("g4s")[:, 0:1], [[1, 16], [0, 128]]), op=ALU.mult),
                 reads=[stnew.b, Cc.b], writes=[stnew.b])
            S.dma("sp", lambda e: e.dma_start(out=retso_d.rearrange("b d e -> d b e"), in_=stnew[:]), stnew.b, reads=[stnew.b])

            KG = Ring(nc, st, "s_kg", [NPG, HW], F32, 2)
            VG = Ring(nc, st, "s_vg", [NPG, HW], F32, 1)
            VB = Ring(nc, st, "s_vb", [NPG, POSH * 128], BF16, 2)
            B_vbh = [[Buf("vbh%d_%d" % (i, hf)) for hf in range(2)] for i in range(2)]
            KTr = Ring(nc, st, "s_kt", [128, 4 * NPG], BF16, 3)
            SBt = Ring(nc, st, "s_sb", [NPG, POSH * 8], F32, 2)
            ATr = Ring(nc, st, "s_at", [NPG, POSH * 8], BF16, 2)
            Rr = Ring(nc, st, "s_r", [NPG, 8], F32, 2)
            stall = sT("s_stall", [8, 32, 130], F32)
            evi = 0
            for it in range(32):
                r8, bl = it // 4, it % 4
                hh = r8 % 4
                alib = cs("alibi8", NPG)[:, r8 * POSH:(r8 + 1) * POSH]
                qb = qbd[:, (hh * 4 + bl) * 8:(hh * 4 + bl + 1) * 8]
                vb = VB.next()
                vbh = B_vbh[it % 2]
                psc, pscb = ps_next("SC", PT_SC)
                for hf in range(2):
                    kg = KG.next(); vg = VG.next()
                    S.dma("pool", lambda e, kg=kg, bl=bl, hf=hf, r8=r8: e.indirect_dma_start(
                        out=kg[:], out_offset=None, in_=ck_d[r8][hf][:, :],
                        in_offset=bass.IndirectOffsetOnAxis(ap=ptT[:, bl:bl + 1], axis=0)), kg.b, reads=[ptT.b], writes=[kg.b])
                    S.dma("pool", lambda e, vg=vg, bl=bl, hf=hf, r8=r8: e.indirect_dma_start(
                        out=vg[:], out_offset=None, in_=cv_d[r8][hf][:, :],
                        in_offset=bass.IndirectOffsetOnAxis(ap=ptT[:, bl:bl + 1], axis=0)), vg.b, reads=[ptT.b], writes=[vg.b])
                    S.op("act", lambda e, vb=vb, vg=vg, hf=hf: e.activation(out=vb[:, hf * HW:(hf + 1) * HW], in_=vg[:], func=AF.Copy),
                         reads=[vg.b, vbh[hf]], writes=[vbh[hf]])
                    for pg in range(HP // 4):
                        ptr, ptrb = ps_next("T", PT_T)

                        def trk(e, pg=pg, kg=kg, ptr=ptr):
                            for k in range(4):
                                pl = pg * 4 + k
                                ins = e.transpose(out=ptr[:, k * NPG:(k + 1) * NPG], in_=kg[:, pl * 128:(pl + 1) * 128],
                                                  identity=cs("ident", NPG)[:, 0:NPG])
                            return ins
                        S.op("pe", trk, reads=[kg.b, Cc.b], writes=[ptrb])
                        kt = KTr.next()
                        if evi % 2 == 0:
                            S.op("dve", lambda e, kt=kt, ptr=ptr: e.tensor_copy(out=kt[:], in_=ptr[:, 0:4 * NPG]), reads=[ptrb], writes=[kt.b])
                        else:
                            S.op("act", lambda e, kt=kt, ptr=ptr: e.activation(out=kt[:], in_=ptr[:, 0:4 * NPG], func=AF.Copy), reads=[ptrb], writes=[kt.b])
                        evi += 1

                        def mms(e, pg=pg, kt=kt, psc=psc, qb=qb, hf=hf):
                            for k in range(4):
                                pos = hf * HP + pg * 4 + k
                                ins = e.matmul(psc[0:NPG, pos * 8:(pos + 1) * 8], lhsT=kt[:, k * NPG:(k + 1) * NPG], rhs=qb,
                                               start=True, stop=True)
                            return ins
                        S.op("pe", mms, reads=[kt.b, qbd.b], writes=[pscb])
                sbt = SBt.next(); at = ATr.next(); rr = Rr.next()
                S.op("dve", lambda e, sbt=sbt, psc=psc, alib=alib: e.tensor_tensor(
                    out=sbt[:].rearrange("p (a b) -> p a b", b=8), in0=psc[0:NPG, 0:POSH * 8].rearrange("p (a b) -> p a b", b=8),
                    in1=mkap(alib[:, 0:1], [[1, POSH], [0, 8]]), op=ALU.add), reads=[pscb, Cc.b], writes=[sbt.b])
                S.op("act", lambda e, sbt=sbt, at=at: e.activation(out=at[:], in_=sbt[:], func=AF.Exp, scale=0.125), reads=[sbt.b], writes=[at.b])
                S.op("dve", lambda e, at=at, rr=rr: e.tensor_reduce(out=rr[:], in_=at[:].rearrange("p (a b) -> p b a", b=8), axis=AX.X, op=ALU.add),
                     reads=[at.b], writes=[rr.b])
                poa, poab = ps_next("OA", PT_OA)

                def mmo(e, at=at, vb=vb, poa=poa, rr=rr):
                    for pos in range(POSH):
                        e.matmul(poa[0:8, 0:128], lhsT=at[:, pos * 8:(pos + 1) * 8], rhs=vb[:, pos * 128:(pos + 1) * 128],
                                 start=(pos == 0), stop=(pos == POSH - 1))
                    return e.matmul(poa[0:8, 128:129], lhsT=rr[:], rhs=cs("ones", NPG), start=True, stop=True)
                S.op("pe", mmo, reads=[at.b, vbh[0], vbh[1], rr.b, Cc.b], writes=[poab])
                S.op("dve", lambda e, poa=poa, it=it: e.tensor_copy(out=stall[:, it, 0:129], in_=poa[0:8, 0:129]), reads=[poab, stall.b], writes=[stall.b])
            for c2 in range(2):
                dst = bass.AP(cctm_d.tensor, cctm_d.offset + c2 * 129, [[8 * 258, 4], [258, 8], [4 * 8 * 258, 4], [1, 129]])
                S.dma("sp", lambda e, c2=c2, dst=dst: e.dma_start(
                    out=dst, in_=stall[c2 * 4:(c2 + 1) * 4, :, 0:129].rearrange("p (r b) e -> p r b e", r=8)), stall.b,
                    reads=[stall.b], writes=[B_cctm])
            tokmaj = sT("s_tokmaj", [16, 8, 258], F32)
            S.dma("sp", lambda e: e.dma_start(out=tokmaj[:].rearrange("p a b -> p (a b)"), in_=cctm_d[:, :]), tokmaj.b, reads=[B_cctm], writes=[tokmaj.b])
            S.op("dve", lambda e: e.tensor_tensor(out=pps_sb[0:16, :, :], in0=tokmaj[:, 0:4, :], in1=tokmaj[:, 4:8, :], op=ALU.add),
                 reads=[tokmaj.b, pps_sb.b], writes=[pps_sb.b])
            S.barrier()

        with ExitStack() as st:
          if "skipA" not in DBG:
            def sT(name, shape, dt):
                return T(nc, st, name, shape, dt)
            PT_T = [0, 1]; PT_M = [2, 3]; PT_DS = [4, 5]; PT_DA = [6, 7]

            winb = sT("a_winb", [128, KC, PROJ], BF16)
            wob = sT("a_wob", [128, KC, D], BF16)
            with ExitStack() as st_w:
                stg = Ring(nc, st_w, "a_stg", [128, PROJ], F32, 2)
                for c in range(KC):
                    sg_ = stg.next()
                    S.dma("sp", lambda e, sg_=sg_, c=c: e.dma_start(out=sg_[:], in_=win_d[c * 128:(c + 1) * 128, :]), sg_.b, writes=[sg_.b])
                    for (eng, c0, c1) in (("pool", 0, 1024), ("dve", 1024, 2304), ("act", 2304, PROJ)):
                        if eng == "act":
                            S.op(eng, lambda e, sg_=sg_, c=c, c0=c0, c1=c1: e.activation(out=winb[:, c, c0:c1], in_=sg_[:, c0:c1], func=AF.Copy, scale=lnc[:, c:c + 1]),
                                 reads=[sg_.b, lnc.b, winb.b], writes=[winb.b])
                        else:
                            S.op(eng, lambda e, sg_=sg_, c=c, c0=c0, c1=c1: e.tensor_scalar(out=winb[:, c, c0:c1], in0=sg_[:, c0:c1], scalar1=lnc[:, c:c + 1], scalar2=None, op0=ALU.mult),
                                 reads=[sg_.b, lnc.b, winb.b], writes=[winb.b])
                for c in range(KC):
                    sg_ = stg.next()
                    S.dma("sp", lambda e, sg_=sg_, c=c: e.dma_start(out=sg_[:, 0:D], in_=wo_d[c * 128:(c + 1) * 128, :]), sg_.b, writes=[sg_.b])
                    eng = "pool" if c % 2 == 0 else "dve"
                    if c < 4:
                        S.op(eng, lambda e, sg_=sg_, c=c: e.tensor_copy(out=wob[:, c, :], in_=sg_[:, 0:D]), reads=[sg_.b, wob.b], writes=[wob.b])
                    else:
                        S.op(eng, lambda e, sg_=sg_, c=c: e.tensor_scalar(out=wob[:, c, :], in0=sg_[:, 0:D], scalar1=subc[:, c - 4:c - 3],
                                                                          scalar2=1.0 - LAM_INIT, op0=ALU.mult, op1=ALU.mult),
                             reads=[sg_.b, subc.b, wob.b], writes=[wob.b])
                S.barrier()

            dkT_all = sT("a_dkT", [128, 4, (NT + 1) * 128], BF16)
            vaug = sT("a_vaug", [128, NT + 1, 4 * 130], BF16)
            S.op("pool", lambda e: e.memset(vaug[:], 1.0), writes=[vaug.b])
            B_dkT = [Buf("dkT%d" % i) for i in range(NT + 1)]
            B_va = [Buf("va%d" % i) for i in range(NT + 1)]
            for bb in B_va:
                bb.w = vaug.b.w
            state32 = sT("a_state32", [128, 512], F32)
            statebf = sT("a_statebf", [128, 512], BF16)

            xt_r = Ring(nc, st, "a_xt", [128, D], F32, 3)
            junk = sT("a_junk", [128, D], BF16)
            ss_r = Ring(nc, st, "a_ss", [128, 1], F32, 2); rstd_r = Ring(nc, st, "a_rstd", [128, 1], F32, 2)
            xn_r = Ring(nc, st, "a_xn", [128, D], BF16, 1)
            xnT_r = Ring(nc, st, "a_xnT", [128, KC * 128], BF16, 2)
            rq_r = Ring(nc, st, "a_rq", [128, 512], BF16, 2); rk_r = Ring(nc, st, "a_rk", [128, 512], BF16, 2)
            rv_r = Ring(nc, st, "a_rv", [128, 512], BF16, 2); sg_r = Ring(nc, st, "a_sg", [128, 512], BF16, 2)
            tmp_r = Ring(nc, st, "a_tmp", [128, 512], F32, 3)
            sq_r = qt_r = ro_r = rsq_r = tmp_r
            ssq_r = Ring(nc, st, "a_ssq", [128, 8], F32, 2); rs8_r = Ring(nc, st, "a_rs8", [128, 8], F32, 2)
            qn_r = Ring(nc, st, "a_qn", [128, 512], BF16, 2); kn_r = Ring(nc, st, "a_kn", [128, 512], BF16, 2)
            k32_r = Ring(nc, st, "a_k32", [128, 512], F32, 1); v32_r = Ring(nc, st, "a_v32", [128, 512], F32, 1)
            rqT_r = Ring(nc, st, "a_rqT", [128, 512], BF16, 2); rkT_r = Ring(nc, st, "a_rkT", [128, 512], BF16, 2)
            dqT_r = Ring(nc, st, "a_dqT", [128, 1024], BF16, 2)
            for _t in dqT_r.items:
                S.op("pool", lambda e, _t=_t: e.memset(_t[:], 0.0), writes=[_t.b])
            sm_r = Ring(nc, st, "a_sm", [128, 512], BF16, 1)
            st4_r = Ring(nc, st, "a_st4", [128, 16], F32, 2)
            mixed_r = Ring(nc, st, "a_mixed", [128, D], BF16, 1)
            mixT_r = Ring(nc, st, "a_mixT", [128, KC * 128], BF16, 1)
            at_r = Ring(nc, st, "a_at", [128, 256], BF16, 4)
            acc_r = Ring(nc, st, "a_acc", [128, 2 * 130], F32, 2)
            fin_r = Ring(nc, st, "a_fin", [128, 8], F32, 2)
            o1_r = Ring(nc, st, "a_o1", [128, 128], F32, 2); dd_r = Ring(nc, st, "a_dd", [128, 128], F32, 2)
            jk_r = Ring(nc, st, "a_jk", [128, 128], BF16, 2)

            def attn_tile(i):
                smp = (i == NT)
                xt = xt_r.next()
                src = xs_d[:, :] if smp else x_d[i * 128:(i + 1) * 128, :]
                S.dma("sp", lambda e: e.dma_start(out=xt[:], in_=src), xt.b, writes=[xt.b])
                ss = ss_r.next(); rstd = rstd_r.next(); xn = xn_r.next(); xnT = xnT_r.next()
                norm_T(xt, 128, junk, ss, rstd, xn, lambda _: xnT[:], xnT.b, PT_T)
                if ASTOP <= 1:
                    return
                tq = cs("tabq_s") if smp else cs("tabq")
                tk = cs("tabk_s") if smp else cs("tabk")
                rq = rq_r.next(); rk = rk_r.next(); rv = rv_r.next(); sg = sg_r.next()
                qn = qn_r.next(); kn = kn_r.next(); k32 = k32_r.next(); v32 = v32_r.next()
                for n in range(7):
                    pm, pmb = ps_next("M", PT_M)

                    def mm(e, n=n, pm=pm):
                        for k in range(KC):
                            ins = e.matmul(pm[:, :], lhsT=xnT[:, k * 128:(k + 1) * 128], rhs=winb[:, k, n * 512:(n + 1) * 512],
                                           start=(k == 0), stop=(k == KC - 1))
                        return ins
                    S.op("pe", mm, reads=[xnT.b, winb.b], writes=[pmb])
                    if n == 0 or n == 1:
                        dst = rq if n == 0 else rk
                        tab = tq if n == 0 else tk
                        S.op("dve", lambda e, pm=pm, dst=dst, tab=tab: e.tensor_tensor(
                            out=dst[:].rearrange("p (a b) -> p a b", a=4), in0=pm[:, :].rearrange("p (a b) -> p a b", a=4),
                            in1=mkap(tab[:, 0:1], [[1, 4], [0, 128]]), op=ALU.mult), reads=[pmb, Cc.b], writes=[dst.b])
                    elif n == 2:
                        S.op("act", lambda e, pm=pm: e.activation(out=rv[:], in_=pm[:, :], func=AF.Copy), reads=[pmb], writes=[rv.b])
                    elif n == 3:
                        S.op("act", lambda e, pm=pm: e.activation(out=sg[:], in_=pm[:, :], func=AF.Silu), reads=[pmb], writes=[sg.b])
                    elif n == 4 or n == 5:
                        sq = sq_r.next(); ssq = ssq_r.next(); rs8 = rs8_r.next(); qt = qt_r.next()
                        S.op("act", lambda e, pm=pm, sq=sq: e.activation(out=sq[:], in_=pm[:, :], func=AF.Square), reads=[pmb], writes=[sq.b])
                        S.op("dve", lambda e, sq=sq, ssq=ssq: e.tensor_reduce(out=ssq[:], in_=sq[:].rearrange("p (a b) -> p a b", a=8),
                                                                              axis=AX.X, op=ALU.add), reads=[sq.b], writes=[ssq.b])
                        S.op("act", lambda e, ssq=ssq, rs8=rs8: e.activation(out=rs8[:], in_=ssq[:], func=AF.Ln, bias=epsc[:], scale=1.0 / 64),
                             reads=[ssq.b, epsc.b], writes=[rs8.b])
                        S.op("act", lambda e, rs8=rs8: e.activation(out=rs8[:], in_=rs8[:], func=AF.Exp, scale=-0.5), reads=[rs8.b], writes=[rs8.b])
                        S.op("dve", lambda e, pm=pm, qt=qt, rs8=rs8: e.tensor_tensor(
                            out=qt[:].rearrange("p (a b) -> p a b", a=8), in0=pm[:, :].rearrange("p (a b) -> p a b", a=8),
                            in1=mkap(rs8[:, 0:1], [[1, 8], [0, 64]]), op=ALU.mult), reads=[pmb, rs8.b], writes=[qt.b])
                        voff = V_QW if n == 4 else V_KW
                        wv = mkap(vec[:, voff:voff + 1], [[0, 8], [1, 64]])
                        if n == 4:
                            S.op("pool", lambda e, qt=qt, wv=wv: e.tensor_tensor(out=qn[:].rearrange("p (a b) -> p a b", a=8),
                                                                                 in0=qt[:].rearrange("p (a b) -> p a b", a=8), in1=wv, op=ALU.mult),
                                 reads=[qt.b, vec.b], writes=[qn.b])
                        else:
                            S.op("pool", lambda e, qt=qt, wv=wv: e.tensor_tensor(out=k32[:].rearrange("p (a b) -> p a b", a=8),
                                                                                 in0=qt[:].rearrange("p (a b) -> p a b", a=8), in1=wv, op=ALU.mult),
                                 reads=[qt.b, vec.b], writes=[k32.b])
                            S.op("pool", lambda e: e.tensor_copy(out=kn[:], in_=k32[:]), reads=[k32.b], writes=[kn.b])
                            kdst = kso_d[:, :] if smp else ko_d[i * 128:(i + 1) * 128, :]
                            S.dma("sp", lambda e, kdst=kdst: e.dma_start(out=kdst, in_=k32[:]), k32.b, reads=[k32.b])
                    else:
                        S.op("act", lambda e, pm=pm: e.activation(out=v32[:], in_=pm[:, :], func=AF.Copy), reads=[pmb], writes=[v32.b])
                        S.op("pool", lambda e: e.tensor_copy(out=vaug[:, i, :].rearrange("p (a b) -> p a b", a=4)[:, :, 0:128],
                                                             in_=v32[:].rearrange("p (a b) -> p a b", a=4)),
                             reads=[v32.b, B_va[i]], writes=[B_va[i]])
                        vdst = vso_d[:, :] if smp else vo_d[i * 128:(i + 1) * 128, :]
                        S.dma("sp", lambda e, vdst=vdst: e.dma_start(out=vdst, in_=v32[:]), v32.b, reads=[v32.b])
                if ASTOP <= 2:
                    return
                rqT = rqT_r.next(); rkT = rkT_r.next(); dqT = dqT_r.next()
                for (srcT, dst_ap, dst_b) in ((rq, rqT[:], rqT.b), (rk, rkT[:], rkT.b), (qn, "dq", dqT.b),
                                              (kn, None, B_dkT[i])):
                    pt, ptb_ = ps_next("T", PT_T)
                    ptv = pt[:].bitcast(BF16)

                    def tr4(e, srcT=srcT, ptv=ptv):
                        for hh in range(4):
                            ins = e.transpose(out=ptv[:, hh * 128:(hh + 1) * 128], in_=srcT[:, hh * 128:(hh + 1) * 128], identity=identb[:])
                        return ins
                    S.op("pe", tr4, reads=[srcT.b, identb.b], writes=[ptb_])
                    if dst_ap is None:
                        S.op("dve", lambda e, ptv=ptv: e.tensor_copy(out=dkT_all[:, :, i * 128:(i + 1) * 128],
                                                                     in_=ptv[:, 0:512].rearrange("p (a b) -> p a b", a=4)),
                             reads=[ptb_], writes=[dst_b])
                    elif dst_ap == "dq":
                        dq4 = dqT[:].rearrange("p (h c q) -> p h c q", h=4, c=2)
                        S.op("act", lambda e, ptv=ptv, dq4=dq4: e.activation(out=dq4[0:64, :, 0, :], in_=ptv[0:64, 0:512].rearrange("p (h q) -> p h q", h=4), func=AF.Copy),
                             reads=[ptb_, dst_b], writes=[dst_b])
                        S.op("dve", lambda e, ptv=ptv, dq4=dq4: e.tensor_copy(out=dq4[64:128, :, 1, :], in_=ptv[64:128, 0:512].rearrange("p (h q) -> p h q", h=4)),
                             reads=[ptb_, dst_b], writes=[dst_b])
                    else:
                        S.op("act", lambda e, ptv=ptv, dst_ap=dst_ap: e.activation(out=dst_ap, in_=ptv[:, 0:512], func=AF.Copy),
                             reads=[ptb_], writes=[dst_b])
                if ASTOP <= 3:
                    return
                pm, pmb = ps_next("M", PT_M)

                def mm_s(e, pm=pm):
                    for hh in range(4):
                        ins = e.matmul(pm[:, hh * 128:(hh + 1) * 128], lhsT=rkT[:, hh * 128:(hh + 1) * 128], rhs=rqT[:, hh * 128:(hh + 1) * 128],
                                       start=True, stop=True)
                    return ins
                S.op("pe", mm_s, reads=[rkT.b, rqT.b], writes=[pmb])
                sm = sm_r.next()
                mk_ = cs("maskS") if smp else cs("maskT")
                S.op("dve", lambda e, pm=pm: e.tensor_tensor(out=sm[:].rearrange("p (a b) -> p a b", a=4), in0=pm[:, :].rearrange("p (a b) -> p a b", a=4),
                                                             in1=mkap(mk_[:, 0:1], [[0, 4], [1, 128]]), op=ALU.mult), reads=[pmb, Cc.b], writes=[sm.b])
                po, pob = ps_next("M", PT_M)
                use_state = (not smp) and i > 0

                def mm_o(e, po=po):
                    for hh in range(4):
                        ins = e.matmul(po[:, hh * 128:(hh + 1) * 128], lhsT=sm[:, hh * 128:(hh + 1) * 128], rhs=rv[:, hh * 128:(hh + 1) * 128],
                                       start=True, stop=not use_state)
                        if use_state:
                            ins = e.matmul(po[:, hh * 128:(hh + 1) * 128], lhsT=rqT[:, hh * 128:(hh + 1) * 128], rhs=statebf[:, hh * 128:(hh + 1) * 128],
                                           start=False, stop=True)
                    return ins
                S.op("pe", mm_o, reads=[sm.b, rv.b, rqT.b] + ([statebf.b] if use_state else []), writes=[pob])
                ro = ro_r.next()
                if smp:
                    S.op("dve", lambda e, po=po: e.tensor_tensor(out=ro[:], in0=po[:, :], in1=cross_sb[:], op=ALU.add), reads=[pob, cross_sb.b], writes=[ro.b])
                else:
                    S.op("act", lambda e, po=po: e.activation(out=ro[:], in_=po[:, :], func=AF.Copy), reads=[pob], writes=[ro.b])
                    pk, pkb = ps_next("M", PT_M)

                    def mm_kv(e, pk=pk):
                        for hh in range(4):
                            ins = e.matmul(pk[:, hh * 128:(hh + 1) * 128], lhsT=rk[:, hh * 128:(hh + 1) * 128], rhs=rv[:, hh * 128:(hh + 1) * 128],
                                           start=True, stop=True)
                        return ins
                    S.op("pe", mm_kv, reads=[rk.b, rv.b], writes=[pkb])
                    gLb = mkap(cs("gL")[:, 0:1], [[1, 4], [0, 128]])
                    if i == 0:
                        S.op("dve", lambda e, pk=pk: e.tensor_tensor(out=state32[:].rearrange("p (a b) -> p a b", a=4), in0=pk[:, :].rearrange("p (a b) -> p a b", a=4),
                                                                     in1=gLb, op=ALU.mult), reads=[pkb, Cc.b, state32.b], writes=[state32.b])
                    else:
                        S.op("dve", lambda e, pk=pk: e.tensor_tensor(out=state32[:], in0=pk[:, :], in1=state32[:], op=ALU.add),
                             reads=[pkb, state32.b], writes=[state32.b])
                        S.op("pool", lambda e: e.tensor_tensor(out=state32[:].rearrange("p (a b) -> p a b", a=4), in0=state32[:].rearrange("p (a b) -> p a b", a=4),
                                                               in1=gLb, op=ALU.mult), reads=[state32.b, Cc.b], writes=[state32.b])
                    if i < NT - 1:
                        S.op("pool", lambda e: e.tensor_copy(out=statebf[:], in_=state32[:]), reads=[state32.b, statebf.b], writes=[statebf.b])
                    else:
                        S.dma("sp", lambda e: e.dma_start(out=reto_d.rearrange("h d e -> d h e"), in_=state32[:].rearrange("p (a b) -> p a b", a=4)),
                              state32.b, reads=[state32.b])
                        out_bufs.append(state32.b)
                if ASTOP <= 4:
                    return
                mixed = mixed_r.next()
                rsq = rsq_r.next(); st4 = st4_r.next()
                S.op("act", lambda e: e.activation(out=rsq[:], in_=ro[:], func=AF.Square), reads=[ro.b], writes=[rsq.b])
                S.op("dve", lambda e: e.tensor_reduce(out=st4[:, 0:4], in_=ro[:].rearrange("p (a b) -> p a b", a=4), axis=AX.X, op=ALU.add),
                     reads=[ro.b], writes=[st4.b])
                S.op("dve", lambda e: e.tensor_reduce(out=st4[:, 4:8], in_=rsq[:].rearrange("p (a b) -> p a b", a=4), axis=AX.X, op=ALU.add),
                     reads=[rsq.b, st4.b], writes=[st4.b])
                S.op("dve", lambda e: e.tensor_scalar(out=st4[:, 0:4], in0=st4[:, 0:4], scalar1=1.0 / 128, scalar2=None, op0=ALU.mult),
                     reads=[st4.b], writes=[st4.b])
                S.op("dve", lambda e: e.tensor_tensor(out=st4[:, 8:12], in0=st4[:, 0:4], in1=st4[:, 0:4], op=ALU.mult), reads=[st4.b], writes=[st4.b])
                S.op("dve", lambda e: e.scalar_tensor_tensor(out=st4[:, 4:8], in0=st4[:, 4:8], scalar=1.0 / 128, in1=st4[:, 8:12],
                                                             op0=ALU.mult, op1=ALU.subtract), reads=[st4.b], writes=[st4.b])
                S.op("act", lambda e: e.activation(out=st4[:, 12:16], in_=st4[:, 4:8], func=AF.Ln, bias=epsc[:], scale=1.0), reads=[st4.b, epsc.b], writes=[st4.b])
                S.op("act", lambda e: e.activation(out=st4[:, 12:16], in_=st4[:, 12:16], func=AF.Exp, scale=-0.5), reads=[st4.b], writes=[st4.b])
                S.op("dve", lambda e: e.tensor_tensor(out=ro[:].rearrange("p (a b) -> p a b", a=4), in0=ro[:].rearrange("p (a b) -> p a b", a=4),
                                                      in1=mkap(st4[:, 0:1], [[1, 4], [0, 128]]), op=ALU.subtract), reads=[ro.b, st4.b], writes=[ro.b])
                S.op("pool", lambda e: e.tensor_tensor(out=ro[:].rearrange("p (a b) -> p a b", a=4), in0=ro[:].rearrange("p (a b) -> p a b", a=4),
                                                       in1=mkap(st4[:, 12:13], [[1, 4], [0, 128]]), op=ALU.mult), reads=[ro.b, st4.b], writes=[ro.b])
                S.op("pool", lambda e: e.tensor_tensor(out=ro[:], in0=ro[:], in1=vec[:, V_GNW:V_GNW + 512], op=ALU.mult), reads=[ro.b, vec.b], writes=[ro.b])
                S.op("pool", lambda e: e.tensor_tensor(out=ro[:], in0=ro[:], in1=vec[:, V_GNB:V_GNB + 512], op=ALU.add), reads=[ro.b, vec.b], writes=[ro.b])
                S.op("dve", lambda e: e.tensor_tensor(out=mixed[:, 0:512], in0=ro[:], in1=sg[:], op=ALU.mult), reads=[ro.b, sg.b, mixed.b], writes=[mixed.b])
                if ASTOP <= 5:
                    return
                if smp:
                    pass
                jlist = [NT] if smp else list(range(i + 1))
                for hh in range(4):
                    pa, pab = ps_next("DA", PT_DA)
                    for jn, jt in enumerate(jlist):
                        pd, pdb = ps_next("DS", PT_DS)

                        def mm_ds(e, pd=pd, jt=jt, hh=hh):
                            return e.matmul(pd[:, 0:256], lhsT=dkT_all[:, hh, jt * 128:(jt + 1) * 128],
                                            rhs=dqT[:, hh * 256:(hh + 1) * 256], start=True, stop=True)
                        S.op("pe", mm_ds, reads=[B_dkT[jt], dqT.b], writes=[pdb])
                        at = at_r.next()
                        if smp:
                            bo = cfg.coff["abias_s"][0] + hh
                        else:
                            bo = cfg.coff["abias"][0] + (i - jt) * 4 + hh
                        S.op("act", lambda e, pd=pd, at=at, bo=bo: e.activation(out=at[:], in_=pd[:, 0:256], func=AF.Exp, bias=Cc[:, bo:bo + 1], scale=0.125),
                             reads=[pdb, Cc.b], writes=[at.b])
                        if smp or jt == i:
                            S.op("pool", lambda e, at=at: e.tensor_tensor(out=at[:].rearrange("p (a b) -> p a b", a=2), in0=at[:].rearrange("p (a b) -> p a b", a=2),
                                                                          in1=mkap(mk_[:, 0:1], [[0, 2], [1, 128]]), op=ALU.mult),
                                 reads=[at.b, Cc.b], writes=[at.b])

                        def mm_av(e, pa=pa, at=at, jt=jt, hh=hh, jn=jn):
                            for c2 in range(2):
                                ins = e.matmul(pa[:, c2 * 256:c2 * 256 + 129], lhsT=at[:, c2 * 128:(c2 + 1) * 128],
                                               rhs=vaug[:, jt, hh * 130:hh * 130 + 129], start=(jn == 0 and c2 == 0),
                                               stop=(jn == len(jlist) - 1 and c2 == 1))
                            return ins
                        S.op("pe", mm_av, reads=[at.b, B_va[jt]], writes=[pab])
                    acc = acc_r.next(); fin = fin_r.next(); o1 = o1_r.next(); dd = dd_r.next(); jk = jk_r.next()
                    accv = acc[:].rearrange("p (a b) -> p a b", a=2)
                    pav = pa[:, :].rearrange("p (a b) -> p a b", a=2)[:, :, 0:129]
                    if smp:
                        S.op("dve", lambda e, pav=pav, accv=accv, hh=hh: e.tensor_tensor(
                            out=accv[:, :, 0:129], in0=pav, in1=pps_sb[:, hh, :].rearrange("p (a b) -> p a b", a=2), op=ALU.add),
                            reads=[pab, pps_sb.b], writes=[acc.b])
                    else:
                        S.op("act", lambda e, pav=pav, accv=accv: e.activation(out=accv[:, :, 0:129], in_=pav, func=AF.Copy), reads=[pab], writes=[acc.b])
                    S.op("dve", lambda e, accv=accv, fin=fin: e.reciprocal(out=fin[:, 0:2], in_=accv[:, :, 128]), reads=[acc.b], writes=[fin.b])
                    S.op("dve", lambda e, fin=fin: e.tensor_tensor(out=fin[:, 2:3], in0=fin[:, 1:2], in1=nlam[:], op=ALU.mult), reads=[fin.b, nlam.b], writes=[fin.b])
                    S.op("dve", lambda e, accv=accv, fin=fin, o1=o1: e.tensor_scalar(out=o1[:], in0=accv[:, 0, 0:128], scalar1=fin[:, 0:1], scalar2=None, op0=ALU.mult),
                         reads=[acc.b, fin.b], writes=[o1.b])
                    S.op("dve", lambda e, accv=accv, fin=fin, o1=o1, dd=dd: e.scalar_tensor_tensor(out=dd[:], in0=accv[:, 1, 0:128], scalar=fin[:, 2:3], in1=o1[:],
                                                                                                 op0=ALU.mult, op1=ALU.add), reads=[acc.b, fin.b, o1.b], writes=[dd.b])
                    S.op("act", lambda e, dd=dd, jk=jk, fin=fin: e.activation(out=jk[:], in_=dd[:], func=AF.Square, accum_out=fin[:, 3:4]), reads=[dd.b, fin.b], writes=[jk.b, fin.b])
                    S.op("act", lambda e, fin=fin: e.activation(out=fin[:, 4:5], in_=fin[:, 3:4], func=AF.Ln, bias=epsc[:], scale=1.0 / 128), reads=[fin.b, epsc.b], writes=[fin.b])
                    S.op("act", lambda e, fin=fin: e.activation(out=fin[:, 4:5], in_=fin[:, 4:5], func=AF.Exp, scale=-0.5), reads=[fin.b], writes=[fin.b])
                    S.op("dve", lambda e, dd=dd, fin=fin, hh=hh: e.tensor_scalar(out=mixed[:, 512 + hh * 128:512 + (hh + 1) * 128], in0=dd[:], scalar1=fin[:, 4:5],
                                                                               scalar2=None, op0=ALU.mult), reads=[dd.b, fin.b, mixed.b], writes=[mixed.b])
                if ASTOP <= 6:
                    return
                mixT = mixT_r.next()
                pt, ptb_ = ps_next("T", PT_T)
                ptv = pt[:].bitcast(BF16)

                def tr8(e, ptv=ptv):
                    for k in range(KC):
                        ins = e.transpose(out=ptv[:, k * 128:(k + 1) * 128], in_=mixed[:, k * 128:(k + 1) * 128], identity=identb[:])
                    return ins
                S.op("pe", tr8, reads=[mixed.b, identb.b], writes=[ptb_])
                S.op("act", lambda e, ptv=ptv: e.activation(out=mixT[:], in_=ptv[:, 0:KC * 128], func=AF.Copy), reads=[ptb_], writes=[mixT.b])
                for mh in range(2):
                    pm, pmb = ps_next("M", PT_M)

                    def mm_wo(e, pm=pm, mh=mh):
                        for k in range(KC):
                            ins = e.matmul(pm[:, :], lhsT=mixT[:, k * 128:(k + 1) * 128], rhs=wob[:, k, mh * 512:(mh + 1) * 512],
                                           start=(k == 0), stop=(k == KC - 1))
                        return ins
                    S.op("pe", mm_wo, reads=[mixT.b, wob.b], writes=[pmb])
                    S.op("dve", lambda e, pm=pm, mh=mh: e.tensor_tensor(out=xt[:, mh * 512:(mh + 1) * 512], in0=pm[:, :], in1=xt[:, mh * 512:(mh + 1) * 512], op=ALU.add),
                         reads=[pmb, xt.b], writes=[xt.b])
                S.dma("sp", lambda e: e.dma_start(out=h1s_d[i * 128:(i + 1) * 128, :], in_=xt[:]), xt.b, reads=[xt.b], writes=[B_h1s[i]])

            for i in range(min(NT + 1, ATILES)):
                attn_tile(i)
            S.barrier()

        with ExitStack() as st:
          if "skipB" not in DBG:
            def sT(name, shape, dt):
                return T(nc, st, name, shape, dt)
            PT_T = [0, 1]; PT_G = [2, 3]; PT_U = [4, 5]; PT_M = [6, 7]
            GMAX = max(len(g) for g in cfg.groups)
            TOKMAX = GMAX * 128
            wpgb = sT("b_wpgb", [128, KC, D], BF16)
            wppb = sT("b_wppb", [128, 2, D], BF16)
            stg = Ring(nc, st, "b_stg", [128, 8 * 256], F32, 2)
            stg_o = Ring(nc, st, "b_stgo", [128, D], F32, 2)
            for c in range(KC):
                sg_ = stg_o.next()
                S.dma("sp", lambda e, sg_=sg_, c=c: e.dma_start(out=sg_[:], in_=wpg_d[c * 128:(c + 1) * 128, :]), sg_.b, writes=[sg_.b])
                S.op("pool", lambda e, sg_=sg_, c=c: e.tensor_scalar(out=wpgb[:, c, :], in0=sg_[:], scalar1=lnc[:, 16 + c:17 + c], scalar2=None, op0=ALU.mult),
                     reads=[sg_.b, lnc.b, wpgb.b], writes=[wpgb.b])
            for c in range(2):
                sg_ = stg_o.next()
                S.dma("sp", lambda e, sg_=sg_, c=c: e.dma_start(out=sg_[:], in_=wpp_d[c * 128:(c + 1) * 128, :]), sg_.b, writes=[sg_.b])
                S.op("pool", lambda e, sg_=sg_, c=c: e.tensor_copy(out=wppb[:, c, :], in_=sg_[:]), reads=[sg_.b, wppb.b], writes=[wppb.b])

            H = sT("b_H", [128, GMAX, D], F32)
            B_H = [Buf("H%d" % k) for k in range(GMAX)]
            hnT = sT("b_hnT", [128, KC, TOKMAX], BF16)
            B_hnT = [Buf("hnT%d" % k) for k in range(GMAX)]
            actT = sT("b_actT", [128, NBH, TOKMAX], BF16)
            wfib = Ring(nc, st, "b_wfib", [128, 8 * 256], BF16, 2)
            wfob = sT("b_wfob", [128, NBH, D], BF16)
            B_wfo = [Buf("wfo%d" % k) for k in range(NBH)]
            B_act = [Buf("act%d" % k) for k in range(NBH)]
            junk = sT("b_junk", [128, D], BF16)
            ss_r = Ring(nc, st, "b_ss", [128, 1], F32, 2); rstd_r = Ring(nc, st, "b_rstd", [128, 1], F32, 2)
            xn_r = Ring(nc, st, "b_xn", [128, D], BF16, 1)
            sgt_r = Ring(nc, st, "b_sgt", [128, 512], BF16, 2)
            h2nT_r = Ring(nc, st, "b_h2nT", [128, KC * 128], BF16, 1)
            p32_r = Ring(nc, st, "b_p32", [128, 256], F32, 1); pbf_r = Ring(nc, st, "b_pbf", [128, 256], BF16, 1)
            pT_r = Ring(nc, st, "b_pT", [128, 256], BF16, 1)
            sig_r = Ring(nc, st, "b_sig", [128, 512], F32, 1)
            yt_r = Ring(nc, st, "b_yt", [128, D], F32, 1)

            for grp in cfg.groups:
                G = len(grp)
                NTOK = G * 128
                tgs = [(t0, min(512, NTOK - t0)) for t0 in range(0, NTOK, 512)]
                for k, ti in enumerate(grp):
                    S.dma("sp", lambda e, k=k, ti=ti: e.dma_start(out=H[:, k, :], in_=h1s_d[ti * 128:(ti + 1) * 128, :]), B_H[k],
                          reads=[B_h1s[ti]], writes=[B_H[k]])
                    ss = ss_r.next(); rstd = rstd_r.next(); xn = xn_r.next()
                    Hk = _View(H, k, B_H[k])
                    _norm_T_view(S, nc, Hk, junk, ss, rstd, xn, hnT, k, B_hnT[k], ps_next, PT_T, identb, epsc)
                for half in range(2):
                    for nbl in range(NBH):
                        nb = half * NBH + nbl
                        sg_ = stg.next()
                        S.dma("sp", lambda e, sg_=sg_, nb=nb: e.dma_start(out=sg_[:], in_=wfi_d[nb, :, :]), sg_.b, writes=[sg_.b])
                        wb = wfib.next()
                        S.op("pool", lambda e, sg_=sg_, wb=wb: e.tensor_tensor(out=wb[:].rearrange("p (c n) -> p c n", c=8), in0=sg_[:].rearrange("p (c n) -> p c n", c=8),
                                                                               in1=mkap(lnc[:, 8:9], [[1, 8], [0, 256]]), op=ALU.mult),
                             reads=[sg_.b, lnc.b], writes=[wb.b])
                        so = stg_o.next()
                        S.dma("sp", lambda e, so=so, nb=nb: e.dma_start(out=so[:], in_=wfo_d[nb * 128:(nb + 1) * 128, :]), so.b, writes=[so.b])
                        S.op("pool", lambda e, so=so, nbl=nbl: e.tensor_copy(out=wfob[:, nbl, :], in_=so[:]), reads=[so.b, B_wfo[nbl]], writes=[B_wfo[nbl]])
                        for (t0, tn) in tgs:
                            ks = list(range(t0 // 128, (t0 + tn) // 128))
                            pg, pgb = ps_next("G", PT_G)
                            pu, pub = ps_next("U", PT_U)

                            def mm_gu(e, wb=wb, pg=pg, pu=pu, t0=t0, tn=tn):
                                for (pp, co) in ((pg, 0), (pu, 128)):
                                    for k in range(KC):
                                        ins = e.matmul(pp[:, 0:tn], lhsT=wb[:, k * 256 + co:k * 256 + co + 128], rhs=hnT[:, k, t0:t0 + tn],
                                                       start=(k == 0), stop=(k == KC - 1))
                                return ins
                            S.op("pe", mm_gu, reads=[wb.b] + [B_hnT[k] for k in ks], writes=[pgb, pub])
                            sgt = sgt_r.next()
                            S.op("act", lambda e, pg=pg, sgt=sgt, tn=tn: e.activation(out=sgt[:, 0:tn], in_=pg[:, 0:tn], func=AF.Silu), reads=[pgb], writes=[sgt.b])
                            S.op("dve", lambda e, pu=pu, sgt=sgt, nbl=nbl, t0=t0, tn=tn: e.tensor_tensor(out=actT[:, nbl, t0:t0 + tn], in0=pu[:, 0:tn], in1=sgt[:, 0:tn], op=ALU.mult),
                                 reads=[pub, sgt.b, B_act[nbl]], writes=[B_act[nbl]])
                    for k in range(G):
                        for mh in range(2):
                            pm, pmb = ps_next("M", PT_M)

                            def mm_fo(e, pm=pm, k=k, mh=mh):
                                for nbl in range(NBH):
                                    ins = e.matmul(pm[:, :], lhsT=actT[:, nbl, k * 128:(k + 1) * 128], rhs=wfob[:, nbl, mh * 512:(mh + 1) * 512],
                                                   start=(nbl == 0), stop=(nbl == NBH - 1))
                                return ins
                            S.op("pe", mm_fo, reads=B_act + B_wfo, writes=[pmb])
                            S.op("dve", lambda e, pm=pm, k=k, mh=mh: e.tensor_tensor(out=H[:, k, mh * 512:(mh + 1) * 512], in0=pm[:, :], in1=H[:, k, mh * 512:(mh + 1) * 512], op=ALU.add),
                                 reads=[pmb, B_H[k]], writes=[B_H[k]])
                for k, ti in enumerate(grp):
                    smp = (ti == NT)
                    ss = ss_r.next(); rstd = rstd_r.next(); xn = xn_r.next(); h2nT = h2nT_r.next()
                    Hk = _View(H, k, B_H[k])
                    _norm_T_flat(S, nc, Hk, junk, ss, rstd, xn, h2nT, ps_next, PT_T, identb, epsc)
                    p32 = p32_r.next(); pbf = pbf_r.next(); pT = pT_r.next()
                    psrc = psm_d[:, :] if smp else p_d[ti * 128:(ti + 1) * 128, :]
                    S.dma("sp", lambda e, p32=p32, psrc=psrc: e.dma_start(out=p32[:], in_=psrc), p32.b, writes=[p32.b])
                    S.op("act", lambda e, p32=p32, pbf=pbf: e.activation(out=pbf[:], in_=p32[:], func=AF.Copy), reads=[p32.b], writes=[pbf.b])
                    pt, ptb_ = ps_next("T", PT_T)
                    ptv = pt[:].bitcast(BF16)

                    def tr2(e, ptv=ptv, pbf=pbf):
                        for c in range(2):
                            ins = e.transpose(out=ptv[:, c * 128:(c + 1) * 128], in_=pbf[:, c * 128:(c + 1) * 128], identity=identb[:])
                        return ins
                    S.op("pe", tr2, reads=[pbf.b, identb.b], writes=[ptb_])
                    S.op("dve", lambda e, ptv=ptv, pT=pT: e.tensor_copy(out=pT[:], in_=ptv[:, 0:256]), reads=[ptb_], writes=[pT.b])
                    yt = yt_r.next()
                    for mh in range(2):
                        pgt, pgtb = ps_next("G", PT_G)
                        ppj, ppjb = ps_next("U", PT_U)

                        def mm_gate(e, pgt=pgt, mh=mh, h2nT=h2nT):
                            for c in range(KC):
                                ins = e.matmul(pgt[:, :], lhsT=h2nT[:, c * 128:(c + 1) * 128], rhs=wpgb[:, c, mh * 512:(mh + 1) * 512], start=(c == 0), stop=(c == KC - 1))
                            return ins
                        S.op("pe", mm_gate, reads=[h2nT.b, wpgb.b], writes=[pgtb])

                        def mm_pp(e, ppj=ppj, mh=mh, pT=pT):
                            for c in range(2):
                                ins = e.matmul(ppj[:, :], lhsT=pT[:, c * 128:(c + 1) * 128], rhs=wppb[:, c, mh * 512:(mh + 1) * 512], start=(c == 0), stop=(c == 1))
                            return ins
                        S.op("pe", mm_pp, reads=[pT.b, wppb.b], writes=[ppjb])
                        sig = sig_r.next()
                        S.op("act", lambda e, pgt=pgt, sig=sig: e.activation(out=sig[:], in_=pgt[:, :], func=AF.Sigmoid), reads=[pgtb], writes=[sig.b])
                        S.op("dve", lambda e, ppj=ppj, sig=sig: e.tensor_tensor(out=sig[:], in0=ppj[:, :], in1=sig[:], op=ALU.mult), reads=[ppjb, sig.b], writes=[sig.b])
                        S.op("pool", lambda e, sig=sig, yt=yt, k=k, mh=mh: e.tensor_tensor(out=yt[:, mh * 512:(mh + 1) * 512], in0=sig[:], in1=H[:, k, mh * 512:(mh + 1) * 512], op=ALU.add),
                             reads=[sig.b, B_H[k], yt.b], writes=[yt.b])
                    ydst = ys_d[:, :] if smp else y_d[ti * 128:(ti + 1) * 128, :]
                    S.dma("sp", lambda e, yt=yt, ydst=ydst: e.dma_start(out=ydst, in_=yt[:]), yt.b, reads=[yt.b])
            S.barrier()
        S.barrier()

        with nc.Block() as block:
            S.emit(block)
    return nc


CC_INC = 16
import os as _os
DBG = set(_os.environ.get("KDBG", "").split(","))
ASTOP = int(_os.environ.get("ASTOP", "99"))
ATILES = int(_os.environ.get("ATILES", "99"))
SSTOP = int(_os.environ.get("SSTOP", "99"))
SUB = int(_os.environ.get("SUB", "99"))


class _Stop(Exception):
    pass


def _chk(n):
    if SUB <= n:
        raise _Stop()
SBATCH = int(_os.environ.get("SBATCH", "32"))


class _View:
    def __init__(self, base, k, buf):
        self.base, self.k, self.b = base, k, buf
        self.shape = [128, D]

    def __getitem__(self, key):
        return self.base.t[:, self.k, :][key]


def _norm_core(S, src, junk, ss, rstd, xn, epsc):
    S.op("act", lambda e: e.activation(out=junk[:], in_=src[:, :], func=AF.Square, accum_out=ss[:]), reads=[src.b], writes=[junk.b, ss.b])
    S.op("act", lambda e: e.activation(out=rstd[:], in_=ss[:], func=AF.Ln, bias=epsc[:], scale=1.0 / D), reads=[ss.b, epsc.b], writes=[rstd.b])
    S.op("act", lambda e: e.activation(out=rstd[:], in_=rstd[:], func=AF.Exp, scale=-0.5), reads=[rstd.b], writes=[rstd.b])
    S.op("dve", lambda e: e.tensor_scalar(out=xn[:], in0=src[:, :], scalar1=rstd[:], scalar2=None, op0=ALU.mult), reads=[src.b, rstd.b], writes=[xn.b])


def _tr8(S, xn, identb, ps_next, PT_T):
    pt, pb = ps_next("T", PT_T)
    ptb = pt[:].bitcast(BF16)

    def tr(e):
        for k in range(KC):
            ins = e.transpose(out=ptb[:, k * 128:(k + 1) * 128], in_=xn[:, k * 128:(k + 1) * 128], identity=identb[:])
        return ins
    S.op("pe", tr, reads=[xn.b, identb.b], writes=[pb])
    return ptb, pb


def _norm_T_view(S, nc, src, junk, ss, rstd, xn, hnT, k, hbuf, ps_next, PT_T, identb, epsc):
    _norm_core(S, src, junk, ss, rstd, xn, epsc)
    ptb, pb = _tr8(S, xn, identb, ps_next, PT_T)
    S.op("dve", lambda e: e.tensor_copy(out=hnT[:, :, k * 128:(k + 1) * 128], in_=ptb[:, 0:KC * 128].rearrange("p (c t) -> p c t", c=KC)),
         reads=[pb], writes=[hbuf])


def _norm_T_flat(S, nc, src, junk, ss, rstd, xn, dst, ps_next, PT_T, identb, epsc):
    _norm_core(S, src, junk, ss, rstd, xn, epsc)
    ptb, pb = _tr8(S, xn, identb, ps_next, PT_T)
    S.op("act", lambda e: e.activation(out=dst[:], in_=ptb[:, 0:KC * 128], func=AF.Copy), reads=[pb], writes=[dst.b])


def make_in_maps(cfg, inp):
    NT, NB = cfg.NT, cfg.NB
    HP = cfg.POSH // 2
    f = lambda a: np.ascontiguousarray(a, dtype=np.float32)
    w_in = f(inp["w_in"][0]); w_o = f(inp["w_o"][0])
    wfi = inp["w_ffn_in"][0]
    wfi_r = f(wfi.reshape(8, 128, 2, NB, 128).transpose(3, 1, 0, 2, 4).reshape(NB, 128, 8 * 256))
    w_fo = f(inp["w_ffn_out"][0]); w_pg = f(inp["w_ple_gate"][0]); w_pp = f(inp["w_ple_proj"][0])
    lncols = f(np.concatenate([inp["ln1_w"][0].reshape(8, 128).T, inp["ln2_w"][0].reshape(8, 128).T,
                               inp["ln_ple_w"][0].reshape(8, 128).T], axis=1))
    sublncol = f(inp["diff_subln_w"][0].reshape(4, 128).T)
    vecs = f(np.concatenate([inp["q_norm_w"][0], inp["k_norm_w"][0], inp["lambda_q1"][0], inp["lambda_q2"][0],
                             inp["lambda_k1"][0], inp["lambda_k2"][0], inp["ret_gn_w"][0], inp["ret_gn_b"][0]])[None, :])
    w_sel = f(np.concatenate([w_in[:, 0:1536], w_in[:, 2048:2560]], axis=1))
    consts = make_consts(cfg)
    ck, cv = inp["cache_k"][0], inp["cache_v"][0]
    shared = dict(w_in=w_in, w_o=w_o, w_sel=w_sel, w_fi=wfi_r, w_fo=w_fo, w_pg=w_pg, w_pp=w_pp,
                  consts=consts, lncols=lncols, sublncol=sublncol, vecs=vecs)
    for r in range(8):
        h, j = r % 4, r // 4
        for hf in range(2):
            p0 = j * cfg.POSH + hf * HP
            shared["ck%d_%d" % (r, hf)] = f(ck[:, p0:p0 + HP, h, :].reshape(cfg.NPHYS, HP * 128))
            shared["cv%d_%d" % (r, hf)] = f(cv[:, p0:p0 + HP, h, :].reshape(cfg.NPHYS, HP * 128))
    maps = []
    for c in range(8):
        xs = np.zeros((128, D), np.float32)
        xs[0:16] = inp["x_sample"][4 * c:4 * c + 4].reshape(16, D)
        psm = np.zeros((128, 256), np.float32)
        psm[0:16] = inp["p_sample"][0, 4 * c:4 * c + 4].reshape(16, 256)
        m = dict(shared)
        m.update(x=f(inp["x_prompt"][c]), xs=xs, p=f(inp["p_prompt"][0, c]), psm=psm,
                 ptT=np.ascontiguousarray(inp["page_table"][4 * c:4 * c + 4].T.astype(np.int32)),
                 st_in=f(inp["state_ret"][0, 4 * c:4 * c + 4].reshape(16, 128, 128)))
        maps.append(m)
    return maps


def assemble(cfg, res):
    NT = cfg.NT
    y = np.stack([res[c]["y"].reshape(NT * 128, D) for c in range(8)])
    kp = np.stack([res[c]["ko"].reshape(NT * 128, 4, 128) for c in range(8)])[None]
    vp = np.stack([res[c]["vo"].reshape(NT * 128, 4, 128) for c in range(8)])[None]
    rp = np.stack([res[c]["reto"].reshape(4, 128, 128) for c in range(8)])[None]
    ysm = np.concatenate([res[c]["ys"][0:16].reshape(4, 4, D) for c in range(8)], axis=0)
    ks = np.concatenate([res[c]["kso"][0:16].reshape(4, 4, 4, 128) for c in range(8)], axis=0)[None]
    vs = np.concatenate([res[c]["vso"][0:16].reshape(4, 4, 4, 128) for c in range(8)], axis=0)[None]
    rs = np.concatenate([res[c]["retso"].reshape(4, 4, 128, 128) for c in range(8)], axis=0)[None]
    return tuple(np.ascontiguousarray(a, dtype=np.float32) for a in (y, ysm, kp, vp, rp, ks, vs, rs))


_NC_CACHE = {}


def kernel(**inputs):
    cfg = Cfg()
    inputs = {k: np.asarray(v) for k, v in inputs.items()}
    if "nc" not in _NC_CACHE:
        _NC_CACHE["nc"] = build(cfg)
    nc = _NC_CACHE["nc"]
    in_maps = make_in_maps(cfg, inputs)
    res = run_bass_kernel_spmd(nc, in_maps, core_ids=list(range(8)))
    return assemble(cfg, res.results)
```

```python
import math
from contextlib import ExitStack

import numpy as np
import concourse.bass as bass
import concourse.mybir as mybir
from concourse.bass_utils import run_bass_kernel_spmd

F32 = mybir.dt.float32
BF16 = mybir.dt.bfloat16
I32 = mybir.dt.int32
AF = mybir.ActivationFunctionType
ALU = mybir.AluOpType
AX = mybir.AxisListType

EPS = 1e-6
LAM_INIT = 0.8 - 0.6 * math.exp(-0.3 * 0)
D = 1024
KC = 8
PROJ = 3584


class Buf:
    __slots__ = ("name", "w", "r", "dsem", "psum")

    def __init__(self, name, psum=False):
        self.name = name
        self.w = None
        self.r = []
        self.dsem = None
        self.psum = psum


class _Rec:
    def __init__(self):
        self.calls = []

    def __getattr__(self, name):
        def f(*a, **k):
            self.calls.append((name, a, k))
            return None
        return f


def _record(fn):
    r = _Rec()
    fn(r)
    assert r.calls
    return r.calls


class Sched:
    COMPUTE = ("pe", "act", "dve", "pool")

    def __init__(self, nc, stack):
        self.nc = nc
        self.stack = stack
        self.lists = {e: [] for e in ("pe", "act", "dve", "pool", "sp")}
        self.sem = {e: stack.enter_context(nc.semaphore("s_" + e)) for e in self.COMPUTE}
        self.cnt = {e: 0 for e in self.COMPUTE}
        self.known = {e: {} for e in self.lists}
        self.dma_sems = {}
        self.nsem = 4

    def _need(self, eng, waits, tok):
        if tok is None:
            return
        sem, val = tok
        if eng == "pe" and sem is self.sem["pe"]:
            return
        if self.known[eng].get(id(sem), (None, 0))[1] >= val:
            return
        if id(sem) not in waits or waits[id(sem)][1] < val:
            waits[id(sem)] = (sem, val)

    def _waits(self, eng, reads, writes):
        waits = {}
        for b in reads:
            self._need(eng, waits, b.w)
            if b.psum:
                for t in b.r:
                    if t[0] is not self.sem.get(eng):
                        self._need(eng, waits, t)
        for b in writes:
            self._need(eng, waits, b.w)
            for t in b.r:
                self._need(eng, waits, t)
        for k, sv in waits.items():
            self.known[eng][k] = sv
        return list(waits.values())

    @staticmethod
    def _commit(tok, reads, writes):
        for b in reads:
            b.r.append(tok)
            if len(b.r) > 64:
                best = {}
                for s, v in b.r:
                    if id(s) not in best or best[id(s)][1] < v:
                        best[id(s)] = (s, v)
                b.r = list(best.values())
        for b in writes:
            b.w = tok
            b.r = []

    def op(self, eng, fn, reads=(), writes=()):
        waits = self._waits(eng, reads, writes)
        self.cnt[eng] += 1
        tok = (self.sem[eng], self.cnt[eng])
        self.lists[eng].append((waits, _record(fn), tok[0], 1))
        self._commit(tok, reads, writes)
        return tok

    def dma(self, q, fn, owner, reads=(), writes=(), inc=16):
        waits = self._waits(q, reads, writes)
        if owner.dsem is None:
            owner.dsem = {}
        if q not in owner.dsem:
            sem = self.stack.enter_context(self.nc.semaphore("d%s_%s" % (q, owner.name)))
            owner.dsem[q] = [sem, 0]
            self.dma_sems[id(sem)] = owner.dsem[q]
            self.nsem += 1
        owner.dsem[q][1] += inc
        tok = (owner.dsem[q][0], owner.dsem[q][1])
        self.lists[q].append((waits, _record(fn), tok[0], inc))
        self._commit(tok, reads, writes)
        return tok

    def barrier(self):
        toks = [(self.sem[e], self.cnt[e]) for e in self.COMPUTE if self.cnt[e] > 0]
        toks += [(s, c) for (s, c) in self.dma_sems.values()]
        for eng in self.lists:
            waits = {}
            for t in toks:
                self._need(eng, waits, t)
            for k, sv in waits.items():
                self.known[eng][k] = sv
            if waits:
                self.lists[eng].append((list(waits.values()), None, None, 0))

    def emit(self, block):
        def mk(name):
            lst = self.lists[name]

            def body(e):
                for waits, fn, sem, inc in lst:
                    for (s, v) in waits:
                        e.wait_ge(s, v)
                    if fn is not None:
                        for (mname, a, k) in fn:
                            ins = getattr(e, mname)(*a, **k)
                        ins.then_inc(sem, inc)
            return body

        block.tensor(mk("pe"))
        block.scalar(mk("act"))
        block.vector(mk("dve"))
        block.gpsimd(mk("pool"))
        block.sync(mk("sp"))


class T:
    def __init__(self, nc, stack, name, shape, dt):
        self.t = stack.enter_context(nc.sbuf_tensor("sb_" + name, list(shape), dt))
        self.b = Buf(name)
        self.shape = list(shape)

    def __getitem__(self, k):
        return self.t[k]


class Ring:
    def __init__(self, nc, stack, name, shape, dt, n):
        self.items = [T(nc, stack, "%s%d" % (name, i), shape, dt) for i in range(n)]
        self.i = 0

    def next(self):
        t = self.items[self.i % len(self.items)]
        self.i += 1
        return t


def mkap(a, dims):
    return bass.AP(a.tensor, a.offset, [[a.ap[0][0], a.ap[0][1]]] + [list(d) for d in dims])


class Cfg:
    def __init__(self, NT=16, DFF=2816, NPG=128, PAGE=128, NPHYS=5120, groups=None):
        self.NT, self.DFF, self.NPG, self.PAGE, self.NPHYS = NT, DFF, NPG, PAGE, NPHYS
        self.POSH = PAGE // 2
        self.NB = DFF // 128
        assert self.NB % 2 == 0
        self.NBH = self.NB // 2
        self.PAST = NPG * PAGE
        if groups is None:
            half = NT // 2
            groups = [list(range(half)), list(range(half, NT)) + [NT]]
        self.groups = groups
        off = {}
        c = 0
        for name, w in [("ident", 128), ("maskT", 128), ("maskS", 128), ("tabq", 4), ("tabk", 4),
                        ("tabq_s", 4), ("tabk_s", 4), ("gL", 4), ("abias", NT * 4), ("abias_s", 4),
                        ("alibi8", 8 * self.POSH), ("g4s", 16), ("mask16", 4), ("ones", 1)]:
            off[name] = (c, w)
            c += w
        self.coff = off
        self.NC = c
        self.CCW = 258
        self.ARW = 4 * 258 + 8 * 128


def make_consts(cfg):
    C = np.zeros((128, cfg.NC), np.float64)

    def put(name, arr):
        o, w = cfg.coff[name]
        C[:, o:o + w] = arr

    p = np.arange(128, dtype=np.float64)
    put("ident", np.eye(128))
    put("maskT", (p[:, None] <= p[None, :]).astype(np.float64))
    put("maskS", ((p[:, None] // 4 == p[None, :] // 4) & (p[:, None] % 4 <= p[None, :] % 4)).astype(np.float64))
    gam = 1.0 - np.exp2(-5.0 - np.arange(4))
    lg = np.log(gam)
    put("tabq", np.exp(lg[None, :] * (p[:, None] + 1.0)))
    put("tabk", np.exp(-lg[None, :] * (p[:, None] + 1.0)) * (128.0 ** -0.5))
    t4 = p % 4
    put("tabq_s", np.exp(lg[None, :] * (t4[:, None] + 1.0)))
    put("tabk_s", np.exp(-lg[None, :] * (t4[:, None] + 1.0)) * (128.0 ** -0.5))
    put("gL", np.tile(np.exp(lg * 128.0)[None, :], (128, 1)))
    slopes = np.exp2(-8.0 / 4 * np.arange(1, 5))
    ab = np.zeros((128, cfg.NT * 4))
    for dl in range(cfg.NT):
        for hh in range(4):
            ab[:, dl * 4 + hh] = slopes[hh] * (p - 127.0 - 128.0 * dl)
    put("abias", ab)
    put("abias_s", slopes[None, :] * t4[:, None])
    pos = np.arange(cfg.POSH, dtype=np.float64)
    al = np.zeros((128, 8 * cfg.POSH))
    for r in range(8):
        hh, jj = r % 4, r // 4
        kpos = p[:, None] * cfg.PAGE + jj * cfg.POSH + pos[None, :]
        al[:, r * cfg.POSH:(r + 1) * cfg.POSH] = 8.0 * slopes[hh] * (kpos - cfg.PAST)
    put("alibi8", al)
    g4s = np.zeros((128, 16))
    for sl in range(16):
        g4s[:, sl] = gam[sl % 4] ** 4
    put("g4s", g4s)
    m16 = np.zeros((128, 4))
    for t in range(16):
        m16[t, t // 4] = 1.0
    put("mask16", m16)
    put("ones", np.ones((128, 1)))
    return C.astype(np.float32)


def build(cfg):
    nc = bass.Bass("TRN2", target_bir_lowering=False)
    NT, DFF, NPG, PAGE, NPHYS = cfg.NT, cfg.DFF, cfg.NPG, cfg.PAGE, cfg.NPHYS
    POSH, NB, NBH, CCW = cfg.POSH, cfg.NB, cfg.NBH, cfg.CCW
    assert POSH * 8 <= 512 and POSH % 4 == 0

    def din(name, shape, dt=F32):
        return nc.dram_tensor(name, list(shape), dt, kind="ExternalInput").ap()

    def dout(name, shape):
        return nc.dram_tensor(name, list(shape), F32, kind="ExternalOutput").ap()

    x_d = din("x", [NT * 128, D]); xs_d = din("xs", [128, D])
    p_d = din("p", [NT * 128, 256]); psm_d = din("psm", [128, 256])
    HW = (POSH // 2) * 128
    ck_d = [[din("ck%d_%d" % (r, i), [NPHYS, HW]) for i in range(2)] for r in range(8)]
    cv_d = [[din("cv%d_%d" % (r, i), [NPHYS, HW]) for i in range(2)] for r in range(8)]
    ptT_d = din("ptT", [NPG, 4], I32)
    stin_d = din("st_in", [16, 128, 128])
    win_d = din("w_in", [D, PROJ]); wo_d = din("w_o", [D, D]); wsel_d = din("w_sel", [D, 2048])
    wfi_d = din("w_fi", [NB, 128, 8 * 256]); wfo_d = din("w_fo", [DFF, D])
    wpg_d = din("w_pg", [D, D]); wpp_d = din("w_pp", [256, D])
    consts_d = din("consts", [128, cfg.NC]); lncols_d = din("lncols", [128, 24])
    subln_d = din("sublncol", [128, 4]); vecs_d = din("vecs", [1, 1408])

    y_d = dout("y", [NT * 128, D]); ys_d = dout("ys", [128, D])
    ko_d = dout("ko", [NT * 128, 512]); vo_d = dout("vo", [NT * 128, 512])
    reto_d = dout("reto", [4, 128, 128]); kso_d = dout("kso", [128, 512]); vso_d = dout("vso", [128, 512])
    retso_d = dout("retso", [16, 128, 128])

    h1s_d = nc.dram_tensor("h1s", [(NT + 1) * 128, D], F32).ap()
    cctm_d = nc.dram_tensor("cc_tm", [16, 8 * 258], F32).ap()
    B_cctm = Buf("cctm")
    B_h1s = [Buf("h1s%d" % i) for i in range(NT + 1)]
    out_bufs = []

    with ExitStack() as gst:
        S = Sched(nc, gst)

        def gT(name, shape, dt):
            return T(nc, gst, name, shape, dt)

        PSB = []
        for k in range(8):
            t = gst.enter_context(nc.psum_tensor("psb%d" % k, [128, 512], F32))
            PSB.append((t, Buf("psb%d" % k, psum=True)))
        ps_ctr = {}

        def ps_next(pool, banks):
            i = ps_ctr.get(pool, 0)
            ps_ctr[pool] = i + 1
            return PSB[banks[i % len(banks)]]

        Cc = gT("consts", [128, cfg.NC], F32)
        lnc = gT("lncols", [128, 24], F32)
        subc = gT("sublncol", [128, 4], F32)
        vec = gT("vecs", [128, 1408], F32)
        S.dma("sp", lambda e: e.dma_start(out=Cc[:], in_=consts_d[:, :]), Cc.b, writes=[Cc.b])
        S.dma("sp", lambda e: e.dma_start(out=lnc[:], in_=lncols_d[:, :]), lnc.b, writes=[lnc.b])
        S.dma("sp", lambda e: e.dma_start(out=subc[:], in_=subln_d[:, :]), subc.b, writes=[subc.b])
        S.dma("sp", lambda e: e.dma_start(out=vec[:], in_=vecs_d[0, :].partition_broadcast(128)), vec.b, writes=[vec.b])

        def cs(name, rows=128):
            o, w = cfg.coff[name]
            return Cc[0:rows, o:o + w]

        V_QW, V_KW, V_L = 0, 64, 128
        V_GNW, V_GNB = 384, 896
        identb = gT("identb", [128, 128], BF16)
        S.op("dve", lambda e: e.tensor_copy(out=identb[:], in_=cs("ident")), reads=[Cc.b], writes=[identb.b])
        epsc = gT("epsc", [128, 1], F32)
        S.op("pool", lambda e: e.memset(epsc[:], EPS), writes=[epsc.b])

        lamt = gT("lamt", [128, 128], F32)
        lams = gT("lams", [128, 2], F32)
        lame = gT("lame", [128, 2], F32)
        nlam = gT("nlam", [128, 1], F32)
        S.op("dve", lambda e: e.tensor_tensor(
            out=lamt[:].rearrange("p (a b) -> p a b", a=2),
            in0=mkap(vec[:, 128:129], [[64, 2], [1, 64]]),
            in1=mkap(vec[:, 256:257], [[64, 2], [1, 64]]), op=ALU.mult),
            reads=[vec.b], writes=[lamt.b])
        S.op("dve", lambda e: e.tensor_reduce(out=lams[:], in_=lamt[:].rearrange("p (a b) -> p a b", a=2),
                                              axis=AX.X, op=ALU.add), reads=[lamt.b], writes=[lams.b])
        S.op("act", lambda e: e.activation(out=lame[:], in_=lams[:], func=AF.Exp), reads=[lams.b], writes=[lame.b])
        S.op("dve", lambda e: e.tensor_tensor(out=nlam[:], in0=lame[:, 1:2], in1=lame[:, 0:1], op=ALU.subtract),
             reads=[lame.b], writes=[nlam.b])
        S.op("dve", lambda e: e.tensor_scalar(out=nlam[:], in0=nlam[:], scalar1=-LAM_INIT, scalar2=None, op0=ALU.add),
             reads=[nlam.b], writes=[nlam.b])

        def rstd_from_ss(ss, rstd, n, inv_count, tag_reads=()):
            S.op("act", lambda e: e.activation(out=rstd[:], in_=ss[:], func=AF.Ln, bias=epsc[0:ss.shape[0], :],
                                               scale=inv_count), reads=[ss.b, epsc.b], writes=[rstd.b])
            S.op("act", lambda e: e.activation(out=rstd[:], in_=rstd[:], func=AF.Exp, scale=-0.5),
                 reads=[rstd.b], writes=[rstd.b])

        def norm_T(src, rows, junk, ss, rstd, xn, xT_dst, xT_buf, pool_T):
            S.op("act", lambda e: e.activation(out=junk[0:rows, :], in_=src[0:rows, :], func=AF.Square,
                                               accum_out=ss[0:rows, :]), reads=[src.b], writes=[junk.b, ss.b])
            S.op("act", lambda e: e.activation(out=rstd[0:rows, :], in_=ss[0:rows, :], func=AF.Ln,
                                               bias=epsc[0:rows, :], scale=1.0 / D),
                 reads=[ss.b, epsc.b], writes=[rstd.b])
            S.op("act", lambda e: e.activation(out=rstd[0:rows, :], in_=rstd[0:rows, :], func=AF.Exp, scale=-0.5),
                 reads=[rstd.b], writes=[rstd.b])
            S.op("dve", lambda e: e.tensor_scalar(out=xn[0:rows, :], in0=src[0:rows, :], scalar1=rstd[0:rows, :],
                                                  scalar2=None, op0=ALU.mult),
                 reads=[src.b, rstd.b], writes=[xn.b])
            pt, pb = ps_next("T", pool_T)
            ptb = pt[:].bitcast(BF16)

            def tr(e):
                for k in range(KC):
                    ins = e.transpose(out=ptb[:, k * rows:(k + 1) * rows], in_=xn[0:rows, k * 128:(k + 1) * 128],
                                      identity=identb[0:rows, 0:rows])
                return ins
            S.op("pe", tr, reads=[xn.b, identb.b], writes=[pb])
            S.op("dve", lambda e: e.tensor_copy(out=xT_dst(None), in_=ptb[:, 0:KC * rows]), reads=[pb], writes=[xT_buf])

        def cast_w(eng, dst_ap, src_ap, col_ap, rd, wr, extra=None):
            if extra is None:
                S.op(eng, lambda e: e.tensor_tensor(out=dst_ap, in0=src_ap, in1=col_ap, op=ALU.mult), reads=rd, writes=wr)
            else:
                S.op(eng, lambda e: e.scalar_tensor_tensor(out=dst_ap, in0=src_ap, scalar=extra, in1=col_ap,
                                                           op0=ALU.mult, op1=ALU.mult), reads=rd, writes=wr)

        cross_sb = gT("cross_sb", [128, 512], F32)
        pps_sb = gT("pps_sb", [128, 4, 258], F32)
        S.op("pool", lambda e: e.memset(pps_sb[:], 0.0), writes=[pps_sb.b])
        with ExitStack() as st:
            def sT(name, shape, dt):
                return T(nc, st, name, shape, dt)
            PT_T = [0, 1]; PT_M = [2, 3]; PT_SC = [4, 5]; PT_OA = [6, 7]
            HP = POSH // 2

            xs_t = sT("s_xs", [128, D], F32)
            S.dma("sp", lambda e: e.dma_start(out=xs_t[:], in_=xs_d[:, :]), xs_t.b, writes=[xs_t.b])
            ptT = sT("s_ptT", [NPG, 4], I32)
            S.dma("sp", lambda e: e.dma_start(out=ptT[:], in_=ptT_d[:, :]), ptT.b, writes=[ptT.b])
            st32 = sT("s_st32", [128, 16, 128], F32)
            S.dma("sp", lambda e: e.dma_start(out=st32[:], in_=stin_d.rearrange("b d e -> d b e")), st32.b, writes=[st32.b])
            stbf = sT("s_stbf", [128, 16, 128], BF16)
            S.op("pool", lambda e: e.tensor_copy(out=stbf[:], in_=st32[:]), reads=[st32.b], writes=[stbf.b])
            junk = sT("s_junk", [128, D], BF16)
            ss = sT("s_ss", [128, 1], F32); rstd = sT("s_rstd", [128, 1], F32)
            xn = sT("s_xn", [128, D], BF16)
            xsT = sT("s_xsT", [128, KC, 128], BF16)
            norm_T(xs_t, 128, junk, ss, rstd, xn, lambda _: xsT[:].rearrange("p k t -> p (k t)"), xsT.b, PT_T)

            rqo = sT("s_rqo", [128, 512], BF16); rko = sT("s_rko", [128, 512], BF16); rvo = sT("s_rvo", [128, 512], BF16)
            qn = sT("s_qn", [128, 512], BF16)
            sq = sT("s_sq", [128, 512], F32); qt = sT("s_qt", [128, 512], F32)
            ssq = sT("s_ssq", [128, 8], F32); rs8 = sT("s_rs8", [128, 8], F32)
            with ExitStack() as st_w:
                wst_r = Ring(nc, st_w, "s_wst", [128, 8, 512], F32, 2)
                wpc_r = Ring(nc, st_w, "s_wpc", [128, 8, 512], BF16, 2)
                for pc in range(4):
                    wst = wst_r.next(); wpc = wpc_r.next()
                    S.dma("sp", lambda e, wst=wst, pc=pc: e.dma_start(
                        out=wst[:], in_=wsel_d[:, pc * 512:(pc + 1) * 512].rearrange("(c p) n -> p c n", p=128)),
                        wst.b, writes=[wst.b])
                    cast_w("pool" if pc % 2 == 0 else "dve", wpc[:], wst[:], mkap(lnc[:, 0:1], [[1, 8], [0, 512]]), [wst.b, lnc.b], [wpc.b])
                    pm, pmb = ps_next("M", PT_M)

                    def mm_p(e, pm=pm, wpc=wpc):
                        for k in range(KC):
                            ins = e.matmul(pm[:, :], lhsT=xsT[:, k, :], rhs=wpc[:, k, :], start=(k == 0), stop=(k == KC - 1))
                        return ins
                    S.op("pe", mm_p, reads=[xsT.b, wpc.b], writes=[pmb])
                    if pc == 0 or pc == 1:
                        dst = rqo if pc == 0 else rko
                        tab = cs("tabq_s") if pc == 0 else cs("tabk_s")
                        S.op("dve", lambda e, pm=pm, dst=dst, tab=tab: e.tensor_tensor(
                            out=dst[:].rearrange("p (a b) -> p a b", a=4), in0=pm[:, :].rearrange("p (a b) -> p a b", a=4),
                            in1=mkap(tab[:, 0:1], [[1, 4], [0, 128]]), op=ALU.mult), reads=[pmb, Cc.b], writes=[dst.b])
                    elif pc == 2:
                        S.op("act", lambda e, pm=pm: e.activation(out=rvo[:], in_=pm[:, :], func=AF.Copy), reads=[pmb], writes=[rvo.b])
                    else:
                        S.op("act", lambda e, pm=pm: e.activation(out=sq[:], in_=pm[:, :], func=AF.Square), reads=[pmb], writes=[sq.b])
                        S.op("dve", lambda e: e.tensor_reduce(out=ssq[:], in_=sq[:].rearrange("p (a b) -> p a b", a=8), axis=AX.X, op=ALU.add),
                             reads=[sq.b], writes=[ssq.b])
                        S.op("act", lambda e: e.activation(out=rs8[:], in_=ssq[:], func=AF.Ln, bias=epsc[:], scale=1.0 / 64), reads=[ssq.b, epsc.b], writes=[rs8.b])
                        S.op("act", lambda e: e.activation(out=rs8[:], in_=rs8[:], func=AF.Exp, scale=-0.5), reads=[rs8.b], writes=[rs8.b])
                        S.op("dve", lambda e, pm=pm: e.tensor_tensor(out=qt[:].rearrange("p (a b) -> p a b", a=8), in0=pm[:, :].rearrange("p (a b) -> p a b", a=8),
                                                                    in1=mkap(rs8[:, 0:1], [[1, 8], [0, 64]]), op=ALU.mult), reads=[pmb, rs8.b], writes=[qt.b])
                        S.op("dve", lambda e: e.tensor_tensor(out=qn[:].rearrange("p (a b) -> p a b", a=8), in0=qt[:].rearrange("p (a b) -> p a b", a=8),
                                                              in1=mkap(vec[:, V_QW:V_QW + 1], [[0, 8], [1, 64]]), op=ALU.mult), reads=[qt.b, vec.b], writes=[qn.b])
                S.barrier()

            pt, ptb_ = ps_next("T", PT_T)
            ptv = pt[:].bitcast(BF16)

            def trq(e, ptv=ptv):
                for hh in range(4):
                    ins = e.transpose(out=ptv[:, hh * 128:(hh + 1) * 128], in_=qn[:, hh * 128:(hh + 1) * 128], identity=identb[:])
                return ins
            S.op("pe", trq, reads=[qn.b, identb.b], writes=[ptb_])
            qbd = sT("s_qbd", [128, 4 * 4 * 8], BF16)
            S.op("pool", lambda e: e.memset(qbd[:], 0.0), writes=[qbd.b])
            S.op("dve", lambda e, ptv=ptv: e.tensor_copy(out=mkap(qbd[0:64, 0:1], [[32, 4], [8, 4], [1, 4]]),
                                                         in_=mkap(ptv[0:64, 0:1], [[128, 4], [4, 4], [1, 4]])),
                 reads=[ptb_, qbd.b], writes=[qbd.b])
            S.op("dve", lambda e, ptv=ptv: e.tensor_copy(out=mkap(qbd[64:128, 4:5], [[32, 4], [8, 4], [1, 4]]),
                                                         in_=mkap(ptv[64:128, 0:1], [[128, 4], [4, 4], [1, 4]])),
                 reads=[ptb_, qbd.b], writes=[qbd.b])

            pt, ptb_ = ps_next("T", PT_T)
            ptv = pt[:].bitcast(BF16)

            def trr(e, ptv=ptv):
                for hh in range(4):
                    ins = e.transpose(out=ptv[:, hh * 128:(hh + 1) * 128], in_=rqo[:, hh * 128:(hh + 1) * 128], identity=identb[:])
                return ins
            S.op("pe", trr, reads=[rqo.b, identb.b], writes=[ptb_])
            qm = sT("s_qm", [128, 16 * 128], BF16)
            S.op("pool", lambda e: e.memset(qm[:], 0.0), writes=[qm.b])
            S.op("dve", lambda e, ptv=ptv: e.tensor_copy(out=mkap(qm[:, 0:1], [[128, 4], [516, 4], [1, 4]]),
                                                         in_=mkap(ptv[:, 0:1], [[128, 4], [4, 4], [1, 4]])),
                 reads=[ptb_, qm.b], writes=[qm.b])
            km = sT("s_km", [128, 16 * 128], BF16)
            S.op("dve", lambda e: e.tensor_tensor(out=km[:].rearrange("p (b h d) -> p b h d", b=4, h=4),
                                                  in0=mkap(rko[:, 0:1], [[0, 4], [128, 4], [1, 128]]),
                                                  in1=mkap(cs("mask16")[:, 0:1], [[1, 4], [0, 4], [0, 128]]), op=ALU.mult),
                 reads=[rko.b, Cc.b], writes=[km.b])
            pm, pmb = ps_next("M", PT_M)

            def mm_cross(e, pm=pm):
                for hh in range(4):
                    for bl in range(4):
                        sl = bl * 4 + hh
                        ins = e.matmul(pm[:, hh * 128:(hh + 1) * 128], lhsT=qm[:, sl * 128:(sl + 1) * 128], rhs=stbf[:, sl, :],
                                       start=(bl == 0), stop=(bl == 3))
                return ins
            S.op("pe", mm_cross, reads=[qm.b, stbf.b], writes=[pmb])
            S.op("act", lambda e, pm=pm: e.activation(out=cross_sb[:], in_=pm[:, :], func=AF.Copy), reads=[pmb], writes=[cross_sb.b])
            stnew = sT("s_stnew", [128, 16, 128], F32)
            for g4 in range(4):
                pm, pmb = ps_next("M", PT_M)

                def mm_kv(e, g4=g4, pm=pm):
                    for bb in range(4):
                        sl = g4 * 4 + bb
                        ins = e.matmul(pm[:, bb * 128:(bb + 1) * 128], lhsT=km[:, sl * 128:(sl + 1) * 128], rhs=rvo[:, bb * 128:(bb + 1) * 128],
                                       start=True, stop=True)
                    return ins
                S.op("pe", mm_kv, reads=[km.b, rvo.b], writes=[pmb])
                S.op("dve", lambda e, g4=g4, pm=pm: e.tensor_tensor(
                    out=stnew[:, g4 * 4:(g4 + 1) * 4, :].rearrange("p a b -> p (a b)"), in0=pm[:, :],
                    in1=st32[:, g4 * 4:(g4 + 1) * 4, :].rearrange("p a b -> p (a b)"), op=ALU.add),
                    reads=[pmb, st32.b, stnew.b], writes=[stnew.b])
            S.op("pool", lambda e: e.tensor_tensor(out=stnew[:], in0=stnew[:], in1=mkap(cs("g4s")[:, 0:1], [[1, 16], [0, 128]]), op=ALU.mult),
                 reads=[stnew.b, Cc.b], writes=[stnew.b])
            S.dma("sp", lambda e: e.dma_start(out=retso_d.rearrange("b d e -> d b e"), in_=stnew[:]), stnew.b, reads=[stnew.b])

            KG = Ring(nc, st, "s_kg", [NPG, HW], F32, 2)
            VG = Ring(nc, st, "s_vg", [NPG, HW], F32, 1)
            VB = Ring(nc, st, "s_vb", [NPG, POSH * 128], BF16, 2)
            B_vbh = [[Buf("vbh%d_%d" % (i, hf)) for hf in range(2)] for i in range(2)]
            KTr = Ring(nc, st, "s_kt", [128, 4 * NPG], BF16, 3)
            SBt = Ring(nc, st, "s_sb", [NPG, POSH * 8], F32, 2)
            ATr = Ring(nc, st, "s_at", [NPG, POSH * 8], BF16, 2)
            Rr = Ring(nc, st, "s_r", [NPG, 8], F32, 2)
            stall = sT("s_stall", [8, 32, 130], F32)
            evi = 0
            for it in range(32):
                r8, bl = it // 4, it % 4
                hh = r8 % 4
                alib = cs("alibi8", NPG)[:, r8 * POSH:(r8 + 1) * POSH]
                qb = qbd[:, (hh * 4 + bl) * 8:(hh * 4 + bl + 1) * 8]
                vb = VB.next()
                vbh = B_vbh[it % 2]
                psc, pscb = ps_next("SC", PT_SC)
                kgs = []
                for hf in range(2):
                    kg = KG.next(); vg = VG.next()
                    kgs.append(kg)
                    S.dma("pool", lambda e, kg=kg, bl=bl, hf=hf, r8=r8: e.indirect_dma_start(
                        out=kg[:], out_offset=None, in_=ck_d[r8][hf][:, :],
                        in_offset=bass.IndirectOffsetOnAxis(ap=ptT[:, bl:bl + 1], axis=0)), kg.b, reads=[ptT.b], writes=[kg.b])
                    S.dma("pool", lambda e, vg=vg, bl=bl, hf=hf, r8=r8: e.indirect_dma_start(
                        out=vg[:], out_offset=None, in_=cv_d[r8][hf][:, :],
                        in_offset=bass.IndirectOffsetOnAxis(ap=ptT[:, bl:bl + 1], axis=0)), vg.b, reads=[ptT.b], writes=[vg.b])
                    S.op("act", lambda e, vb=vb, vg=vg, hf=hf: e.activation(out=vb[:, hf * HW:(hf + 1) * HW], in_=vg[:], func=AF.Copy),
                         reads=[vg.b, vbh[hf]], writes=[vbh[hf]])
                steps = [(hf, pg) for hf in range(2) for pg in range(HP // 4)]

                def do_tr(step):
                    nonlocal evi
                    hf, pg = step
                    kg = kgs[hf]
                    ptr, ptrb = ps_next("T", PT_T)

                    def trk(e, pg=pg, kg=kg, ptr=ptr):
                        for k in range(4):
                            pl = pg * 4 + k
                            ins = e.transpose(out=ptr[:, k * NPG:(k + 1) * NPG], in_=kg[:, pl * 128:(pl + 1) * 128],
                                              identity=cs("ident", NPG)[:, 0:NPG])
                        return ins
                    S.op("pe", trk, reads=[kg.b, Cc.b], writes=[ptrb])
                    kt = KTr.next()
                    if evi % 2 == 0:
                        S.op("dve", lambda e, kt=kt, ptr=ptr: e.tensor_copy(out=kt[:], in_=ptr[:, 0:4 * NPG]), reads=[ptrb], writes=[kt.b])
                    else:
                        S.op("act", lambda e, kt=kt, ptr=ptr: e.activation(out=kt[:], in_=ptr[:, 0:4 * NPG], func=AF.Copy), reads=[ptrb], writes=[kt.b])
                    evi += 1
                    return kt
                kt_next = do_tr(steps[0])
                for si, (hf, pg) in enumerate(steps):
                    kt = kt_next
                    if si + 1 < len(steps):
                        kt_next = do_tr(steps[si + 1])

                    def mms(e, pg=pg, kt=kt, psc=psc, qb=qb, hf=hf):
                        for k in range(4):
                            pos = hf * HP + pg * 4 + k
                            ins = e.matmul(psc[0:NPG, pos * 8:(pos + 1) * 8], lhsT=kt[:, k * NPG:(k + 1) * NPG], rhs=qb,
                                           start=True, stop=True)
                        return ins
                    S.op("pe", mms, reads=[kt.b, qbd.b], writes=[pscb])
                sbt = SBt.next(); at = ATr.next(); rr = Rr.next()
                S.op("dve", lambda e, sbt=sbt, psc=psc, alib=alib: e.tensor_tensor(
                    out=sbt[:].rearrange("p (a b) -> p a b", b=8), in0=psc[0:NPG, 0:POSH * 8].rearrange("p (a b) -> p a b", b=8),
                    in1=mkap(alib[:, 0:1], [[1, POSH], [0, 8]]), op=ALU.add), reads=[pscb, Cc.b], writes=[sbt.b])
                S.op("act", lambda e, sbt=sbt, at=at: e.activation(out=at[:], in_=sbt[:], func=AF.Exp, scale=0.125), reads=[sbt.b], writes=[at.b])
                S.op("dve", lambda e, at=at, rr=rr: e.tensor_reduce(out=rr[:], in_=at[:].rearrange("p (a b) -> p b a", b=8), axis=AX.X, op=ALU.add),
                     reads=[at.b], writes=[rr.b])
                poa, poab = ps_next("OA", PT_OA)

                def mmo(e, at=at, vb=vb, poa=poa, rr=rr):
                    for pos in range(POSH):
                        e.matmul(poa[0:8, 0:128], lhsT=at[:, pos * 8:(pos + 1) * 8], rhs=vb[:, pos * 128:(pos + 1) * 128],
                                 start=(pos == 0), stop=(pos == POSH - 1))
                    return e.matmul(poa[0:8, 128:129], lhsT=rr[:], rhs=cs("ones", NPG), start=True, stop=True)
                S.op("pe", mmo, reads=[at.b, vbh[0], vbh[1], rr.b, Cc.b], writes=[poab])
                S.op("dve", lambda e, poa=poa, it=it: e.tensor_copy(out=stall[:, it, 0:129], in_=poa[0:8, 0:129]), reads=[poab, stall.b], writes=[stall.b])
            for c2 in range(2):
                dst = bass.AP(cctm_d.tensor, cctm_d.offset + c2 * 129, [[8 * 258, 4], [258, 8], [4 * 8 * 258, 4], [1, 129]])
                S.dma("sp", lambda e, c2=c2, dst=dst: e.dma_start(
                    out=dst, in_=stall[c2 * 4:(c2 + 1) * 4, :, 0:129].rearrange("p (r b) e -> p r b e", r=8)), stall.b,
                    reads=[stall.b], writes=[B_cctm])
            tokmaj = sT("s_tokmaj", [16, 8, 258], F32)
            S.dma("sp", lambda e: e.dma_start(out=tokmaj[:].rearrange("p a b -> p (a b)"), in_=cctm_d[:, :]), tokmaj.b, reads=[B_cctm], writes=[tokmaj.b])
            S.op("dve", lambda e: e.tensor_tensor(out=pps_sb[0:16, :, :], in0=tokmaj[:, 0:4, :], in1=tokmaj[:, 4:8, :], op=ALU.add),
                 reads=[tokmaj.b, pps_sb.b], writes=[pps_sb.b])
            S.barrier()

        with ExitStack() as st:
          if "skipA" not in DBG:
            def sT(name, shape, dt):
                return T(nc, st, name, shape, dt)
            PT_T = [0, 1]; PT_M = [2, 3]; PT_DS = [4, 5]; PT_DA = [6, 7]

            winb = sT("a_winb", [128, KC, PROJ], BF16)
            wob = sT("a_wob", [128, KC, D], BF16)
            with ExitStack() as st_w:
                stg = Ring(nc, st_w, "a_stg", [128, PROJ], F32, 2)
                for c in range(KC):
                    sg_ = stg.next()
                    S.dma("sp", lambda e, sg_=sg_, c=c: e.dma_start(out=sg_[:], in_=win_d[c * 128:(c + 1) * 128, :]), sg_.b, writes=[sg_.b])
                    for (eng, c0, c1) in (("pool", 0, 1024), ("dve", 1024, 2304), ("act", 2304, PROJ)):
                        if eng == "act":
                            S.op(eng, lambda e, sg_=sg_, c=c, c0=c0, c1=c1: e.activation(out=winb[:, c, c0:c1], in_=sg_[:, c0:c1], func=AF.Copy, scale=lnc[:, c:c + 1]),
                                 reads=[sg_.b, lnc.b, winb.b], writes=[winb.b])
                        else:
                            S.op(eng, lambda e, sg_=sg_, c=c, c0=c0, c1=c1: e.tensor_scalar(out=winb[:, c, c0:c1], in0=sg_[:, c0:c1], scalar1=lnc[:, c:c + 1], scalar2=None, op0=ALU.mult),
                                 reads=[sg_.b, lnc.b, winb.b], writes=[winb.b])
                for c in range(KC):
                    sg_ = stg.next()
                    S.dma("sp", lambda e, sg_=sg_, c=c: e.dma_start(out=sg_[:, 0:D], in_=wo_d[c * 128:(c + 1) * 128, :]), sg_.b, writes=[sg_.b])
                    eng = "pool" if c % 2 == 0 else "dve"
                    if c < 4:
                        S.op(eng, lambda e, sg_=sg_, c=c: e.tensor_copy(out=wob[:, c, :], in_=sg_[:, 0:D]), reads=[sg_.b, wob.b], writes=[wob.b])
                    else:
                        S.op(eng, lambda e, sg_=sg_, c=c: e.tensor_scalar(out=wob[:, c, :], in0=sg_[:, 0:D], scalar1=subc[:, c - 4:c - 3],
                                                                          scalar2=1.0 - LAM_INIT, op0=ALU.mult, op1=ALU.mult),
                             reads=[sg_.b, subc.b, wob.b], writes=[wob.b])
                S.barrier()

            dkT_all = sT("a_dkT", [128, 4, (NT + 1) * 128], BF16)
            vaug = sT("a_vaug", [128, NT + 1, 4 * 130], BF16)
            S.op("pool", lambda e: e.memset(vaug[:], 1.0), writes=[vaug.b])
            B_dkT = [Buf("dkT%d" % i) for i in range(NT + 1)]
            B_va = [Buf("va%d" % i) for i in range(NT + 1)]
            for bb in B_va:
                bb.w = vaug.b.w
            state32 = sT("a_state32", [128, 512], F32)
            statebf = sT("a_statebf", [128, 512], BF16)

            xt_r = Ring(nc, st, "a_xt", [128, D], F32, 3)
            junk = sT("a_junk", [128, D], BF16)
            ss_r = Ring(nc, st, "a_ss", [128, 1], F32, 2); rstd_r = Ring(nc, st, "a_rstd", [128, 1], F32, 2)
            xn_r = Ring(nc, st, "a_xn", [128, D], BF16, 1)
            xnT_r = Ring(nc, st, "a_xnT", [128, KC * 128], BF16, 2)
            rq_r = Ring(nc, st, "a_rq", [128, 512], BF16, 2); rk_r = Ring(nc, st, "a_rk", [128, 512], BF16, 2)
            rv_r = Ring(nc, st, "a_rv", [128, 512], BF16, 2); sg_r = Ring(nc, st, "a_sg", [128, 512], BF16, 2)
            tmp_r = Ring(nc, st, "a_tmp", [128, 512], F32, 3)
            sq_r = qt_r = ro_r = rsq_r = tmp_r
            ssq_r = Ring(nc, st, "a_ssq", [128, 8], F32, 2); rs8_r = Ring(nc, st, "a_rs8", [128, 8], F32, 2)
            qn_r = Ring(nc, st, "a_qn", [128, 512], BF16, 2); kn_r = Ring(nc, st, "a_kn", [128, 512], BF16, 2)
            k32_r = Ring(nc, st, "a_k32", [128, 512], F32, 1); v32_r = Ring(nc, st, "a_v32", [128, 512], F32, 1)
            rqT_r = Ring(nc, st, "a_rqT", [128, 512], BF16, 2); rkT_r = Ring(nc, st, "a_rkT", [128, 512], BF16, 2)
            dqT_r = Ring(nc, st, "a_dqT", [128, 1024], BF16, 2)
            for _t in dqT_r.items:
                S.op("pool", lambda e, _t=_t: e.memset(_t[:], 0.0), writes=[_t.b])
            sm_r = Ring(nc, st, "a_sm", [128, 512], BF16, 1)
            st4_r = Ring(nc, st, "a_st4", [128, 16], F32, 2)
            mixed_r = Ring(nc, st, "a_mixed", [128, D], BF16, 1)
            mixT_r = Ring(nc, st, "a_mixT", [128, KC * 128], BF16, 1)
            at_r = Ring(nc, st, "a_at", [128, 256], BF16, 4)
            acc_r = Ring(nc, st, "a_acc", [128, 2 * 130], F32, 2)
            fin_r = Ring(nc, st, "a_fin", [128, 8], F32, 2)
            o1_r = Ring(nc, st, "a_o1", [128, 128], F32, 2); dd_r = Ring(nc, st, "a_dd", [128, 128], F32, 2)
            jk_r = Ring(nc, st, "a_jk", [128, 128], BF16, 2)

            def attn_tile(i):
                smp = (i == NT)
                xt = xt_r.next()
                src = xs_d[:, :] if smp else x_d[i * 128:(i + 1) * 128, :]
                S.dma("sp", lambda e: e.dma_start(out=xt[:], in_=src), xt.b, writes=[xt.b])
                ss = ss_r.next(); rstd = rstd_r.next(); xn = xn_r.next(); xnT = xnT_r.next()
                norm_T(xt, 128, junk, ss, rstd, xn, lambda _: xnT[:], xnT.b, PT_T)
                if ASTOP <= 1:
                    return
                tq = cs("tabq_s") if smp else cs("tabq")
                tk = cs("tabk_s") if smp else cs("tabk")
                rq = rq_r.next(); rk = rk_r.next(); rv = rv_r.next(); sg = sg_r.next()
                qn = qn_r.next(); kn = kn_r.next(); k32 = k32_r.next(); v32 = v32_r.next()
                for n in range(7):
                    pm, pmb = ps_next("M", PT_M)

                    def mm(e, n=n, pm=pm):
                        for k in range(KC):
                            ins = e.matmul(pm[:, :], lhsT=xnT[:, k * 128:(k + 1) * 128], rhs=winb[:, k, n * 512:(n + 1) * 512],
                                           start=(k == 0), stop=(k == KC - 1))
                        return ins
                    S.op("pe", mm, reads=[xnT.b, winb.b], writes=[pmb])
                    if n == 0 or n == 1:
                        dst = rq if n == 0 else rk
                        tab = tq if n == 0 else tk
                        S.op("dve", lambda e, pm=pm, dst=dst, tab=tab: e.tensor_tensor(
                            out=dst[:].rearrange("p (a b) -> p a b", a=4), in0=pm[:, :].rearrange("p (a b) -> p a b", a=4),
                            in1=mkap(tab[:, 0:1], [[1, 4], [0, 128]]), op=ALU.mult), reads=[pmb, Cc.b], writes=[dst.b])
                    elif n == 2:
                        S.op("act", lambda e, pm=pm: e.activation(out=rv[:], in_=pm[:, :], func=AF.Copy), reads=[pmb], writes=[rv.b])
                    elif n == 3:
                        S.op("act", lambda e, pm=pm: e.activation(out=sg[:], in_=pm[:, :], func=AF.Silu), reads=[pmb], writes=[sg.b])
                    elif n == 4 or n == 5:
                        sq = sq_r.next(); ssq = ssq_r.next(); rs8 = rs8_r.next(); qt = qt_r.next()
                        S.op("act", lambda e, pm=pm, sq=sq: e.activation(out=sq[:], in_=pm[:, :], func=AF.Square), reads=[pmb], writes=[sq.b])
                        S.op("dve", lambda e, sq=sq, ssq=ssq: e.tensor_reduce(out=ssq[:], in_=sq[:].rearrange("p (a b) -> p a b", a=8),
                                                                              axis=AX.X, op=ALU.add), reads=[sq.b], writes=[ssq.b])
                        S.op("act", lambda e, ssq=ssq, rs8=rs8: e.activation(out=rs8[:], in_=ssq[:], func=AF.Ln, bias=epsc[:], scale=1.0 / 64),
                             reads=[ssq.b, epsc.b], writes=[rs8.b])
                        S.op("act", lambda e, rs8=rs8: e.activation(out=rs8[:], in_=rs8[:], func=AF.Exp, scale=-0.5), reads=[rs8.b], writes=[rs8.b])
                        S.op("dve", lambda e, pm=pm, qt=qt, rs8=rs8: e.tensor_tensor(
                            out=qt[:].rearrange("p (a b) -> p a b", a=8), in0=pm[:, :].rearrange("p (a b) -> p a b", a=8),
                            in1=mkap(rs8[:, 0:1], [[1, 8], [0, 64]]), op=ALU.mult), reads=[pmb, rs8.b], writes=[qt.b])
                        voff = V_QW if n == 4 else V_KW
                        wv = mkap(vec[:, voff:voff + 1], [[0, 8], [1, 64]])
                        if n == 4:
                            S.op("pool", lambda e, qt=qt, wv=wv: e.tensor_tensor(out=qn[:].rearrange("p (a b) -> p a b", a=8),
                                                                                 in0=qt[:].rearrange("p (a b) -> p a b", a=8), in1=wv, op=ALU.mult),
                                 reads=[qt.b, vec.b], writes=[qn.b])
                        else:
                            S.op("pool", lambda e, qt=qt, wv=wv: e.tensor_tensor(out=k32[:].rearrange("p (a b) -> p a b", a=8),
                                                                                 in0=qt[:].rearrange("p (a b) -> p a b", a=8), in1=wv, op=ALU.mult),
                                 reads=[qt.b, vec.b], writes=[k32.b])
                            S.op("pool", lambda e: e.tensor_copy(out=kn[:], in_=k32[:]), reads=[k32.b], writes=[kn.b])
                            kdst = kso_d[:, :] if smp else ko_d[i * 128:(i + 1) * 128, :]
                            S.dma("sp", lambda e, kdst=kdst: e.dma_start(out=kdst, in_=k32[:]), k32.b, reads=[k32.b])
                    else:
                        S.op("act", lambda e, pm=pm: e.activation(out=v32[:], in_=pm[:, :], func=AF.Copy), reads=[pmb], writes=[v32.b])
                        S.op("pool", lambda e: e.tensor_copy(out=vaug[:, i, :].rearrange("p (a b) -> p a b", a=4)[:, :, 0:128],
                                                             in_=v32[:].rearrange("p (a b) -> p a b", a=4)),
                             reads=[v32.b, B_va[i]], writes=[B_va[i]])
                        vdst = vso_d[:, :] if smp else vo_d[i * 128:(i + 1) * 128, :]
                        S.dma("sp", lambda e, vdst=vdst: e.dma_start(out=vdst, in_=v32[:]), v32.b, reads=[v32.b])
                if ASTOP <= 2:
                    return
                rqT = rqT_r.next(); rkT = rkT_r.next(); dqT = dqT_r.next()
                for (srcT, dst_ap, dst_b) in ((rq, rqT[:], rqT.b), (rk, rkT[:], rkT.b), (qn, "dq", dqT.b),
                                              (kn, None, B_dkT[i])):
                    pt, ptb_ = ps_next("T", PT_T)
                    ptv = pt[:].bitcast(BF16)

                    def tr4(e, srcT=srcT, ptv=ptv):
                        for hh in range(4):
                            ins = e.transpose(out=ptv[:, hh * 128:(hh + 1) * 128], in_=srcT[:, hh * 128:(hh + 1) * 128], identity=identb[:])
                        return ins
                    S.op("pe", tr4, reads=[srcT.b, identb.b], writes=[ptb_])
                    if dst_ap is None:
                        S.op("dve", lambda e, ptv=ptv: e.tensor_copy(out=dkT_all[:, :, i * 128:(i + 1) * 128],
                                                                     in_=ptv[:, 0:512].rearrange("p (a b) -> p a b", a=4)),
                             reads=[ptb_], writes=[dst_b])
                    elif dst_ap == "dq":
                        dq4 = dqT[:].rearrange("p (h c q) -> p h c q", h=4, c=2)
                        S.op("act", lambda e, ptv=ptv, dq4=dq4: e.activation(out=dq4[0:64, :, 0, :], in_=ptv[0:64, 0:512].rearrange("p (h q) -> p h q", h=4), func=AF.Copy),
                             reads=[ptb_, dst_b], writes=[dst_b])
                        S.op("dve", lambda e, ptv=ptv, dq4=dq4: e.tensor_copy(out=dq4[64:128, :, 1, :], in_=ptv[64:128, 0:512].rearrange("p (h q) -> p h q", h=4)),
                             reads=[ptb_, dst_b], writes=[dst_b])
                    else:
                        S.op("act", lambda e, ptv=ptv, dst_ap=dst_ap: e.activation(out=dst_ap, in_=ptv[:, 0:512], func=AF.Copy),
                             reads=[ptb_], writes=[dst_b])
                if ASTOP <= 3:
                    return
                pm, pmb = ps_next("M", PT_M)

                def mm_s(e, pm=pm):
                    for hh in range(4):
                        ins = e.matmul(pm[:, hh * 128:(hh + 1) * 128], lhsT=rkT[:, hh * 128:(hh + 1) * 128], rhs=rqT[:, hh * 128:(hh + 1) * 128],
                                       start=True, stop=True)
                    return ins
                S.op("pe", mm_s, reads=[rkT.b, rqT.b], writes=[pmb])
                sm = sm_r.next()
                mk_ = cs("maskS") if smp else cs("maskT")
                S.op("dve", lambda e, pm=pm: e.tensor_tensor(out=sm[:].rearrange("p (a b) -> p a b", a=4), in0=pm[:, :].rearrange("p (a b) -> p a b", a=4),
                                                             in1=mkap(mk_[:, 0:1], [[0, 4], [1, 128]]), op=ALU.mult), reads=[pmb, Cc.b], writes=[sm.b])
                po, pob = ps_next("M", PT_M)
                use_state = (not smp) and i > 0

                def mm_o(e, po=po):
                    for hh in range(4):
                        ins = e.matmul(po[:, hh * 128:(hh + 1) * 128], lhsT=sm[:, hh * 128:(hh + 1) * 128], rhs=rv[:, hh * 128:(hh + 1) * 128],
                                       start=True, stop=not use_state)
                        if use_state:
                            ins = e.matmul(po[:, hh * 128:(hh + 1) * 128], lhsT=rqT[:, hh * 128:(hh + 1) * 128], rhs=statebf[:, hh * 128:(hh + 1) * 128],
                                           start=False, stop=True)
                    return ins
                S.op("pe", mm_o, reads=[sm.b, rv.b, rqT.b] + ([statebf.b] if use_state else []), writes=[pob])
                ro = ro_r.next()
                if smp:
                    S.op("dve", lambda e, po=po: e.tensor_tensor(out=ro[:], in0=po[:, :], in1=cross_sb[:], op=ALU.add), reads=[pob, cross_sb.b], writes=[ro.b])
                else:
                    S.op("act", lambda e, po=po: e.activation(out=ro[:], in_=po[:, :], func=AF.Copy), reads=[pob], writes=[ro.b])
                    pk, pkb = ps_next("M", PT_M)

                    def mm_kv(e, pk=pk):
                        for hh in range(4):
                            ins = e.matmul(pk[:, hh * 128:(hh + 1) * 128], lhsT=rk[:, hh * 128:(hh + 1) * 128], rhs=rv[:, hh * 128:(hh + 1) * 128],
                                           start=True, stop=True)
                        return ins
                    S.op("pe", mm_kv, reads=[rk.b, rv.b], writes=[pkb])
                    gLb = mkap(cs("gL")[:, 0:1], [[1, 4], [0, 128]])
                    if i == 0:
                        S.op("dve", lambda e, pk=pk: e.tensor_tensor(out=state32[:].rearrange("p (a b) -> p a b", a=4), in0=pk[:, :].rearrange("p (a b) -> p a b", a=4),
                                                                     in1=gLb, op=ALU.mult), reads=[pkb, Cc.b, state32.b], writes=[state32.b])
                    else:
                        S.op("dve", lambda e, pk=pk: e.tensor_tensor(out=state32[:], in0=pk[:, :], in1=state32[:], op=ALU.add),
                             reads=[pkb, state32.b], writes=[state32.b])
                        S.op("pool", lambda e: e.tensor_tensor(out=state32[:].rearrange("p (a b) -> p a b", a=4), in0=state32[:].rearrange("p (a b) -> p a b", a=4),
                                                               in1=gLb, op=ALU.mult), reads=[state32.b, Cc.b], writes=[state32.b])
                    if i < NT - 1:
                        S.op("pool", lambda e: e.tensor_copy(out=statebf[:], in_=state32[:]), reads=[state32.b, statebf.b], writes=[statebf.b])
                    else:
                        S.dma("sp", lambda e: e.dma_start(out=reto_d.rearrange("h d e -> d h e"), in_=state32[:].rearrange("p (a b) -> p a b", a=4)),
                              state32.b, reads=[state32.b])
                        out_bufs.append(state32.b)
                if ASTOP <= 4:
                    return
                mixed = mixed_r.next()
                rsq = rsq_r.next(); st4 = st4_r.next()
                S.op("act", lambda e: e.activation(out=rsq[:], in_=ro[:], func=AF.Square), reads=[ro.b], writes=[rsq.b])
                S.op("dve", lambda e: e.tensor_reduce(out=st4[:, 0:4], in_=ro[:].rearrange("p (a b) -> p a b", a=4), axis=AX.X, op=ALU.add),
                     reads=[ro.b], writes=[st4.b])
                S.op("dve", lambda e: e.tensor_reduce(out=st4[:, 4:8], in_=rsq[:].rearrange("p (a b) -> p a b", a=4), axis=AX.X, op=ALU.add),
                     reads=[rsq.b, st4.b], writes=[st4.b])
                S.op("dve", lambda e: e.tensor_scalar(out=st4[:, 0:4], in0=st4[:, 0:4], scalar1=1.0 / 128, scalar2=None, op0=ALU.mult),
                     reads=[st4.b], writes=[st4.b])
                S.op("dve", lambda e: e.tensor_tensor(out=st4[:, 8:12], in0=st4[:, 0:4], in1=st4[:, 0:4], op=ALU.mult), reads=[st4.b], writes=[st4.b])
                S.op("dve", lambda e: e.scalar_tensor_tensor(out=st4[:, 4:8], in0=st4[:, 4:8], scalar=1.0 / 128, in1=st4[:, 8:12],
                                                             op0=ALU.mult, op1=ALU.subtract), reads=[st4.b], writes=[st4.b])
                S.op("act", lambda e: e.activation(out=st4[:, 12:16], in_=st4[:, 4:8], func=AF.Ln, bias=epsc[:], scale=1.0), reads=[st4.b, epsc.b], writes=[st4.b])
                S.op("act", lambda e: e.activation(out=st4[:, 12:16], in_=st4[:, 12:16], func=AF.Exp, scale=-0.5), reads=[st4.b], writes=[st4.b])
                S.op("dve", lambda e: e.tensor_tensor(out=ro[:].rearrange("p (a b) -> p a b", a=4), in0=ro[:].rearrange("p (a b) -> p a b", a=4),
                                                      in1=mkap(st4[:, 0:1], [[1, 4], [0, 128]]), op=ALU.subtract), reads=[ro.b, st4.b], writes=[ro.b])
                S.op("pool", lambda e: e.tensor_tensor(out=ro[:].rearrange("p (a b) -> p a b", a=4), in0=ro[:].rearrange("p (a b) -> p a b", a=4),
                                                       in1=mkap(st4[:, 12:13], [[1, 4], [0, 128]]), op=ALU.mult), reads=[ro.b, st4.b], writes=[ro.b])
                S.op("pool", lambda e: e.tensor_tensor(out=ro[:], in0=ro[:], in1=vec[:, V_GNW:V_GNW + 512], op=ALU.mult), reads=[ro.b, vec.b], writes=[ro.b])
                S.op("pool", lambda e: e.tensor_tensor(out=ro[:], in0=ro[:], in1=vec[:, V_GNB:V_GNB + 512], op=ALU.add), reads=[ro.b, vec.b], writes=[ro.b])
                S.op("dve", lambda e: e.tensor_tensor(out=mixed[:, 0:512], in0=ro[:], in1=sg[:], op=ALU.mult), reads=[ro.b, sg.b, mixed.b], writes=[mixed.b])
                if ASTOP <= 5:
                    return
                if smp:
                    pass
                jlist = [NT] if smp else list(range(i + 1))
                for hh in range(4):
                    pa, pab = ps_next("DA", PT_DA)
                    def do_ds(jt, hh=hh):
                        pd, pdb = ps_next("DS", PT_DS)

                        def mm_ds(e, pd=pd, jt=jt, hh=hh):
                            return e.matmul(pd[:, 0:256], lhsT=dkT_all[:, hh, jt * 128:(jt + 1) * 128],
                                            rhs=dqT[:, hh * 256:(hh + 1) * 256], start=True, stop=True)
                        S.op("pe", mm_ds, reads=[B_dkT[jt], dqT.b], writes=[pdb])
                        return pd, pdb
                    ds_next = do_ds(jlist[0])
                    for jn, jt in enumerate(jlist):
                        pd, pdb = ds_next
                        if jn + 1 < len(jlist):
                            ds_next = do_ds(jlist[jn + 1])
                        at = at_r.next()
                        if smp:
                            bo = cfg.coff["abias_s"][0] + hh
                        else:
                            bo = cfg.coff["abias"][0] + (i - jt) * 4 + hh
                        S.op("act", lambda e, pd=pd, at=at, bo=bo: e.activation(out=at[:], in_=pd[:, 0:256], func=AF.Exp, bias=Cc[:, bo:bo + 1], scale=0.125),
                             reads=[pdb, Cc.b], writes=[at.b])
                        if smp or jt == i:
                            S.op("pool", lambda e, at=at: e.tensor_tensor(out=at[:].rearrange("p (a b) -> p a b", a=2), in0=at[:].rearrange("p (a b) -> p a b", a=2),
                                                                          in1=mkap(mk_[:, 0:1], [[0, 2], [1, 128]]), op=ALU.mult),
                                 reads=[at.b, Cc.b], writes=[at.b])

                        def mm_av(e, pa=pa, at=at, jt=jt, hh=hh, jn=jn):
                            for c2 in range(2):
                                ins = e.matmul(pa[:, c2 * 256:c2 * 256 + 129], lhsT=at[:, c2 * 128:(c2 + 1) * 128],
                                               rhs=vaug[:, jt, hh * 130:hh * 130 + 129], start=(jn == 0 and c2 == 0),
                                               stop=(jn == len(jlist) - 1 and c2 == 1))
                            return ins
                        S.op("pe", mm_av, reads=[at.b, B_va[jt]], writes=[pab])
                    acc = acc_r.next(); fin = fin_r.next(); o1 = o1_r.next(); dd = dd_r.next(); jk = jk_r.next()
                    accv = acc[:].rearrange("p (a b) -> p a b", a=2)
                    pav = pa[:, :].rearrange("p (a b) -> p a b", a=2)[:, :, 0:129]
                    if smp:
                        S.op("dve", lambda e, pav=pav, accv=accv, hh=hh: e.tensor_tensor(
                            out=accv[:, :, 0:129], in0=pav, in1=pps_sb[:, hh, :].rearrange("p (a b) -> p a b", a=2), op=ALU.add),
                            reads=[pab, pps_sb.b], writes=[acc.b])
                    else:
                        S.op("act", lambda e, pav=pav, accv=accv: e.activation(out=accv[:, :, 0:129], in_=pav, func=AF.Copy), reads=[pab], writes=[acc.b])
                    S.op("dve", lambda e, accv=accv, fin=fin: e.reciprocal(out=fin[:, 0:2], in_=accv[:, :, 128]), reads=[acc.b], writes=[fin.b])
                    S.op("dve", lambda e, fin=fin: e.tensor_tensor(out=fin[:, 2:3], in0=fin[:, 1:2], in1=nlam[:], op=ALU.mult), reads=[fin.b, nlam.b], writes=[fin.b])
                    S.op("dve", lambda e, accv=accv, fin=fin, o1=o1: e.tensor_scalar(out=o1[:], in0=accv[:, 0, 0:128], scalar1=fin[:, 0:1], scalar2=None, op0=ALU.mult),
                         reads=[acc.b, fin.b], writes=[o1.b])
                    S.op("dve", lambda e, accv=accv, fin=fin, o1=o1, dd=dd: e.scalar_tensor_tensor(out=dd[:], in0=accv[:, 1, 0:128], scalar=fin[:, 2:3], in1=o1[:],
                                                                                                 op0=ALU.mult, op1=ALU.add), reads=[acc.b, fin.b, o1.b], writes=[dd.b])
                    S.op("act", lambda e, dd=dd, jk=jk, fin=fin: e.activation(out=jk[:], in_=dd[:], func=AF.Square, accum_out=fin[:, 3:4]), reads=[dd.b, fin.b], writes=[jk.b, fin.b])
                    S.op("act", lambda e, fin=fin: e.activation(out=fin[:, 4:5], in_=fin[:, 3:4], func=AF.Ln, bias=epsc[:], scale=1.0 / 128), reads=[fin.b, epsc.b], writes=[fin.b])
                    S.op("act", lambda e, fin=fin: e.activation(out=fin[:, 4:5], in_=fin[:, 4:5], func=AF.Exp, scale=-0.5), reads=[fin.b], writes=[fin.b])
                    S.op("dve", lambda e, dd=dd, fin=fin, hh=hh: e.tensor_scalar(out=mixed[:, 512 + hh * 128:512 + (hh + 1) * 128], in0=dd[:], scalar1=fin[:, 4:5],
                                                                               scalar2=None, op0=ALU.mult), reads=[dd.b, fin.b, mixed.b], writes=[mixed.b])
                if ASTOP <= 6:
                    return
                mixT = mixT_r.next()
                pt, ptb_ = ps_next("T", PT_T)
                ptv = pt[:].bitcast(BF16)

                def tr8(e, ptv=ptv):
                    for k in range(KC):
                        ins = e.transpose(out=ptv[:, k * 128:(k + 1) * 128], in_=mixed[:, k * 128:(k + 1) * 128], identity=identb[:])
                    return ins
                S.op("pe", tr8, reads=[mixed.b, identb.b], writes=[ptb_])
                S.op("act", lambda e, ptv=ptv: e.activation(out=mixT[:], in_=ptv[:, 0:KC * 128], func=AF.Copy), reads=[ptb_], writes=[mixT.b])
                for mh in range(2):
                    pm, pmb = ps_next("M", PT_M)

                    def mm_wo(e, pm=pm, mh=mh):
                        for k in range(KC):
                            ins = e.matmul(pm[:, :], lhsT=mixT[:, k * 128:(k + 1) * 128], rhs=wob[:, k, mh * 512:(mh + 1) * 512],
                                           start=(k == 0), stop=(k == KC - 1))
                        return ins
                    S.op("pe", mm_wo, reads=[mixT.b, wob.b], writes=[pmb])
                    S.op("dve", lambda e, pm=pm, mh=mh: e.tensor_tensor(out=xt[:, mh * 512:(mh + 1) * 512], in0=pm[:, :], in1=xt[:, mh * 512:(mh + 1) * 512], op=ALU.add),
                         reads=[pmb, xt.b], writes=[xt.b])
                S.dma("sp", lambda e: e.dma_start(out=h1s_d[i * 128:(i + 1) * 128, :], in_=xt[:]), xt.b, reads=[xt.b], writes=[B_h1s[i]])

            for i in range(min(NT + 1, ATILES)):
                attn_tile(i)
            S.barrier()

        with ExitStack() as st:
          if "skipB" not in DBG:
            def sT(name, shape, dt):
                return T(nc, st, name, shape, dt)
            PT_T = [0, 1]; PT_G = [2, 3]; PT_U = [4, 5]; PT_M = [6, 7]
            GMAX = max(len(g) for g in cfg.groups)
            TOKMAX = GMAX * 128
            wpgb = sT("b_wpgb", [128, KC, D], BF16)
            wppb = sT("b_wppb", [128, 2, D], BF16)
            stg = Ring(nc, st, "b_stg", [128, 8 * 256], F32, 2)
            stg_o = Ring(nc, st, "b_stgo", [128, D], F32, 2)
            for c in range(KC):
                sg_ = stg_o.next()
                S.dma("sp", lambda e, sg_=sg_, c=c: e.dma_start(out=sg_[:], in_=wpg_d[c * 128:(c + 1) * 128, :]), sg_.b, writes=[sg_.b])
                S.op("pool", lambda e, sg_=sg_, c=c: e.tensor_scalar(out=wpgb[:, c, :], in0=sg_[:], scalar1=lnc[:, 16 + c:17 + c], scalar2=None, op0=ALU.mult),
                     reads=[sg_.b, lnc.b, wpgb.b], writes=[wpgb.b])
            for c in range(2):
                sg_ = stg_o.next()
                S.dma("sp", lambda e, sg_=sg_, c=c: e.dma_start(out=sg_[:], in_=wpp_d[c * 128:(c + 1) * 128, :]), sg_.b, writes=[sg_.b])
                S.op("pool", lambda e, sg_=sg_, c=c: e.tensor_copy(out=wppb[:, c, :], in_=sg_[:]), reads=[sg_.b, wppb.b], writes=[wppb.b])

            H = sT("b_H", [128, GMAX, D], F32)
            B_H = [Buf("H%d" % k) for k in range(GMAX)]
            hnT = sT("b_hnT", [128, KC, TOKMAX], BF16)
            B_hnT = [Buf("hnT%d" % k) for k in range(GMAX)]
            actT = sT("b_actT", [128, NBH, TOKMAX], BF16)
            wfib = Ring(nc, st, "b_wfib", [128, 8 * 256], BF16, 2)
            wfob = sT("b_wfob", [128, NBH, D], BF16)
            B_wfo = [Buf("wfo%d" % k) for k in range(NBH)]
            B_act = [Buf("act%d" % k) for k in range(NBH)]
            junk = sT("b_junk", [128, D], BF16)
            ss_r = Ring(nc, st, "b_ss", [128, 1], F32, 2); rstd_r = Ring(nc, st, "b_rstd", [128, 1], F32, 2)
            xn_r = Ring(nc, st, "b_xn", [128, D], BF16, 1)
            sgt_r = Ring(nc, st, "b_sgt", [128, 512], BF16, 2)
            h2nT_r = Ring(nc, st, "b_h2nT", [128, KC * 128], BF16, 1)
            p32_r = Ring(nc, st, "b_p32", [128, 256], F32, 1); pbf_r = Ring(nc, st, "b_pbf", [128, 256], BF16, 1)
            pT_r = Ring(nc, st, "b_pT", [128, 256], BF16, 1)
            sig_r = Ring(nc, st, "b_sig", [128, 512], F32, 1)
            yt_r = Ring(nc, st, "b_yt", [128, D], F32, 1)

            for grp in cfg.groups:
                G = len(grp)
                NTOK = G * 128
                tgs = [(t0, min(512, NTOK - t0)) for t0 in range(0, NTOK, 512)]
                for k, ti in enumerate(grp):
                    S.dma("sp", lambda e, k=k, ti=ti: e.dma_start(out=H[:, k, :], in_=h1s_d[ti * 128:(ti + 1) * 128, :]), B_H[k],
                          reads=[B_h1s[ti]], writes=[B_H[k]])
                    ss = ss_r.next(); rstd = rstd_r.next(); xn = xn_r.next()
                    Hk = _View(H, k, B_H[k])
                    _norm_T_view(S, nc, Hk, junk, ss, rstd, xn, hnT, k, B_hnT[k], ps_next, PT_T, identb, epsc)
                for half in range(2):
                    for nbl in range(NBH):
                        nb = half * NBH + nbl
                        sg_ = stg.next()
                        S.dma("sp", lambda e, sg_=sg_, nb=nb: e.dma_start(out=sg_[:], in_=wfi_d[nb, :, :]), sg_.b, writes=[sg_.b])
                        wb = wfib.next()
                        S.op("pool", lambda e, sg_=sg_, wb=wb: e.tensor_tensor(out=wb[:].rearrange("p (c n) -> p c n", c=8), in0=sg_[:].rearrange("p (c n) -> p c n", c=8),
                                                                               in1=mkap(lnc[:, 8:9], [[1, 8], [0, 256]]), op=ALU.mult),
                             reads=[sg_.b, lnc.b], writes=[wb.b])
                        so = stg_o.next()
                        S.dma("sp", lambda e, so=so, nb=nb: e.dma_start(out=so[:], in_=wfo_d[nb * 128:(nb + 1) * 128, :]), so.b, writes=[so.b])
                        S.op("pool", lambda e, so=so, nbl=nbl: e.tensor_copy(out=wfob[:, nbl, :], in_=so[:]), reads=[so.b, B_wfo[nbl]], writes=[B_wfo[nbl]])
                        for (t0, tn) in tgs:
                            ks = list(range(t0 // 128, (t0 + tn) // 128))
                            pg, pgb = ps_next("G", PT_G)
                            pu, pub = ps_next("U", PT_U)

                            def mm_gu(e, wb=wb, pg=pg, pu=pu, t0=t0, tn=tn):
                                for (pp, co) in ((pg, 0), (pu, 128)):
                                    for k in range(KC):
                                        ins = e.matmul(pp[:, 0:tn], lhsT=wb[:, k * 256 + co:k * 256 + co + 128], rhs=hnT[:, k, t0:t0 + tn],
                                                       start=(k == 0), stop=(k == KC - 1))
                                return ins
                            S.op("pe", mm_gu, reads=[wb.b] + [B_hnT[k] for k in ks], writes=[pgb, pub])
                            sgt = sgt_r.next()
                            S.op("act", lambda e, pg=pg, sgt=sgt, tn=tn: e.activation(out=sgt[:, 0:tn], in_=pg[:, 0:tn], func=AF.Silu), reads=[pgb], writes=[sgt.b])
                            S.op("dve", lambda e, pu=pu, sgt=sgt, nbl=nbl, t0=t0, tn=tn: e.tensor_tensor(out=actT[:, nbl, t0:t0 + tn], in0=pu[:, 0:tn], in1=sgt[:, 0:tn], op=ALU.mult),
                                 reads=[pub, sgt.b, B_act[nbl]], writes=[B_act[nbl]])
                    for k in range(G):
                        for mh in range(2):
                            pm, pmb = ps_next("M", PT_M)

                            def mm_fo(e, pm=pm, k=k, mh=mh):
                                for nbl in range(NBH):
                                    ins = e.matmul(pm[:, :], lhsT=actT[:, nbl, k * 128:(k + 1) * 128], rhs=wfob[:, nbl, mh * 512:(mh + 1) * 512],
                                                   start=(nbl == 0), stop=(nbl == NBH - 1))
                                return ins
                            S.op("pe", mm_fo, reads=B_act + B_wfo, writes=[pmb])
                            S.op("dve", lambda e, pm=pm, k=k, mh=mh: e.tensor_tensor(out=H[:, k, mh * 512:(mh + 1) * 512], in0=pm[:, :], in1=H[:, k, mh * 512:(mh + 1) * 512], op=ALU.add),
                                 reads=[pmb, B_H[k]], writes=[B_H[k]])
                for k, ti in enumerate(grp):
                    smp = (ti == NT)
                    ss = ss_r.next(); rstd = rstd_r.next(); xn = xn_r.next(); h2nT = h2nT_r.next()
                    Hk = _View(H, k, B_H[k])
                    _norm_T_flat(S, nc, Hk, junk, ss, rstd, xn, h2nT, ps_next, PT_T, identb, epsc)
                    p32 = p32_r.next(); pbf = pbf_r.next(); pT = pT_r.next()
                    psrc = psm_d[:, :] if smp else p_d[ti * 128:(ti + 1) * 128, :]
                    S.dma("sp", lambda e, p32=p32, psrc=psrc: e.dma_start(out=p32[:], in_=psrc), p32.b, writes=[p32.b])
                    S.op("act", lambda e, p32=p32, pbf=pbf: e.activation(out=pbf[:], in_=p32[:], func=AF.Copy), reads=[p32.b], writes=[pbf.b])
                    pt, ptb_ = ps_next("T", PT_T)
                    ptv = pt[:].bitcast(BF16)

                    def tr2(e, ptv=ptv, pbf=pbf):
                        for c in range(2):
                            ins = e.transpose(out=ptv[:, c * 128:(c + 1) * 128], in_=pbf[:, c * 128:(c + 1) * 128], identity=identb[:])
                        return ins
                    S.op("pe", tr2, reads=[pbf.b, identb.b], writes=[ptb_])
                    S.op("dve", lambda e, ptv=ptv, pT=pT: e.tensor_copy(out=pT[:], in_=ptv[:, 0:256]), reads=[ptb_], writes=[pT.b])
                    yt = yt_r.next()
                    for mh in range(2):
                        pgt, pgtb = ps_next("G", PT_G)
                        ppj, ppjb = ps_next("U", PT_U)

                        def mm_gate(e, pgt=pgt, mh=mh, h2nT=h2nT):
                            for c in range(KC):
                                ins = e.matmul(pgt[:, :], lhsT=h2nT[:, c * 128:(c + 1) * 128], rhs=wpgb[:, c, mh * 512:(mh + 1) * 512], start=(c == 0), stop=(c == KC - 1))
                            return ins
                        S.op("pe", mm_gate, reads=[h2nT.b, wpgb.b], writes=[pgtb])

                        def mm_pp(e, ppj=ppj, mh=mh, pT=pT):
                            for c in range(2):
                                ins = e.matmul(ppj[:, :], lhsT=pT[:, c * 128:(c + 1) * 128], rhs=wppb[:, c, mh * 512:(mh + 1) * 512], start=(c == 0), stop=(c == 1))
                            return ins
                        S.op("pe", mm_pp, reads=[pT.b, wppb.b], writes=[ppjb])
                        sig = sig_r.next()
                        S.op("act", lambda e, pgt=pgt, sig=sig: e.activation(out=sig[:], in_=pgt[:, :], func=AF.Sigmoid), reads=[pgtb], writes=[sig.b])
                        S.op("dve", lambda e, ppj=ppj, sig=sig: e.tensor_tensor(out=sig[:], in0=ppj[:, :], in1=sig[:], op=ALU.mult), reads=[ppjb, sig.b], writes=[sig.b])
                        S.op("pool", lambda e, sig=sig, yt=yt, k=k, mh=mh: e.tensor_tensor(out=yt[:, mh * 512:(mh + 1) * 512], in0=sig[:], in1=H[:, k, mh * 512:(mh + 1) * 512], op=ALU.add),
                             reads=[sig.b, B_H[k], yt.b], writes=[yt.b])
                    ydst = ys_d[:, :] if smp else y_d[ti * 128:(ti + 1) * 128, :]
                    S.dma("sp", lambda e, yt=yt, ydst=ydst: e.dma_start(out=ydst, in_=yt[:]), yt.b, reads=[yt.b])
            S.barrier()
        S.barrier()

        with nc.Block() as block:
            S.emit(block)
    return nc


CC_INC = 16
import os as _os
DBG = set(_os.environ.get("KDBG", "").split(","))
ASTOP = int(_os.environ.get("ASTOP", "99"))
ATILES = int(_os.environ.get("ATILES", "99"))
SSTOP = int(_os.environ.get("SSTOP", "99"))
SUB = int(_os.environ.get("SUB", "99"))


class _Stop(Exception):
    pass


def _chk(n):
    if SUB <= n:
        raise _Stop()
SBATCH = int(_os.environ.get("SBATCH", "32"))


class _View:
    def __init__(self, base, k, buf):
        self.base, self.k, self.b = base, k, buf
        self.shape = [128, D]

    def __getitem__(self, key):
        return self.base.t[:, self.k, :][key]


def _norm_core(S, src, junk, ss, rstd, xn, epsc):
    S.op("act", lambda e: e.activation(out=junk[:], in_=src[:, :], func=AF.Square, accum_out=ss[:]), reads=[src.b], writes=[junk.b, ss.b])
    S.op("act", lambda e: e.activation(out=rstd[:], in_=ss[:], func=AF.Ln, bias=epsc[:], scale=1.0 / D), reads=[ss.b, epsc.b], writes=[rstd.b])
    S.op("act", lambda e: e.activation(out=rstd[:], in_=rstd[:], func=AF.Exp, scale=-0.5), reads=[rstd.b], writes=[rstd.b])
    S.op("dve", lambda e: e.tensor_scalar(out=xn[:], in0=src[:, :], scalar1=rstd[:], scalar2=None, op0=ALU.mult), reads=[src.b, rstd.b], writes=[xn.b])


def _tr8(S, xn, identb, ps_next, PT_T):
    pt, pb = ps_next("T", PT_T)
    ptb = pt[:].bitcast(BF16)

    def tr(e):
        for k in range(KC):
            ins = e.transpose(out=ptb[:, k * 128:(k + 1) * 128], in_=xn[:, k * 128:(k + 1) * 128], identity=identb[:])
        return ins
    S.op("pe", tr, reads=[xn.b, identb.b], writes=[pb])
    return ptb, pb


def _norm_T_view(S, nc, src, junk, ss, rstd, xn, hnT, k, hbuf, ps_next, PT_T, identb, epsc):
    _norm_core(S, src, junk, ss, rstd, xn, epsc)
    ptb, pb = _tr8(S, xn, identb, ps_next, PT_T)
    S.op("dve", lambda e: e.tensor_copy(out=hnT[:, :, k * 128:(k + 1) * 128], in_=ptb[:, 0:KC * 128].rearrange("p (c t) -> p c t", c=KC)),
         reads=[pb], writes=[hbuf])


def _norm_T_flat(S, nc, src, junk, ss, rstd, xn, dst, ps_next, PT_T, identb, epsc):
    _norm_core(S, src, junk, ss, rstd, xn, epsc)
    ptb, pb = _tr8(S, xn, identb, ps_next, PT_T)
    S.op("act", lambda e: e.activation(out=dst[:], in_=ptb[:, 0:KC * 128], func=AF.Copy), reads=[pb], writes=[dst.b])


def make_in_maps(cfg, inp):
    NT, NB = cfg.NT, cfg.NB
    HP = cfg.POSH // 2
    f = lambda a: np.ascontiguousarray(a, dtype=np.float32)
    w_in = f(inp["w_in"][0]); w_o = f(inp["w_o"][0])
    wfi = inp["w_ffn_in"][0]
    wfi_r = f(wfi.reshape(8, 128, 2, NB, 128).transpose(3, 1, 0, 2, 4).reshape(NB, 128, 8 * 256))
    w_fo = f(inp["w_ffn_out"][0]); w_pg = f(inp["w_ple_gate"][0]); w_pp = f(inp["w_ple_proj"][0])
    lncols = f(np.concatenate([inp["ln1_w"][0].reshape(8, 128).T, inp["ln2_w"][0].reshape(8, 128).T,
                               inp["ln_ple_w"][0].reshape(8, 128).T], axis=1))
    sublncol = f(inp["diff_subln_w"][0].reshape(4, 128).T)
    vecs = f(np.concatenate([inp["q_norm_w"][0], inp["k_norm_w"][0], inp["lambda_q1"][0], inp["lambda_q2"][0],
                             inp["lambda_k1"][0], inp["lambda_k2"][0], inp["ret_gn_w"][0], inp["ret_gn_b"][0]])[None, :])
    w_sel = f(np.concatenate([w_in[:, 0:1536], w_in[:, 2048:2560]], axis=1))
    consts = make_consts(cfg)
    ck, cv = inp["cache_k"][0], inp["cache_v"][0]
    shared = dict(w_in=w_in, w_o=w_o, w_sel=w_sel, w_fi=wfi_r, w_fo=w_fo, w_pg=w_pg, w_pp=w_pp,
                  consts=consts, lncols=lncols, sublncol=sublncol, vecs=vecs)
    for r in range(8):
        h, j = r % 4, r // 4
        for hf in range(2):
            p0 = j * cfg.POSH + hf * HP
            shared["ck%d_%d" % (r, hf)] = f(ck[:, p0:p0 + HP, h, :].reshape(cfg.NPHYS, HP * 128))
            shared["cv%d_%d" % (r, hf)] = f(cv[:, p0:p0 + HP, h, :].reshape(cfg.NPHYS, HP * 128))
    maps = []
    for c in range(8):
        xs = np.zeros((128, D), np.float32)
        xs[0:16] = inp["x_sample"][4 * c:4 * c + 4].reshape(16, D)
        psm = np.zeros((128, 256), np.float32)
        psm[0:16] = inp["p_sample"][0, 4 * c:4 * c + 4].reshape(16, 256)
        m = dict(shared)
        m.update(x=f(inp["x_prompt"][c]), xs=xs, p=f(inp["p_prompt"][0, c]), psm=psm,
                 ptT=np.ascontiguousarray(inp["page_table"][4 * c:4 * c + 4].T.astype(np.int32)),
                 st_in=f(inp["state_ret"][0, 4 * c:4 * c + 4].reshape(16, 128, 128)))
        maps.append(m)
    return maps


def assemble(cfg, res):
    NT = cfg.NT
    y = np.stack([res[c]["y"].reshape(NT * 128, D) for c in range(8)])
    kp = np.stack([res[c]["ko"].reshape(NT * 128, 4, 128) for c in range(8)])[None]
    vp = np.stack([res[c]["vo"].reshape(NT * 128, 4, 128) for c in range(8)])[None]
    rp = np.stack([res[c]["reto"].reshape(4, 128, 128) for c in range(8)])[None]
    ysm = np.concatenate([res[c]["ys"][0:16].reshape(4, 4, D) for c in range(8)], axis=0)
    ks = np.concatenate([res[c]["kso"][0:16].reshape(4, 4, 4, 128) for c in range(8)], axis=0)[None]
    vs = np.concatenate([res[c]["vso"][0:16].reshape(4, 4, 4, 128) for c in range(8)], axis=0)[None]
    rs = np.concatenate([res[c]["retso"].reshape(4, 4, 128, 128) for c in range(8)], axis=0)[None]
    return tuple(np.ascontiguousarray(a, dtype=np.float32) for a in (y, ysm, kp, vp, rp, ks, vs, rs))


_NC_CACHE = {}


def kernel(**inputs):
    cfg = Cfg()
    inputs = {k: np.asarray(v) for k, v in inputs.items()}
    if "nc" not in _NC_CACHE:
        _NC_CACHE["nc"] = build(cfg)
    nc = _NC_CACHE["nc"]
    in_maps = make_in_maps(cfg, inputs)
    res = run_bass_kernel_spmd(nc, in_maps, core_ids=list(range(8)))
    return assemble(cfg, res.results)
```
